# Optimizing a Trainium2 kernel written in Bass

```python
import math
import jax
import jax.numpy as jnp
from jax import lax
import numpy as np


D_MODEL = 1024
BATCH = 8
SEQ = 2048
DEPTH = 4

GRID_W = 64
CTX_LEN = 256
EPS = 1e-6
CHUNK = 128
ATTN_BLOCK = 128
D_FF = 4 * D_MODEL
CONV_K = 5

MLA_HEADS = 8
MLA_NOPE = 64
MLA_ROPE = 32
MLA_V = 64
Q_LORA = 256
KV_LORA = 256
ROPE_THETA = 10000.0
MLA_SCALE = (MLA_NOPE + MLA_ROPE) ** -0.5

SSM_HEADS = 8
SSM_HEADDIM = 64
SSM_DINNER = SSM_HEADS * SSM_HEADDIM
SSM_GROUPS = 2
SSM_STATE = 128

MLSTM_HEADS = 8
MLSTM_DQK = 64
MLSTM_DV = 128
MLSTM_QK = MLSTM_HEADS * MLSTM_DQK
MLSTM_VW = MLSTM_HEADS * MLSTM_DV

EVEN_IN = Q_LORA + KV_LORA + MLA_ROPE + SSM_DINNER + SSM_DINNER + 2 * SSM_GROUPS * SSM_STATE + 2 * SSM_HEADS
ODD_IN = 2 * MLSTM_QK + 2 * MLSTM_VW + 4 * MLSTM_HEADS

kernel_name = 'hybrid_mla_ssd_mlstm_prefix_dit'


def split_cols(u, sizes):
    offs = np.cumsum(sizes)[:-1].tolist()
    return jnp.split(u, offs, axis=-1)


def rmsnorm(x, w):
    xf = x.astype(jnp.float32)
    y = xf * lax.rsqrt(jnp.mean(xf * xf, axis=-1, keepdims=True) + EPS)
    return (y * w).astype(x.dtype)


def modulate(h, shift, scale):
    return h * (1.0 + scale) + shift


def sqrelu_mlp(h, w1, w2):
    return (jnp.square(jax.nn.relu(h @ w1)) @ w2).astype(h.dtype)


def dwconv_silu(x, w, b):
    ch = x.shape[-1]
    y = lax.conv_general_dilated(x, w[:, None, :].astype(x.dtype), window_strides=(1,),
                                 padding=[(CONV_K // 2, CONV_K // 2)],
                                 dimension_numbers=('NWC', 'WIO', 'NWC'), feature_group_count=ch)
    return jax.nn.silu(y + b)


def rope2d_tables(length):
    n_rows = length // GRID_W
    row = jnp.repeat(jnp.arange(n_rows), GRID_W).astype(jnp.float32)
    col = jnp.tile(jnp.arange(GRID_W), n_rows).astype(jnp.float32)
    half = MLA_ROPE // 2
    inv = 1.0 / (ROPE_THETA ** (jnp.arange(0, half, 2, dtype=jnp.float32) / half))
    ar = row[:, None] * inv
    ac = col[:, None] * inv
    return tuple(t[None, :, None, :] for t in (jnp.cos(ar), jnp.sin(ar), jnp.cos(ac), jnp.sin(ac)))


def rope_rotate(x, cos, sin):
    x1, x2 = jnp.split(x, 2, axis=-1)
    return jnp.concatenate([x1 * cos - x2 * sin, x2 * cos + x1 * sin], axis=-1)


def rope2d(x, tabs):
    cr, sr, cc, sc = tabs
    xr, xc = jnp.split(x, 2, axis=-1)
    return jnp.concatenate([rope_rotate(xr, cr, sr), rope_rotate(xc, cc, sc)], axis=-1).astype(x.dtype)


def attend(q, k, v):
    s = jnp.einsum('bqhd,bkhd->bhqk', q, k).astype(jnp.float32) * MLA_SCALE
    p = jax.nn.softmax(s, axis=-1).astype(v.dtype)
    return jnp.einsum('bhqk,bkhd->bqhd', p, v)


def ssd_scan(x, dt, a_coef, bm, cm, h0):
    bsz, length, nh, hp = x.shape
    ng, ns = bm.shape[-2:]
    nr = nh // ng
    nc = length // CHUNK
    xf = x.astype(jnp.float32).reshape(bsz, nc, CHUNK, ng, nr, hp)
    dtc = dt.reshape(bsz, nc, CHUNK, ng, nr)
    bc = bm.astype(jnp.float32).reshape(bsz, nc, CHUNK, ng, ns)
    cc = cm.astype(jnp.float32).reshape(bsz, nc, CHUNK, ng, ns)
    acum = jnp.cumsum(dtc * a_coef.reshape(ng, nr), axis=2)
    causal = jnp.tril(jnp.ones((CHUNK, CHUNK), bool))
    seg = acum[:, :, :, None] - acum[:, :, None, :]
    decay = jnp.exp(jnp.where(causal[:, :, None, None], seg, -jnp.inf))
    cb = jnp.einsum('bctgn,bcsgn->bctsg', cc, bc)
    w = cb[..., None] * decay * dtc[:, :, None]
    y = jnp.einsum('bctsgr,bcsgrp->bctgrp', w, xf)
    to_end = jnp.exp(acum[:, :, -1:] - acum) * dtc
    s_chunk = jnp.einsum('bcsgr,bcsgrp,bcsgn->bcgrpn', to_end, xf, bc)
    chunk_decay = jnp.exp(acum[:, :, -1])

    def step(h, inp):
        s_c, d_c = inp
        return d_c[..., None, None] * h + s_c, h

    h_last, h_in = lax.scan(step, h0.reshape(bsz, ng, nr, hp, ns),
                            (jnp.moveaxis(s_chunk, 1, 0), jnp.moveaxis(chunk_decay, 1, 0)))
    h_in = jnp.moveaxis(h_in, 0, 1)
    y = y + jnp.einsum('bctgn,bcgrpn->bctgrp', cc, h_in) * jnp.exp(acum)[..., None]
    return y.reshape(bsz, length, nh, hp), h_last.reshape(bsz, nh, hp, ns)


def mlstm_scan(q, k, v, i_pre, f_pre, state, with_output):
    bsz, length, nh, dk = q.shape
    dv = v.shape[-1]
    nc = length // CHUNK
    qc = q.astype(jnp.float32).reshape(bsz, nc, CHUNK, nh, dk)
    kc = k.astype(jnp.float32).reshape(bsz, nc, CHUNK, nh, dk)
    vc = v.astype(jnp.float32).reshape(bsz, nc, CHUNK, nh, dv)
    ic = i_pre.reshape(bsz, nc, CHUNK, nh)
    b = jnp.cumsum(jax.nn.log_sigmoid(f_pre).reshape(bsz, nc, CHUNK, nh), axis=2)
    b_last = b[:, :, -1]
    w_end = b_last[:, :, None] - b + ic

    def step(carry, inp):
        c_st, n_st, m_st = carry
        k_c, v_c, w_c, bl_c = inp
        m_new = jnp.maximum(bl_c + m_st, jnp.max(w_c, axis=1))
        keep = jnp.exp(bl_c + m_st - m_new)
        wt = jnp.exp(w_c - m_new[:, None])
        c_new = keep[..., None, None] * c_st + jnp.einsum('bsh,bshv,bshk->bhvk', wt, v_c, k_c)
        n_new = keep[..., None] * n_st + jnp.einsum('bsh,bshk->bhk', wt, k_c)
        return (c_new, n_new, m_new), (c_st, n_st, m_st)

    xs = (jnp.moveaxis(kc, 1, 0), jnp.moveaxis(vc, 1, 0), jnp.moveaxis(w_end, 1, 0), jnp.moveaxis(b_last, 1, 0))
    final, (c_in, n_in, m_in) = lax.scan(step, state, xs)
    if not with_output:
        return None, final
    c_in = jnp.moveaxis(c_in, 0, 1)
    n_in = jnp.moveaxis(n_in, 0, 1)
    m_in = jnp.moveaxis(m_in, 0, 1)
    causal = jnp.tril(jnp.ones((CHUNK, CHUNK), bool))
    logd = b[:, :, :, None] - b[:, :, None, :] + ic[:, :, None, :]
    logd = jnp.where(causal[:, :, None], logd, -jnp.inf)
    g = b + m_in[:, :, None]
    m_t = jnp.maximum(g, jnp.max(logd, axis=3))
    dmat = jnp.exp(logd - m_t[:, :, :, None])
    keep = jnp.exp(g - m_t)
    a = jnp.einsum('bcthd,bcshd->bctsh', qc, kc) * dmat
    num = jnp.einsum('bctsh,bcshv->bcthv', a, vc) + keep[..., None] * jnp.einsum('bcthk,bchvk->bcthv', qc, c_in)
    den = jnp.sum(a, axis=3) + keep * jnp.einsum('bcthk,bchk->bcth', qc, n_in)
    h = num / jnp.maximum(jnp.abs(den), jnp.exp(-m_t))[..., None]
    return h.reshape(bsz, length, nh, dv), final


def gated_rmsnorm(y, z, w):
    b_, l_ = z.shape[:2]
    g = y.reshape(b_, l_, SSM_GROUPS, -1) * jax.nn.silu(z.astype(jnp.float32)).reshape(b_, l_, SSM_GROUPS, -1)
    g = g * lax.rsqrt(jnp.mean(g * g, axis=-1, keepdims=True) + EPS)
    return g.reshape(b_, l_, SSM_DINNER) * w


def mla_ssd_mixer(h_ctx, h_lat, rope, w_in, q_norm_w, w_uq, kv_norm_w, w_ukv, conv_w, conv_b,
                  dt_bias, a_log, d_skip, ssm_norm_w, w_out, need_ctx):
    sizes = [Q_LORA, KV_LORA, MLA_ROPE, SSM_DINNER, SSM_DINNER + 2 * SSM_GROUPS * SSM_STATE, 2 * SSM_HEADS]
    parts_c = split_cols(h_ctx @ w_in, sizes)
    parts_l = split_cols(h_lat @ w_in, sizes)

    def mla_qkv(cq, ckv, k_rope, tabs):
        b_, l_ = cq.shape[:2]
        q = (rmsnorm(cq, q_norm_w) @ w_uq).reshape(b_, l_, MLA_HEADS, MLA_NOPE + MLA_ROPE)
        kv = (rmsnorm(ckv, kv_norm_w) @ w_ukv).reshape(b_, l_, MLA_HEADS, MLA_NOPE + MLA_V)
        q_nope, q_rope = jnp.split(q, [MLA_NOPE], axis=-1)
        k_nope, v = jnp.split(kv, [MLA_NOPE], axis=-1)
        k_rope = k_rope[:, :, None, :]
        if tabs is not None:
            q_rope = rope2d(q_rope, tabs)
            k_rope = rope2d(k_rope, tabs)
        q = jnp.concatenate([q_nope, q_rope], axis=-1)
        k = jnp.concatenate([k_nope, jnp.broadcast_to(k_rope, k_nope.shape[:3] + (MLA_ROPE,)).astype(k_nope.dtype)], axis=-1)
        return q, k, v

    q_c, k_c, v_c = mla_qkv(parts_c[0], parts_c[1], parts_c[2], None)
    q_l, k_l, v_l = mla_qkv(parts_l[0], parts_l[1], parts_l[2], rope)
    k_all = jnp.concatenate([k_l, k_c], axis=1)
    v_all = jnp.concatenate([v_l, v_c], axis=1)
    b_, l_ = h_lat.shape[:2]
    nb = l_ // ATTN_BLOCK
    q_blocks = jnp.moveaxis(q_l.reshape(b_, nb, ATTN_BLOCK, MLA_HEADS, MLA_NOPE + MLA_ROPE), 1, 0)
    o_l = lax.map(lambda qb: attend(qb, k_all, v_all), q_blocks)
    o_l = jnp.moveaxis(o_l, 0, 1).reshape(b_, l_, MLA_HEADS * MLA_V)

    def ssd_in(parts):
        z, xbc, dtr = parts[3], parts[4], parts[5]
        bb, ll = z.shape[:2]
        xbc = dwconv_silu(xbc, conv_w, conv_b)
        xs, bm, cm = split_cols(xbc, [SSM_DINNER, SSM_GROUPS * SSM_STATE, SSM_GROUPS * SSM_STATE])
        dt = jax.nn.softplus(dtr.astype(jnp.float32).reshape(bb, ll, 2, SSM_HEADS) + dt_bias)
        return (z, xs.reshape(bb, ll, SSM_HEADS, SSM_HEADDIM), bm.reshape(bb, ll, SSM_GROUPS, SSM_STATE),
                cm.reshape(bb, ll, SSM_GROUPS, SSM_STATE), dt)

    z_c, x_c, b_c, c_c, dt_c = ssd_in(parts_c)
    z_l, x_l, b_l, c_l, dt_l = ssd_in(parts_l)
    a_coef = -jnp.exp(a_log.astype(jnp.float32))
    h0 = jnp.zeros((b_, SSM_HEADS, SSM_HEADDIM, SSM_STATE), jnp.float32)
    y_c = d_skip[:, None] * x_c.astype(jnp.float32)
    y_l = d_skip[:, None] * x_l.astype(jnp.float32)
    for d in range(2):
        fl = (lambda t: jnp.flip(t, axis=1)) if d else (lambda t: t)
        yc_d, h_d = ssd_scan(fl(x_c), fl(dt_c[:, :, d]), a_coef[d], fl(b_c), fl(c_c), h0)
        yl_d, _ = ssd_scan(fl(x_l), fl(dt_l[:, :, d]), a_coef[d], fl(b_l), fl(c_l), h_d)
        y_c = y_c + fl(yc_d)
        y_l = y_l + fl(yl_d)
    s_l = gated_rmsnorm(y_l, z_l, ssm_norm_w).astype(o_l.dtype)
    out_l = (jnp.concatenate([o_l, s_l], axis=-1) @ w_out).astype(h_lat.dtype)
    if not need_ctx:
        return None, out_l
    o_c = attend(q_c, k_c, v_c).reshape(b_, q_c.shape[1], MLA_HEADS * MLA_V)
    s_c = gated_rmsnorm(y_c, z_c, ssm_norm_w).astype(o_c.dtype)
    out_c = (jnp.concatenate([o_c, s_c], axis=-1) @ w_out).astype(h_ctx.dtype)
    return out_c, out_l


def mlstm_mixer(h_ctx, h_lat, w_in, conv_w, conv_b, i_bias, f_bias, head_norm_w, w_out, need_ctx):
    def project(h):
        bb, ll = h.shape[:2]
        qk, v, o, ig, fg = split_cols(h @ w_in, [2 * MLSTM_QK, MLSTM_VW, MLSTM_VW, 2 * MLSTM_HEADS, 2 * MLSTM_HEADS])
        q, k = jnp.split(dwconv_silu(qk, conv_w, conv_b), 2, axis=-1)
        q = q.reshape(bb, ll, MLSTM_HEADS, MLSTM_DQK)
        k = k.reshape(bb, ll, MLSTM_HEADS, MLSTM_DQK) * (MLSTM_DQK ** -0.5)
        v = v.reshape(bb, ll, MLSTM_HEADS, MLSTM_DV)
        ig = ig.astype(jnp.float32).reshape(bb, ll, 2, MLSTM_HEADS) + i_bias
        fg = fg.astype(jnp.float32).reshape(bb, ll, 2, MLSTM_HEADS) + f_bias
        return q, k, v, o, ig, fg

    q_c, k_c, v_c, o_c, i_c, f_c = project(h_ctx)
    q_l, k_l, v_l, o_l, i_l, f_l = project(h_lat)
    b_ = h_lat.shape[0]
    zero = (jnp.zeros((b_, MLSTM_HEADS, MLSTM_DV, MLSTM_DQK), jnp.float32),
            jnp.zeros((b_, MLSTM_HEADS, MLSTM_DQK), jnp.float32),
            jnp.zeros((b_, MLSTM_HEADS), jnp.float32))
    lat_h = []
    ctx_h = []
    for d in range(2):
        fl = (lambda t: jnp.flip(t, axis=1)) if d else (lambda t: t)
        hc_d, st = mlstm_scan(fl(q_c), fl(k_c), fl(v_c), fl(i_c[:, :, d]), fl(f_c[:, :, d]), zero, need_ctx)
        hl_d, _ = mlstm_scan(fl(q_l), fl(k_l), fl(v_l), fl(i_l[:, :, d]), fl(f_l[:, :, d]), st, True)
        lat_h.append(fl(hl_d))
        if need_ctx:
            ctx_h.append(fl(hc_d))

    def finish(ht, o, like):
        bb, ll = ht.shape[:2]
        hn = ht * lax.rsqrt(jnp.mean(ht * ht, axis=-1, keepdims=True) + EPS)
        hn = hn.reshape(bb, ll, MLSTM_VW) * head_norm_w
        return ((jax.nn.sigmoid(o.astype(jnp.float32)) * hn) @ w_out).astype(like.dtype)

    out_l = finish(lat_h[0] + lat_h[1], o_l, h_lat)
    if not need_ctx:
        return None, out_l
    return finish(ctx_h[0] + ctx_h[1], o_c, h_ctx), out_l


def setup_inputs(seed: int = 0) -> dict:
    key = jax.random.key(seed)
    ks = iter(jax.random.split(key, 48))
    f32 = jnp.float32
    ne = (DEPTH + 1) // 2
    no = DEPTH // 2

    def nrm(shape, scale):
        return jax.random.normal(next(ks), shape, f32) * scale

    def gain(shape):
        return 1.0 + nrm(shape, 0.02)

    x = nrm((BATCH, SEQ, D_MODEL), 1.0)
    c = nrm((BATCH, D_MODEL), 1.0)
    ctx = nrm((BATCH, CTX_LEN, D_MODEL), 1.0)
    c_ctx = nrm((D_MODEL,), 1.0)
    ada_w = nrm((DEPTH, D_MODEL, 6 * D_MODEL), 0.5 * D_MODEL ** -0.5)
    ada_b = nrm((DEPTH, 6 * D_MODEL), 0.02)
    norm_mix_w = gain((DEPTH, D_MODEL))
    norm_mlp_w = gain((DEPTH, D_MODEL))
    mlp_w1 = nrm((DEPTH, D_MODEL, D_FF), D_MODEL ** -0.5)
    mlp_w2 = nrm((DEPTH, D_FF, D_MODEL), D_FF ** -0.5)
    ev_w_in = nrm((ne, D_MODEL, EVEN_IN), D_MODEL ** -0.5)
    ev_q_norm_w = gain((ne, Q_LORA))
    ev_w_uq = nrm((ne, Q_LORA, MLA_HEADS * (MLA_NOPE + MLA_ROPE)), Q_LORA ** -0.5)
    ev_kv_norm_w = gain((ne, KV_LORA))
    ev_w_ukv = nrm((ne, KV_LORA, MLA_HEADS * (MLA_NOPE + MLA_V)), KV_LORA ** -0.5)
    ev_conv_w = nrm((ne, CONV_K, SSM_DINNER + 2 * SSM_GROUPS * SSM_STATE), CONV_K ** -0.5)
    ev_conv_b = nrm((ne, SSM_DINNER + 2 * SSM_GROUPS * SSM_STATE), 0.02)
    dt0 = jnp.exp(jax.random.uniform(next(ks), (ne, 2, SSM_HEADS), f32, math.log(1e-3), math.log(1e-1)))
    ev_dt_bias = dt0 + jnp.log(-jnp.expm1(-dt0))
    ev_a_log = jnp.log(jax.random.uniform(next(ks), (ne, 2, SSM_HEADS), f32, 1.0, 16.0))
    ev_d_skip = gain((ne, SSM_HEADS))
    ev_ssm_norm_w = gain((ne, SSM_DINNER))
    ev_w_out = nrm((ne, MLA_HEADS * MLA_V + SSM_DINNER, D_MODEL), (MLA_HEADS * MLA_V + SSM_DINNER) ** -0.5)
    od_w_in = nrm((no, D_MODEL, ODD_IN), D_MODEL ** -0.5)
    od_conv_w = nrm((no, CONV_K, 2 * MLSTM_QK), CONV_K ** -0.5)
    od_conv_b = nrm((no, 2 * MLSTM_QK), 0.02)
    od_i_bias = nrm((no, 2, MLSTM_HEADS), 0.1)
    od_f_bias = jnp.linspace(3.0, 6.0, MLSTM_HEADS, dtype=f32) + nrm((no, 2, MLSTM_HEADS), 0.1)
    od_head_norm_w = gain((no, MLSTM_VW))
    od_w_out = nrm((no, MLSTM_VW, D_MODEL), MLSTM_VW ** -0.5)
    final_norm_w = gain((D_MODEL,))
    return {'x': x, 'c': c, 'ctx': ctx, 'c_ctx': c_ctx, 'ada_w': ada_w, 'ada_b': ada_b,
            'norm_mix_w': norm_mix_w, 'norm_mlp_w': norm_mlp_w, 'mlp_w1': mlp_w1, 'mlp_w2': mlp_w2,
            'ev_w_in': ev_w_in, 'ev_q_norm_w': ev_q_norm_w, 'ev_w_uq': ev_w_uq, 'ev_kv_norm_w': ev_kv_norm_w,
            'ev_w_ukv': ev_w_ukv, 'ev_conv_w': ev_conv_w, 'ev_conv_b': ev_conv_b, 'ev_dt_bias': ev_dt_bias,
            'ev_a_log': ev_a_log, 'ev_d_skip': ev_d_skip, 'ev_ssm_norm_w': ev_ssm_norm_w, 'ev_w_out': ev_w_out,
            'od_w_in': od_w_in, 'od_conv_w': od_conv_w, 'od_conv_b': od_conv_b, 'od_i_bias': od_i_bias,
            'od_f_bias': od_f_bias, 'od_head_norm_w': od_head_norm_w, 'od_w_out': od_w_out,
            'final_norm_w': final_norm_w}


def reference(x, c, ctx, c_ctx, ada_w, ada_b, norm_mix_w, norm_mlp_w, mlp_w1, mlp_w2,
              ev_w_in, ev_q_norm_w, ev_w_uq, ev_kv_norm_w, ev_w_ukv, ev_conv_w, ev_conv_b,
              ev_dt_bias, ev_a_log, ev_d_skip, ev_ssm_norm_w, ev_w_out,
              od_w_in, od_conv_w, od_conv_b, od_i_bias, od_f_bias, od_head_norm_w, od_w_out,
              final_norm_w):
    rope = rope2d_tables(x.shape[1])
    s_lat = jax.nn.silu(c)
    s_ctx = jax.nn.silu(c_ctx)
    for layer in range(DEPTH):
        last = layer == DEPTH - 1
        sh1, sc1, g1, sh2, sc2, g2 = jnp.split(s_lat @ ada_w[layer] + ada_b[layer], 6, axis=-1)
        csh1, csc1, cg1, csh2, csc2, cg2 = jnp.split(s_ctx @ ada_w[layer] + ada_b[layer], 6, axis=-1)
        h_lat = modulate(rmsnorm(x, norm_mix_w[layer]), sh1[:, None], sc1[:, None])
        h_ctx = modulate(rmsnorm(ctx, norm_mix_w[layer]), csh1, csc1)
        if layer % 2 == 0:
            e = layer // 2
            o_ctx, o_lat = mla_ssd_mixer(h_ctx, h_lat, rope, ev_w_in[e], ev_q_norm_w[e], ev_w_uq[e],
                                         ev_kv_norm_w[e], ev_w_ukv[e], ev_conv_w[e], ev_conv_b[e],
                                         ev_dt_bias[e], ev_a_log[e], ev_d_skip[e], ev_ssm_norm_w[e],
                                         ev_w_out[e], not last)
        else:
            o = layer // 2
            o_ctx, o_lat = mlstm_mixer(h_ctx, h_lat, od_w_in[o], od_conv_w[o], od_conv_b[o], od_i_bias[o],
                                       od_f_bias[o], od_head_norm_w[o], od_w_out[o], not last)
        x = x + g1[:, None] * o_lat
        x = x + g2[:, None] * sqrelu_mlp(modulate(rmsnorm(x, norm_mlp_w[layer]), sh2[:, None], sc2[:, None]),
                                         mlp_w1[layer], mlp_w2[layer])
        if not last:
            ctx = ctx + cg1 * o_ctx
            ctx = ctx + cg2 * sqrelu_mlp(modulate(rmsnorm(ctx, norm_mlp_w[layer]), csh2, csc2),
                                         mlp_w1[layer], mlp_w2[layer])
    return rmsnorm(x, final_norm_w)
```

```python
import bisect
from contextlib import ExitStack

import numpy as np
import concourse.bass as bass
import concourse.mybir as mybir
from concourse.bass_utils import run_bass_kernel_spmd

F32 = mybir.dt.float32
BF16 = mybir.dt.bfloat16
AF = mybir.ActivationFunctionType
ALU = mybir.AluOpType
AX = mybir.AxisListType

COMPUTE = ("pe", "act", "dve", "pool")
QUEUES = ("pe", "act", "dve", "pool", "sp")


class Buf:
    __slots__ = ("name", "w", "r", "dkey", "dcount", "psum")

    def __init__(self, name):
        self.name = name
        self.psum = False
        self.w = None
        self.r = []
        self.dkey = None
        self.dcount = 0


class T:
    __slots__ = ("ap", "buf")

    def __init__(self, ap, buf):
        self.ap = ap
        self.buf = buf

    def __getitem__(self, key):
        return T(self.ap[key], self.buf)

    def v(self, ap):
        return T(ap, self.buf)

    def sub(self, name, key):
        return T(self.ap[key], Buf(name))

    def bc(self, shape):
        return T(self.ap.to_broadcast(shape), self.buf)

    def re(self, s, **kw):
        return T(self.ap.rearrange(s, **kw), self.buf)


class Op:
    __slots__ = ("fn", "waits", "signal", "dma", "idx")

    def __init__(self, fn, waits, dma=None):
        self.fn = fn
        self.waits = waits
        self.signal = False
        self.dma = dma
        self.idx = 0


class Prog:
    def __init__(self):
        self.nc = bass.Bass("TRN2", target_bir_lowering=False)
        self.es = ExitStack()
        self.ops = {q: [] for q in QUEUES}
        self.count = {q: 0 for q in COMPUTE}
        self.known = {q: {} for q in QUEUES}
        self.snaps = {q: [(0, {})] for q in COMPUTE}
        self.dsnap = {}
        self.ndma = 0
        self.sbuf_used = 0
        self.psum_i = 0
        self.psum = []

    def dram(self, name, shape, dtype, kind):
        return self.nc.dram_tensor(name, list(shape), dtype, kind=kind).ap()

    def tile(self, name, shape, dtype):
        t = self.es.enter_context(self.nc.sbuf_tensor(name, list(shape), dtype))
        per = int(np.prod(shape[1:])) * (4 if dtype == F32 else 2)
        self.sbuf_used += per
        return T(t[:] if not isinstance(t, bass.AP) else t, Buf(name))

    def psum_tile(self, name, shape, dtype):
        t = self.es.enter_context(self.nc.psum_tensor(name, list(shape), dtype))
        b = Buf(name)
        b.psum = True
        return T(t[:] if not isinstance(t, bass.AP) else t, b)

    def _snapshot(self, key, count):
        if isinstance(key, str):
            sn = self.snaps[key]
            i = bisect.bisect_right(sn, count, key=lambda e: e[0]) - 1
            return sn[i][1]
        return self.dsnap.get((key, count), {})

    def _resolve(self, q, reads, writes):
        need = {}

        def add(tok, is_write_dep_on_write=False):
            if tok is None:
                return
            k, c = tok
            if need.get(k, 0) < c:
                need[k] = c

        def addw(w, pe_ok):
            if w is None:
                return
            if isinstance(w, list):
                for x in w:
                    add(x)
            elif not (pe_ok and w[0] == "pe"):
                add(w)

        for b in reads:
            addw(b.w, False)
            if b.psum:
                for tok in b.r:
                    if tok[0] != q:
                        add(tok)
        for b in writes:
            addw(b.w, q == "pe")
            for tok in b.r:
                add(tok)
        known = self.known[q]
        waits = []
        changed = False
        for k, c in need.items():
            if known.get(k, 0) >= c:
                continue
            waits.append((k, c))
        if waits:
            known = dict(known)
            for k, c in waits:
                snap = self._snapshot(k, c)
                for kk, cc in snap.items():
                    if known.get(kk, 0) < cc:
                        known[kk] = cc
                if known.get(k, 0) < c:
                    known[k] = c
            self.known[q] = known
            changed = True
        return waits, changed

    def op(self, q, fn, reads=(), writes=()):
        rb = [t.buf for t in reads if isinstance(t, T)]
        wb = [t.buf for t in writes if isinstance(t, T)]
        waits, changed = self._resolve(q, rb, wb)
        self.count[q] += 1
        idx = self.count[q]
        if changed:
            self.snaps[q].append((idx, self.known[q]))
        o = Op(fn, waits)
        o.idx = idx
        self.ops[q].append(o)
        tok = (q, idx)
        for b in rb:
            b.r.append(tok)
        for b in wb:
            b.w = tok
            b.r = []
        return o

    def dma(self, q, out, in_, owner=None, **kw):
        reads = [in_] if isinstance(in_, T) else []
        writes = [out] if isinstance(out, T) else []
        own = owner if owner is not None else (out if isinstance(out, T) else in_)
        ob = own.buf
        rb = [t.buf for t in reads]
        wb = [t.buf for t in writes]
        waits, changed = self._resolve(q, rb, wb)
        if q in COMPUTE:
            pass
        if ob.dkey is None:
            self.ndma += 1
            ob.dkey = self.ndma
        ob.dcount += 1
        tok = (ob.dkey, ob.dcount)
        self.dsnap[tok] = self.known[q]
        oap = out.ap if isinstance(out, T) else out
        iap = in_.ap if isinstance(in_, T) else in_
        o = Op(lambda e: e.dma_start(out=oap, in_=iap, **kw), waits, dma=ob.dkey)
        self.ops[q].append(o)
        for b in rb:
            b.r.append(tok)
        for b in wb:
            b.w = tok
            b.r = []
        return tok

    def wait_all_dma(self, q, toks):
        o = Op(None, list(toks))
        self.ops[q].append(o)

    def build(self):
        nc = self.nc
        sig = {q: set() for q in COMPUTE}
        for q in QUEUES:
            for o in self.ops[q]:
                for k, c in o.waits:
                    if isinstance(k, str):
                        sig[k].add(c)
        sigmap = {}
        for q in COMPUTE:
            s = sorted(sig[q])
            sigmap[q] = {c: i + 1 for i, c in enumerate(s)}
        for q in COMPUTE:
            for o in self.ops[q]:
                if o.dma is None and o.fn is not None and o.idx in sigmap[q]:
                    o.signal = True
        sems = {q: self.es.enter_context(nc.semaphore("s_" + q)) for q in COMPUTE}
        dsems = {k: self.es.enter_context(nc.semaphore("d%d" % k)) for k in range(1, self.ndma + 1)}
        self.nsig = {q: len(sigmap[q]) for q in COMPUTE}

        def emit(q, eng):
            for o in self.ops[q]:
                for k, c in o.waits:
                    if isinstance(k, str):
                        eng.wait_ge(sems[k], sigmap[k][c])
                    else:
                        eng.wait_ge(dsems[k], 16 * c)
                if o.fn is None:
                    continue
                ins = o.fn(eng)
                if o.dma is not None:
                    ins.then_inc(dsems[o.dma], 16)
                elif o.signal:
                    ins.then_inc(sems[q], 1)

        with nc.Block() as block:
            @block.tensor
            def _(e):
                emit("pe", e)

            @block.scalar
            def _(e):
                emit("act", e)

            @block.vector
            def _(e):
                emit("dve", e)

            @block.gpsimd
            def _(e):
                emit("pool", e)

            @block.sync
            def _(e):
                emit("sp", e)
        self.es.close()
        return nc

    @staticmethod
    def _a(x):
        return x.ap if isinstance(x, T) else x

    def mm(self, out, lhsT, rhs, start=True, stop=True):
        o, l, r = out.ap, lhsT.ap, rhs.ap
        return self.op("pe", lambda e: e.matmul(o, l, r, start=start, stop=stop),
                       reads=(lhsT, rhs), writes=(out,))

    def transpose(self, out, in_, ident):
        o, i, d = out.ap, in_.ap, ident.ap
        return self.op("pe", lambda e: e.transpose(o, i, d), reads=(in_, ident), writes=(out,))

    def act(self, out, in_, func, bias=0.0, scale=1.0, accum=None, q="act"):
        o, i = out.ap, in_.ap
        b, s = self._a(bias), self._a(scale)
        kw = {}
        if accum is not None:
            kw["accum_out"] = accum.ap
        wr = (out,) if accum is None else (out, accum)
        return self.op(q, lambda e: e.activation(o, i, func, bias=b, scale=s, **kw),
                       reads=(in_, bias, scale), writes=wr)

    def tt(self, q, out, in0, in1, op):
        o, a, b = out.ap, in0.ap, in1.ap
        return self.op(q, lambda e: e.tensor_tensor(o, a, b, op), reads=(in0, in1), writes=(out,))

    def ts(self, q, out, in0, s1, s2, op0, op1=None, accum=None):
        o, a = out.ap, in0.ap
        x1, x2 = self._a(s1), self._a(s2)
        kw = {}
        if op1 is not None:
            kw["op1"] = op1
        if accum is not None:
            kw["accum_out"] = accum.ap
        wr = (out,) if accum is None else (out, accum)
        return self.op(q, lambda e: e.tensor_scalar(o, a, x1, x2, op0, **kw),
                       reads=(in0, s1, s2), writes=wr)

    def stt(self, q, out, in0, scalar, in1, op0, op1):
        o, a, b = out.ap, in0.ap, in1.ap
        s = self._a(scalar)
        return self.op(q, lambda e: e.scalar_tensor_tensor(o, a, s, b, op0, op1),
                       reads=(in0, scalar, in1), writes=(out,))

    def copy(self, q, out, in_):
        o, i = out.ap, in_.ap
        if q == "act":
            return self.op(q, lambda e: e.copy(o, i), reads=(in_,), writes=(out,))
        return self.op(q, lambda e: e.tensor_copy(o, i), reads=(in_,), writes=(out,))

    def memset(self, q, out, val):
        o = out.ap
        return self.op(q, lambda e: e.memset(o, val), writes=(out,))

    def reduce(self, q, out, in_, op, axis=AX.X):
        o, i = out.ap, in_.ap
        return self.op(q, lambda e: e.tensor_reduce(o, i, axis, op), reads=(in_,), writes=(out,))

    def recip(self, out, in_):
        o, i = out.ap, in_.ap
        return self.op("dve", lambda e: e.reciprocal(o, i), reads=(in_,), writes=(out,))
import ml_dtypes

D = 1024
NL = 2048
NCX = 256
NT = NL + NCX
EPS = 1e-6
TILES = [(0, 512), (512, 512), (1024, 512), (1536, 512), (2048, 256)]
NCH = 18


class Rot:
    def __init__(self, items):
        self.items = items
        self.i = 0

    def next(self):
        t = self.items[self.i % len(self.items)]
        self.i += 1
        return t


class MK:
    def __init__(self, depth=4, mixers=("even", "odd"), dbg=None, stop=None):
        self.stop = stop
        self.depth = depth
        self.mixers = mixers
        P = self.P = Prog()
        self.dbg = dbg
        self.declare_io()
        self.alloc()
        self.prologue()
        for l in range(depth):
            self.layer(l)
        self.epilogue()

    def declare_io(self):
        P = self.P
        I = "ExternalInput"
        d = {}
        d["x"] = P.dram("x", [NL, D], F32, I)
        d["ctx"] = P.dram("ctx", [NCX, D], F32, I)
        d["c"] = P.dram("c", [8, 128], F32, I)
        d["c_ctx"] = P.dram("c_ctx", [8, 128], F32, I)
        d["ada_w"] = P.dram("ada_w", [4, D, 6 * D], F32, I)
        d["ada_b"] = P.dram("ada_b", [4 * 48, 128], F32, I)
        d["norm_mix_w"] = P.dram("norm_mix_w", [32, 128], F32, I)
        d["norm_mlp_w"] = P.dram("norm_mlp_w", [32, 128], F32, I)
        d["mlp_w1"] = P.dram("mlp_w1", [4, D, 4 * D], F32, I)
        d["mlp_w2"] = P.dram("mlp_w2", [4, 4 * D, D], F32, I)
        d["ev_w_in"] = P.dram("ev_w_in", [2, D, 2096], F32, I)
        d["ev_q_norm_w"] = P.dram("ev_q_norm_w", [4, 128], F32, I)
        d["ev_w_uq"] = P.dram("ev_w_uq", [2, 256, 768], F32, I)
        d["ev_kv_norm_w"] = P.dram("ev_kv_norm_w", [4, 128], F32, I)
        d["ev_w_ukv"] = P.dram("ev_w_ukv", [2, 256, 1024], F32, I)
        d["ev_conv_w"] = P.dram("ev_conv_w", [80, 128], F32, I)
        d["ev_conv_b"] = P.dram("ev_conv_b", [16, 128], F32, I)
        d["ev_dt_bias"] = P.dram("ev_dt_bias", [2, 16], F32, I)
        d["ev_a_log"] = P.dram("ev_a_log", [2, 16], F32, I)
        d["ev_d_skip"] = P.dram("ev_d_skip", [2, 8], F32, I)
        d["ev_ssm_norm_w"] = P.dram("ev_ssm_norm_w", [2, 512], F32, I)
        d["ev_w_out"] = P.dram("ev_w_out", [2, D, D], F32, I)
        d["od_w_in"] = P.dram("od_w_in", [2, D, 3104], F32, I)
        d["od_conv_w"] = P.dram("od_conv_w", [80, 128], F32, I)
        d["od_conv_b"] = P.dram("od_conv_b", [16, 128], F32, I)
        d["od_i_bias"] = P.dram("od_i_bias", [2, 16], F32, I)
        d["od_f_bias"] = P.dram("od_f_bias", [2, 16], F32, I)
        d["od_head_norm_w"] = P.dram("od_head_norm_w", [2, D], F32, I)
        d["od_w_out"] = P.dram("od_w_out", [2, D, D], F32, I)
        d["final_norm_w"] = P.dram("final_norm_w", [1, D], F32, I)
        d["cst"] = P.dram("cst", [128, 5 * 128], F32, I)
        d["ropeC"] = P.dram("ropeC", [32, NT], F32, I)
        d["ropeS"] = P.dram("ropeS", [32, NT], F32, I)
        self.d = d
        self.out = P.dram("out", [NL, D], F32, "ExternalOutput")
        zs = P.nc.dram_tensor("zscr", [NT, 512], BF16, kind="Internal").ap()
        self.zscr = zs
        self.zch = [T(zs[i * 128:(i + 1) * 128, :], Buf("zch%d" % i)) for i in range(NCH)]

    def alloc(self):
        P = self.P
        self.XT = P.tile("XT", [128, 8, NT], F32)
        self.xt = [[self.XT.sub("xt%d_%d" % (c, i), (slice(None), c, slice(t0, t0 + w)))
                    for i, (t0, w) in enumerate(TILES)] for c in range(8)]
        self.PR = P.tile("PR", [128, 49152], BF16)
        self.pr_bufs = []
        self.pr_fence = []
        self.arena = [P.tile("wa%d" % i, [128, 4096], BF16) for i in range(3)]
        self.arot = Rot(self.arena)
        self.SCR = P.tile("SCR", [128, 4096], BF16)
        self.cst = P.tile("cst_sb", [128, 5 * 128], F32)
        self.identf = self.cst[:, 0:128]
        self.tri = [self.cst[:, 128:256], self.cst[:, 256:384]]
        self.neg = [self.cst[:, 384:512], self.cst[:, 512:640]]
        self.ident = P.tile("ident", [128, 128], BF16)
        self.ones_bf = P.tile("ones_bf", [128, 128], BF16)
        self.ones_f = P.tile("ones_f", [128, 128], F32)
        self.epsc = P.tile("epsc", [128, 1], F32)
        self.VEC = P.tile("VEC", [128, 512], F32)
        self.SV = P.tile("SV", [128, 8, 2], BF16)
        self.MOD = [P.tile("MOD%d" % l, [128, 48, 2], F32) for l in range(4)]
        self.AB = [P.tile("AB%d" % l, [128, 2, 8, 2], F32) for l in range(4)]
        self.stat = Rot([P.tile("stat%d" % i, [128, 8], F32) for i in range(4)])
        banks = [P.psum_tile("pb%d" % i, [128, 512], F32) for i in range(8)]
        self.banks = banks
        self.psA = Rot(banks[0:4])
        self.psB = Rot(banks[4:7])
        self.psC = Rot(banks[7:8])
        self.scr_bufs = []
        self.scr_fence = []

    def _carve(self, base, off, shape, dtype, name, bufs, fence):
        n = int(np.prod(shape[1:]))
        nb = n * (4 if dtype == F32 else 2)
        assert off % 4 == 0 and off + nb <= base.ap.shape[1] * 2, (name, off, nb)
        ap = base.ap[0:shape[0], off // 2: (off + nb) // 2]
        if dtype == F32:
            ap = ap.bitcast(F32)
        if len(shape) > 2:
            names = " ".join("a%d" % i for i in range(len(shape) - 1))
            kw = {"a%d" % i: shape[i + 1] for i in range(len(shape) - 2)}
            ap = ap.rearrange("p (%s) -> p %s" % (names, names), **kw)
        b = Buf(name)
        b.r = list(fence)
        bufs.append(b)
        return T(ap, b)

    def pr(self, off_kib, shape, dtype, name):
        return self._carve(self.PR, int(off_kib * 1024), shape, dtype, name, self.pr_bufs, self.pr_fence)

    def pbump(self, shape, dtype, name):
        n = int(np.prod(shape[1:])) * (4 if dtype == F32 else 2)
        off = (self.bump + 31) // 32 * 32
        assert off + n <= self.bump_end, (name, off, n, self.bump_end)
        self.bump = off + n
        return self._carve(self.PR, off, shape, dtype, name, self.pr_bufs, self.pr_fence)

    def scr(self, off_kib, shape, dtype, name):
        return self._carve(self.SCR, int(off_kib * 1024), shape, dtype, name, self.scr_bufs, self.scr_fence)

    def fence(self):
        for bufs, attr in ((self.pr_bufs, "pr_fence"), (self.scr_bufs, "scr_fence")):
            toks = {}
            for b in bufs:
                ws = b.w if isinstance(b.w, list) else ([b.w] if b.w else [])
                for tok in ws + b.r:
                    if toks.get(tok[0], 0) < tok[1]:
                        toks[tok[0]] = tok[1]
            for k, c in getattr(self, attr):
                if toks.get(k, 0) < c:
                    toks[k] = c
            setattr(self, attr, list(toks.items()))
            for b in bufs:
                if len(b.r) > 8:
                    m = {}
                    for k, c in b.r:
                        if m.get(k, 0) < c:
                            m[k] = c
                    b.r = list(m.items())

    def subs(self, t, name):
        out = []
        for c in range(t.ap.shape[1]):
            row = []
            for i, (t0, w) in enumerate(TILES):
                b = Buf("%s%d_%d" % (name, c, i))
                b.r = list(t.buf.r)
                self.pr_bufs.append(b)
                row.append(T(t.ap[:, c, t0:t0 + w], b))
            out.append(row)
        return out

    def norm_scratch(self):
        self.sq = Rot([self.scr(i, [128, 512], BF16, "sq%d" % i) for i in range(2)])
        self.rs = Rot([self.scr(2, [128, 512], F32, "rs0")])
        self.tmpf = Rot([self.scr(4 + 2 * i, [128, 512], F32, "tmpf%d" % i) for i in range(2)])

    def wslot(self):
        return self.arot.next()

    def load_w(self, dram_ap, rows, cols, slot=None):
        P = self.P
        k = rows // 128
        s = slot if slot is not None else self.wslot()
        v = s[:, 0:k * cols].re("p (k n) -> p k n", k=k)
        P.dma("pool", v, dram_ap.rearrange("(k p) n -> p k n", p=128))
        return v

    def prologue(self):
        P = self.P
        d = self.d
        P.dma("sp", self.cst, d["cst"])
        P.copy("dve", self.ident, self.identf)
        P.memset("dve", self.ones_bf, 1.0)
        P.memset("dve", self.ones_f, 1.0)
        P.memset("dve", self.epsc, EPS)
        rows = [("ada_b", 192), ("norm_mix_w", 32), ("norm_mlp_w", 32), ("final_norm_w_rows", 8), ("c", 8),
                ("c_ctx", 8), ("ev_conv_w", 80), ("ev_conv_b", 16), ("od_conv_w", 80), ("od_conv_b", 16),
                ("ev_q_norm_w", 4), ("ev_kv_norm_w", 4)]
        self.voff = {}
        off = 0
        for n, k in rows:
            self.voff[n] = off
            off += k
        assert off <= 512
        stg = [self.pr(16 + 0.5 * i, [128, 128], F32, "vstg%d" % i) for i in range(4)]
        for s in stg:
            P.memset("dve", s, 0.0)
        for n, k in rows:
            src = d["final_norm_w"].rearrange("o (r p) -> (o r) p", p=128) if n == "final_norm_w_rows" else d[n]
            o = self.voff[n]
            done = 0
            while done < k:
                ti, r0 = divmod(o + done, 128)
                m = min(k - done, 128 - r0)
                P.dma("sp", stg[ti][r0:r0 + m, :], src[done:done + m, :])
                done += m
        for i in range(4):
            pt = self.psC.next()
            P.transpose(pt[:, 0:128], stg[i], self.identf)
            P.copy("dve", self.VEC[:, i * 128:(i + 1) * 128], pt[:, 0:128])
        oc, occ = self.voff["c"], self.voff["c_ctx"]
        P.act(self.SV[:, :, 0], self.VEC[:, oc:oc + 8], AF.Silu)
        P.act(self.SV[:, :, 1], self.VEC[:, occ:occ + 8], AF.Silu)
        xs = Rot([self.pr(4 * i, [128, D], F32, "xstg%d" % i) for i in range(4)])
        for ti, (t0, w) in enumerate(TILES):
            subs = []
            for j in range(w // 128):
                s = xs.next()
                tok = t0 + j * 128
                src = d["x"][tok:tok + 128, :] if tok < NL else d["ctx"][tok - NL:tok - NL + 128, :]
                P.dma("sp", s, src)
                subs.append(s)
            for c in range(8):
                pt = self.psA.next()
                for j, s in enumerate(subs):
                    P.transpose(pt[:, j * 128:(j + 1) * 128], s[:, c * 128:(c + 1) * 128], self.identf)
                eng = "act" if c % 2 else "dve"
                P.copy(eng, self.xt[c][ti], pt[:, 0:w])
        for q in range(12):
            self.ada_piece(0, q)
        self.ada_finish(0)

    def ada_load(self, l, q, slot=None):
        return self.load_w(self.d["ada_w"][l][:, q * 512:(q + 1) * 512], 1024, 512, slot=slot)

    def ada_mm(self, l, q, wv):
        P = self.P
        pt = self.psA.next()
        av = pt[:, 0:8].re("p (j v) -> p j v", v=2)
        for i in range(4):
            for k in range(8):
                P.mm(av[:, i, :], wv[:, k, i * 128:(i + 1) * 128], self.SV[:, k, :], start=(k == 0), stop=(k == 7))
        P.copy("act", self.MOD[l][:, q * 4:(q + 1) * 4, :], av)

    def ada_piece(self, l, q):
        self.ada_mm(l, q, self.ada_load(l, q))

    def ada_finish(self, l):
        P = self.P
        ob = self.voff["ada_b"] + l * 48
        P.tt("dve", self.MOD[l], self.MOD[l], self.VEC[:, ob:ob + 48].re("p (j o) -> p j o", o=1).bc([128, 48, 2]), ALU.add)
        for which, (scj, nwn) in enumerate(((1, "norm_mix_w"), (4, "norm_mlp_w"))):
            on = self.voff[nwn] + l * 8
            P.stt("dve", self.AB[l][:, which], self.MOD[l][:, scj * 8:(scj + 1) * 8, :], 1.0,
                  self.VEC[:, on:on + 8].re("p (j o) -> p j o", o=1).bc([128, 8, 2]), ALU.add, ALU.mult)

    class AdaStream:
        def __init__(self, mk, l):
            self.mk, self.l = mk, l
            self.q_loaded, self.q_done, self.wv = 0, 0, None
            self.slot = None

        def step(self, load=True):
            mk, l = self.mk, self.l
            if l is None:
                return
            if self.wv is not None:
                mk.ada_mm(l, self.q_done, self.wv)
                self.q_done += 1
                self.wv = None
            if load and self.q_loaded < 12:
                self.wv = mk.ada_load(l, self.q_loaded, slot=self.slot)
                self.q_loaded += 1

        def finish(self):
            if self.l is None:
                return
            while self.q_done < 12:
                self.step()
            self.mk.ada_finish(self.l)

    def norm_phase(self, l, which, tiles=None):
        P = self.P
        self.norm_scratch()
        self.HT = self.pr(60, [128, 8, NT], BF16, "HT")
        self.ht = self.subs(self.HT, "ht")
        shj = 0 if which == 0 else 3
        for ti, (t0, w) in enumerate(TILES):
            if tiles is not None and ti not in tiles:
                continue
            v = 1 if ti == 4 else 0
            ssp = self.psC.next()
            for c in range(8):
                sq = self.sq.next()
                if c % 4 == 3:
                    P.act(sq[:, :w], self.xt[c][ti], AF.Square)
                else:
                    P.tt("pool", sq[:, :w], self.xt[c][ti], self.xt[c][ti], ALU.mult)
                P.mm(ssp[:, :w], self.ones_bf, sq[:, :w], start=(c == 0), stop=(c == 7))
            rs = self.rs.next()
            P.act(rs[:, :w], ssp[:, :w], AF.Ln, bias=self.epsc, scale=1.0 / D)
            P.act(rs[:, :w], rs[:, :w], AF.Exp, scale=-0.5)
            for c in range(8):
                tmp = self.tmpf.next()
                P.tt("dve", tmp[:, :w], self.xt[c][ti], rs[:, :w], ALU.mult)
                P.act(self.ht[c][ti], tmp[:, :w], AF.Identity,
                      bias=self.MOD[l][:, shj * 8 + c, v:v + 1], scale=self.AB[l][:, which, c, v:v + 1])

    def mlp(self, l):
        P = self.P
        d = self.d
        last = (l == self.depth - 1)
        tiles = [i for i in range(5) if not (last and i == 4)]
        self.fence()
        self.norm_phase(l, 1, tiles)
        self.fence()
        self.relu = Rot([self.scr(i, [128, 512], BF16, "relu%d" % i) for i in range(3)])
        hid = [self.subs(self.pr(18 * b, [128, 4, NT], BF16, "hid%d" % b), "hid%d_" % b) for b in range(2)]
        nxt = None
        for g in range(8):
            w1 = self.load_w(d["mlp_w1"][l][:, g * 512:(g + 1) * 512], 1024, 512)
            w2 = self.load_w(d["mlp_w2"][l][g * 512:(g + 1) * 512, :], 512, 1024)
            hb = hid[g % 2]
            for ti in tiles:
                t0, w = TILES[ti]
                for hc in range(4):
                    pt = self.psA.next()
                    for k in range(8):
                        P.mm(pt[:, :w], w1[:, k, hc * 128:(hc + 1) * 128], self.ht[k][ti],
                             start=(k == 0), stop=(k == 7))
                    r = self.relu.next()
                    P.act(r[:, :w], pt[:, :w], AF.Relu)
                    P.tt("pool", hb[hc][ti], r[:, :w], r[:, :w], ALU.mult)
            if nxt is not None and g < 6:
                self.ada_piece(nxt, 2 * g)
            for ti in tiles:
                t0, w = TILES[ti]
                v = 1 if ti == 4 else 0
                for dc in range(8):
                    po = self.psB.next()
                    for hc in range(4):
                        P.mm(po[:, :w], w2[:, hc, dc * 128:(dc + 1) * 128], hb[hc][ti],
                             start=(hc == 0), stop=(hc == 3))
                    P.stt("dve", self.xt[dc][ti], po[:, :w], self.MOD[l][:, 40 + dc, v:v + 1], self.xt[dc][ti],
                          ALU.mult, ALU.add)
            if nxt is not None and g < 6:
                self.ada_piece(nxt, 2 * g + 1)
        if nxt is not None:
            self.ada_finish(nxt)

    def layer(self, l):
        if getattr(self, "halt", False):
            return
        self.fence()
        self.ada = MK.AdaStream(self, l + 1 if l + 1 < self.depth else None)
        if l % 2 == 0 and "even" in self.mixers:
            self.even_mixer(l)
        if l % 2 == 1 and "odd" in self.mixers:
            self.odd_mixer(l)
        if getattr(self, "halt", False):
            return
        self.ada.finish()
        self.mlp(l)

    def even_mixer(self, l):
        P = self.P
        d = self.d
        e = l // 2
        last = (l == self.depth - 1)
        tiles = [i for i in range(5) if not (last and i == 4)]
        do_ssd = "nossd" not in self.mixers
        do_attn = "noattn" not in self.mixers
        self.norm_phase(l, 0)
        self.fence()
        htall = self.join([t for row in self.ht for t in row], self.HT.ap, "htall")
        Win = d["ev_w_in"][e]
        XSB = self.pr(0, [128, 4, NT], BF16, "XSB")
        BCB = self.pr(18, [128, 4, NT], BF16, "BCB")
        CQT = self.pr(36, [128, 2, NT], BF16, "CQT")
        CKVT = self.pr(45, [128, 2, NT], BF16, "CKVT")
        KRAB = self.pr(54, [128, NT], BF16, "KRAB")
        self.bump, self.bump_end = int(58.5 * 1024), 60 * 1024
        DT = self.pbump([128, 18, 16], F32, "DT")
        A_b = self.pbump([128, 16], F32, "A_b")
        DTB = self.pbump([128, 16], F32, "DTB")
        DSK = self.pbump([128, 8], F32, "DSK")
        P.dma("sp", A_b, d["ev_a_log"][e:e + 1, :].to_broadcast([128, 16]))
        P.dma("sp", DTB, d["ev_dt_bias"][e:e + 1, :].to_broadcast([128, 16]))
        P.dma("sp", DSK, d["ev_d_skip"][e:e + 1, :].to_broadcast([128, 8]))
        P.act(A_b, A_b, AF.Exp)
        P.ts("dve", A_b, A_b, -1.0, None, ALU.mult)
        stg_rot = Rot([self.scr(1.03125 * i, [128, 516], BF16, "stg%d" % i) for i in range(3)])
        dg_rot = Rot([self.scr(3.125, [128, 5, 128], BF16, "dg0")])
        zs_rot = Rot([self.scr(4.5 + i, [128, 512], BF16, "zs%d" % i) for i in range(2)])
        cw = self.voff["ev_conv_w"] + e * 40
        cb = self.voff["ev_conv_b"] + e * 8
        wz = self.load_w(Win[:, 544:1056], 1024, 512)
        slot_dk = self.wslot()
        wdk = slot_dk[:, 0:8 * 112].re("p (k n) -> p k n", k=8)
        P.memset("pool", wdk[:, :, 48:80], 0.0)
        P.dma("pool", wdk[:, :, 0:16], Win[:, 2080:2096].rearrange("(k p) n -> p k n", p=128))
        P.dma("pool", wdk[:, :, 80:112], Win[:, 512:544].rearrange("(k p) n -> p k n", p=128))
        for b_ in range(2):
            for hf in range(2):
                so = 512 + 16 * b_ + 8 * (1 - hf)
                do = 16 + 16 * b_ + 8 * hf
                P.dma("pool", wdk[:, :, do:do + 8], Win[:, so:so + 8].rearrange("(k p) n -> p k n", p=128))
        for i in range(NCH):
            cs = slice(i * 128, (i + 1) * 128)
            pz = self.psA.next()
            for k in range(8):
                P.mm(pz, htall[:, k, cs], wz[:, k, :], start=(k == 0), stop=(k == 7))
            zs = zs_rot.next()
            P.act(zs, pz, AF.Silu)
            P.dma("sp", self.zch[i], zs, owner=zs)
            pd = self.psA.next()
            for k in range(8):
                P.mm(pd[:, 0:16], htall[:, k, cs], wdk[:, k, 0:16], start=(k == 0), stop=(k == 7))
            P.tt("dve", DT[:, i, :], pd[:, 0:16], DTB, ALU.add)
        P.act(DT, DT, AF.Exp)
        P.act(DT, DT, AF.Ln, bias=1.0)
        for ti, (t0, w) in enumerate(TILES):
            ts_ = slice(t0, t0 + w)
            pk = self.psA.next()
            for k in range(8):
                P.mm(pk[0:96, :w], wdk[:, k, 16:112], htall[:, k, ts_], start=(k == 0), stop=(k == 7))
            P.copy("act", KRAB[0:96, ts_], pk[0:96, :w])
        for half in range(2):
            wx = self.load_w(Win[:, 1056 + 512 * half:1056 + 512 * half + 512], 1024, 512)
            for cc in range(4):
                c = half * 4 + cc
                dst = XSB[:, c, :] if c < 4 else BCB[:, c - 4, :]
                self.conv_chunk(wx, cc * 128, cw + c, cb + c, dst, htall, stg_rot, dg_rot)
        self.fence()
        wl = self.load_w(Win[:, 0:512], 1024, 512)
        rawl = [self.scr(2 * i, [128, 512], F32, "rawl%d" % i) for i in range(2)]
        sqs = [self.scr(4 + i, [128, 512], BF16, "sql%d" % i) for i in range(2)]
        rsl = self.scr(6, [128, 512], F32, "rsl")
        for ti, (t0, w) in enumerate(TILES):
            ts_ = slice(t0, t0 + w)
            for lat in range(2):
                dstT = CQT if lat == 0 else CKVT
                nwo = self.voff["ev_q_norm_w" if lat == 0 else "ev_kv_norm_w"] + e * 2
                ssp = self.psC.next()
                for c in range(2):
                    pl = self.psA.next()
                    for k in range(8):
                        P.mm(pl[:, :w], wl[:, k, (lat * 2 + c) * 128:(lat * 2 + c + 1) * 128], htall[:, k, ts_],
                             start=(k == 0), stop=(k == 7))
                    P.copy("act", rawl[c][:, :w], pl[:, :w])
                    P.act(sqs[c][:, :w], rawl[c][:, :w], AF.Square)
                    P.mm(ssp[:, :w], self.ones_bf, sqs[c][:, :w], start=(c == 0), stop=(c == 1))
                P.act(rsl[:, :w], ssp[:, :w], AF.Ln, bias=self.epsc, scale=1.0 / 256)
                P.act(rsl[:, :w], rsl[:, :w], AF.Exp, scale=-0.5)
                for c in range(2):
                    P.stt("dve", dstT[:, c, ts_], rawl[c][:, :w], self.VEC[:, nwo + c:nwo + c + 1], rsl[:, :w],
                          ALU.mult, ALU.mult)
        self.fence()
        if do_ssd:
            self.ssd_scan(l, e, XSB, BCB, DT, A_b, DSK, last)
            wo = self.load_w(d["ev_w_out"][e][512:1024, :], 512, 1024)
            self.out_proj(l, wo, 4, [XSB[:, c, :] for c in range(4)], tiles)
        self.fence()
        if do_attn:
            self.attention(l, e, CQT, CKVT, KRAB, last, tiles)

    def ssd_scan(self, l, e, XSB, BCB, DT, A_b, DSK, last):
        P = self.P
        d = self.d
        self.bump, self.bump_end = 60 * 1024, 96 * 1024
        SNAP = self.pbump([128, 18, 512], BF16, "SSNAP")
        RA = self.pbump([128, 8, 128], F32, "RA")
        WTt = self.pbump([128, 8, 128], BF16, "WTt")
        CBM = [self.pbump([128, 2, 128], BF16, "CBM%d" % i) for i in range(2)]
        XS_TM = self.pbump([128, 8, 64], BF16, "XS_TM")
        BM_TM = self.pbump([128, 2, 128], BF16, "BM_TM")
        XP = [self.pbump([128, 8, 64], BF16, "XP%d" % i) for i in range(2)]
        H = self.pbump([128, 8, 64], F32, "Hst")
        HBF = self.pbump([128, 512], BF16, "HBF")
        SNW = self.pbump([128, 512], F32, "SNW")
        GNb = self.pbump([128, 512], BF16, "GNb")
        TMP = self.scr(0, [128, 512], F32, "ssd_tmp")
        ACCs = [self.scr(2 + 2 * i, [128, 512], F32, "ssd_acc%d" % i) for i in range(2)]
        zin = Rot([self.scr(6 + i, [128, 512], BF16, "zin%d" % i) for i in range(2)])
        junk = GNb[:, 0:256]
        P.dma("sp", SNW, d["ev_ssm_norm_w"][e:e + 1, :].to_broadcast([128, 512]))

        sl = self.wslot()
        def slv(k):
            return sl.v(sl.ap[:, k * 576:(k + 1) * 576].bitcast(F32).rearrange("p (d c h) -> p d c h", d=2, c=18))
        DTA, NACa, EACa, TOEa, CDa, DTOE = slv(0), slv(1), slv(2), slv(3), slv(4), slv(5)
        for dd in range(2):
            P.tt("dve", DTA[:, dd], DT[:, :, dd * 8:dd * 8 + 8],
                 A_b[:, dd * 8:dd * 8 + 8].re("p (o h) -> p o h", o=1).bc([128, 18, 8]), ALU.mult)
        pa = self.psA.next()
        pb = self.psA.next()
        for dd in range(2):
            P.mm(pa[:, dd * 144:(dd + 1) * 144], self.tri[dd], DTA[:, dd].re("p c h -> p (c h)"))
        P.mm(pb[:, 0:288], self.ones_f, DTA.re("p d c h -> p (d c h)"))
        pav = pa[:, 0:288].re("p (d c h) -> p d c h", d=2, c=18)
        pbv = pb[:, 0:288].re("p (d c h) -> p d c h", d=2, c=18)
        P.act(NACa, pav, AF.Copy, scale=-1.0)
        P.act(EACa, pav, AF.Exp)
        P.act(CDa, pbv, AF.Exp)
        P.tt("dve", TOEa, pbv, NACa, ALU.add)
        P.act(TOEa, TOEa, AF.Exp)
        for dd in range(2):
            P.tt("dve", DTOE[:, dd], TOEa[:, dd], DT[:, :, dd * 8:dd * 8 + 8], ALU.mult)

        sl2 = self.wslot()
        RA2 = sl2.v(sl2.ap[:, 0:2048].bitcast(F32).rearrange("p (h t) -> p h t", h=8))
        WTt2 = sl2.v(sl2.ap[:, 2048:3072].rearrange("p (h t) -> p h t", h=8))
        RAs, WTts = [RA, RA2], [WTt, WTt2]

        def prep(dd, i, need_decay):
            cs = slice(i * 128, (i + 1) * 128)
            dt = DT[:, i, dd * 8:dd * 8 + 8]
            dtA, NAC, EAC, TOE, CD = (DTA[:, dd, i, :], NACa[:, dd, i, :], EACa[:, dd, i, :], TOEa[:, dd, i, :], CDa[:, dd, i, :])
            r = dict(dt=dt, dtA=dtA, NAC=NAC, EAC=EAC, TOE=TOE, CD=CD, cs=cs, DTOE=DTOE[:, dd, i, :])
            return r

        def decay_a(dd, pr_):
            RAd = RAs[dd]
            P.tt("pool", RAd, self.tri[dd].re("p (o t) -> p o t", o=1).bc([128, 8, 128]),
                 pr_["dtA"].re("p (h o) -> p h o", o=1).bc([128, 8, 128]), ALU.mult)
            pA = [self.psA.next(), self.psA.next()]
            for hf in range(2):
                P.mm(pA[hf], self.ones_f, RAd[:, 4 * hf:4 * hf + 4, :].re("p h t -> p (h t)"))
            return pA

        def decay_b(dd, pr_, pA):
            RAd = RAs[dd]
            for hf in range(2):
                P.tt("dve", RAd[:, 4 * hf:4 * hf + 4, :], pA[hf].re("p (h t) -> p h t", h=4),
                     pr_["NAC"][:, 4 * hf:4 * hf + 4].re("p (h o) -> p h o", o=1).bc([128, 4, 128]), ALU.add)
            P.act(RAd, RAd, AF.Relu, scale=-1.0)
            P.act(RAd, RAd, AF.Exp, scale=-1.0)

        def transposes(i, need_x=True):
            cs = slice(i * 128, (i + 1) * 128)
            pt = self.psA.next()
            ptb = pt.v(pt.ap.bitcast(BF16))
            for c in range(4):
                P.transpose(ptb[:, c * 128:(c + 1) * 128], XSB[:, c, cs], self.ident)
            P.copy("act", XS_TM.re("p h v -> p (h v)"), ptb[:, 0:512])
            pt2 = self.psA.next()
            ptb2 = pt2.v(pt2.ap.bitcast(BF16))
            for g in range(2):
                P.transpose(ptb2[:, g * 128:(g + 1) * 128], BCB[:, g, cs], self.ident)
            P.copy("act", BM_TM.re("p g n -> p (g n)"), ptb2[:, 0:256])

        def state_update(pr_, xp, XPP):
            P.tt("pool", XPP, XS_TM, pr_["DTOE"].re("p (h o) -> p h o", o=1).bc([128, 8, 64]), ALU.mult)
            ph = self.psB.next()
            for g in range(2):
                P.mm(ph[:, g * 256:(g + 1) * 256], BM_TM[:, g, :], XPP[:, 4 * g:4 * g + 4, :].re("p h v -> p (h v)"))
            P.tt("dve", H, H, pr_["CD"].re("p (h o) -> p h o", o=1).bc([128, 8, 64]), ALU.mult)
            P.tt("dve", H.re("p h v -> p (h v)"), H.re("p h v -> p (h v)"), ph, ALU.add)

        P.memset("dve", H, 0.0)
        orderA = [16, 17] + list(range(16))
        for n_, i in enumerate(orderA):
            P.copy("dve", SNAP[:, i, :], H.re("p h v -> p (h v)"))
            if n_ == len(orderA) - 1:
                break
            pr_ = prep(0, i, False)
            transposes(i)
            state_update(pr_, None, XP[1])
        P.memset("dve", H, 0.0)
        orderB = [17, 16] + list(range(15, -1, -1))

        def head(n_, i, need_out):
            cs = slice(i * 128, (i + 1) * 128)
            ACC = ACCs[n_ % 2]
            transposes(i)
            zt = None
            if need_out:
                zt = zin.next()
                P.dma("sp", zt, self.zch[i], owner=zt)
                P.copy("dve", HBF, H.re("p h v -> p (h v)"))
                pcb = self.psA.next()
                for g in range(2):
                    P.mm(pcb[:, g * 128:(g + 1) * 128], BCB[:, g, cs], BCB[:, 2 + g, cs])
                pcv = pcb[:, 0:256].re("p (g t) -> p g t", g=2)
                for dd in range(2):
                    P.tt("dve", CBM[dd], pcv, self.tri[dd].re("p (o t) -> p o t", o=1).bc([128, 2, 128]), ALU.mult)
                P.tt("pool", ACC.re("p (h v) -> p h v", h=8), XS_TM, DSK.re("p (h o) -> p h o", o=1).bc([128, 8, 64]), ALU.mult)
            prs = [prep(dd, i, need_out) for dd in range(2)]
            for dd in range(2):
                P.tt("pool", XP[dd], XS_TM, prs[dd]["dt"].re("p (h o) -> p h o", o=1).bc([128, 8, 64]), ALU.mult)
            if need_out:
                pAs = [decay_a(dd, prs[dd]) for dd in range(2)]
                for dd in range(2):
                    decay_b(dd, prs[dd], pAs[dd])
                pys = []
                for dd in range(2):
                    for g in range(2):
                        P.tt("dve", WTts[dd][:, 4 * g:4 * g + 4, :], RAs[dd][:, 4 * g:4 * g + 4, :],
                             CBM[dd][:, g:g + 1, :].bc([128, 4, 128]), ALU.mult)
                    py = self.psB.next()
                    for h in range(8):
                        P.mm(py[:, h * 64:(h + 1) * 64], WTts[dd][:, h, :], XP[dd][:, h, :])
                    pys.append(py)
                for dd in range(2):
                    pyi = self.psB.next() if dd == 0 else self.psC.next()
                    hsrc = SNAP[:, i, :] if dd == 0 else HBF
                    for g in range(2):
                        P.mm(pyi[:, g * 256:(g + 1) * 256], BCB[:, 2 + g, cs], hsrc[:, g * 256:(g + 1) * 256])
                    P.tt("dve", TMP.re("p (h v) -> p h v", h=8), pyi.re("p (h v) -> p h v", h=8),
                         prs[dd]["EAC"].re("p (h o) -> p h o", o=1).bc([128, 8, 64]), ALU.mult)
                    P.tt("dve", TMP, TMP, pys[dd], ALU.add)
                    P.tt("pool", ACC, ACC, TMP, ALU.add)
            if n_ < len(orderB) - 1:
                state_update(prs[1], XP[1], XP[0])
            if not need_out:
                return None

            def tail():
                Rr = self.stat.next()
                P.tt("pool", ACC, ACC, zt, ALU.mult)
                for g in range(2):
                    P.act(junk, ACC[:, g * 256:(g + 1) * 256], AF.Square, accum=Rr[:, g:g + 1])
                P.act(Rr[:, 2:4], Rr[:, 0:2], AF.Ln, bias=self.epsc, scale=1.0 / 256)
                P.act(Rr[:, 2:4], Rr[:, 2:4], AF.Exp, scale=-0.5)
                for g in range(2):
                    gs = slice(g * 256, (g + 1) * 256)
                    P.stt("dve", GNb[:, gs], ACC[:, gs], Rr[:, 2 + g:3 + g], SNW[:, gs], ALU.mult, ALU.mult)
                pg = self.psA.next()
                pgb = pg.v(pg.ap.bitcast(BF16))
                for c in range(4):
                    P.transpose(pgb[:, c * 128:(c + 1) * 128], GNb[:, c * 128:(c + 1) * 128], self.ident)
                P.copy("act", XSB[:, :, cs], pgb[:, 0:512].re("p (c t) -> p c t", c=4))
            return tail

        pending = None
        for n_, i in enumerate(orderB):
            need_out = not (last and i >= 16)
            t_ = head(n_, i, need_out)
            if pending is not None:
                pending()
            pending = t_
        if pending is not None:
            pending()

    def attention(self, l, e, CQT, CKVT, KRAB, last, tiles):
        P = self.P
        d = self.d
        SCALE = 96.0 ** -0.5
        CT = self.pr(0, [128, NT], F32, "ropeCT")
        ST = self.pr(9, [128, NT], F32, "ropeST")
        GTA = self.pr(18, [128, 4, NT], BF16, "GTA")
        self.bump, self.bump_end = 60 * 1024, 96 * 1024
        QTs = [self.pbump([128, NT], BF16, "QT%d" % i) for i in range(2)]
        KTs = [self.pbump([128, NT], BF16, "KT%d" % i) for i in range(2)]
        VA = [self.pbump([128, 18, 128], BF16, "VA%d" % i) for i in range(2)]
        pts = Rot([self.pbump([128, 512], BF16, "PT%d" % i) for i in range(5)])
        T1 = self.scr(0, [128, 512], F32, "aT1")
        T2 = self.scr(2, [128, 512], F32, "aT2")
        RD = self.scr(4, [128, 512], F32, "aRD")
        T2s = self.scr(6, [128, 512], F32, "aT2s")
        P.memset("dve", CT[0:64, :], 1.0)
        P.dma("sp", CT[64:96, :], d["ropeC"])
        P.dma("sp", ST[64:96, :], d["ropeS"])
        P.dma("sp", ST[0:32, :], d["ropeS"])
        P.memset("dve", ST[32:64, :], 0.0)
        P.memset("pool", VA[0][:, :, 64:128], 1.0)
        P.memset("pool", VA[1][:, :, 0:64], 1.0)
        s1 = self.wslot()
        WUQ = s1[:, 0:1536].re("p (k n) -> p k n", k=2)
        WUQP = s1[:, 1536:3072].re("p (k n) -> p k n", k=2)
        P.memset("pool", WUQP, 0.0)
        P.dma("pool", WUQ, d["ev_w_uq"][e].rearrange("(k p) n -> p k n", p=128))
        src5 = d["ev_w_uq"][e].rearrange("(k p) (h f) -> p k h f", p=128, f=96)
        dst5 = WUQP.re("p k (h f) -> p k h f", f=96)
        for k in range(2):
            for b_ in range(2):
                for hf in range(2):
                    so = 64 + 16 * b_ + 8 * (1 - hf)
                    do = 64 + 16 * b_ + 8 * hf
                    P.dma("pool", dst5[:, k, :, do:do + 8], src5[:, k, :, so:so + 8])
        WUKV = self.load_w(d["ev_w_ukv"][e], 256, 1024)
        for ti, (t0, w) in enumerate(TILES):
            ts_ = slice(t0, t0 + w)
            P.tt("dve", T2[0:32, :w], KRAB[0:32, ts_], ST[0:32, ts_], ALU.mult)
            P.copy("act", T2s[64:96, :w], T2[0:32, :w])
            P.tt("dve", T1[64:96, :w], KRAB[64:96, ts_], CT[64:96, ts_], ALU.mult)
            P.tt("pool", KTs[0][64:96, ts_], T1[64:96, :w], T2s[64:96, :w], ALU.add)
            P.tt("pool", KTs[1][64:96, ts_], T1[64:96, :w], T2s[64:96, :w], ALU.add)

        def prep(h):
            par = h % 2
            QT, KT = QTs[par], KTs[par]
            for ti, (t0, w) in enumerate(TILES):
                ts_ = slice(t0, t0 + w)
                pa_ = self.psA.next()
                pb_ = self.psA.next()
                for k in range(2):
                    P.mm(pa_[0:96, :w], WUQ[:, k, h * 96:(h + 1) * 96], CQT[:, k, ts_], start=(k == 0), stop=(k == 1))
                for k in range(2):
                    P.mm(pb_[0:96, :w], WUQP[:, k, h * 96:(h + 1) * 96], CQT[:, k, ts_], start=(k == 0), stop=(k == 1))
                P.tt("dve", T1[0:96, :w], pa_[0:96, :w], CT[0:96, ts_], ALU.mult)
                P.tt("dve", T2[0:96, :w], pb_[0:96, :w], ST[0:96, ts_], ALU.mult)
                P.tt("pool", QT[0:96, ts_], T1[0:96, :w], T2[0:96, :w], ALU.add)
                pk = self.psA.next()
                for k in range(2):
                    P.mm(pk[0:64, :w], WUKV[:, k, h * 128:h * 128 + 64], CKVT[:, k, ts_], start=(k == 0), stop=(k == 1))
                P.copy("dve", KT[0:64, ts_], pk[0:64, :w])
            va = VA[par]
            vo = 64 * par
            for i0 in range(0, NCH, 8):
                n = min(8, NCH - i0)
                pv = self.psA.next()
                for jj in range(n):
                    i = i0 + jj
                    for k in range(2):
                        P.mm(pv[:, jj * 64:(jj + 1) * 64], CKVT[:, k, i * 128:(i + 1) * 128],
                             WUKV[:, k, h * 128 + 64:h * 128 + 128], start=(k == 0), stop=(k == 1))
                P.copy("dve", va[:, i0:i0 + n, vo:vo + 64], pv[:, 0:n * 64].re("p (c v) -> p c v", c=n))

        def attend(h):
            par = h % 2
            QT, KT, va = QTs[par], KTs[par], VA[par]
            orow = slice(64 * par, 64 * par + 64)
            drow = slice(64 * (1 - par), 64 * (1 - par) + 64)
            for ti in tiles:
                t0, w = TILES[ti]
                ts_ = slice(t0, t0 + w)
                chunks = list(range(NCH)) if ti < 4 else [16, 17]
                po = self.psB.next()
                pend = []

                def do_pv(idx, i, ptile):
                    P.mm(po[:, :w], va[:, i, :], ptile[:, :w], start=(idx == 0), stop=(idx == len(chunks) - 1))

                for idx, i in enumerate(chunks):
                    ps_ = self.psA.next()
                    P.mm(ps_[:, :w], KT[0:96, i * 128:(i + 1) * 128], QT[0:96, ts_])
                    ptile = pts.next()
                    P.act(ptile[:, :w], ps_[:, :w], AF.Exp, scale=SCALE)
                    pend.append((idx, i, ptile))
                    if len(pend) > 2:
                        do_pv(*pend.pop(0))
                while pend:
                    do_pv(*pend.pop(0))
                P.recip(RD[orow, :w], po[drow, :w])
                P.tt("dve", GTA[orow, h // 2, ts_], po[orow, :w], RD[orow, :w], ALU.mult)

        self.ada.slot = self.wslot()
        prep(0)
        for h in range(8):
            if h + 1 < 8:
                prep(h + 1)
            self.ada.step()
            attend(h)
            if h < 4:
                self.ada.step()
        if self.ada.wv is not None:
            self.ada.step(load=False)
        self.ada.slot = None
        if self.stop == "attn_end":
            self.halt = True
            return
        wo = self.load_w(d["ev_w_out"][e][0:512, :], 512, 1024)
        self.out_proj(l, wo, 4, [GTA[:, c, :] for c in range(4)], tiles)

    def join(self, tlist, ap, name):
        b = Buf(name)
        ws = []
        for t in tlist:
            w = t.buf.w
            if w is None:
                continue
            ws.extend(w if isinstance(w, list) else [w])
        b.w = ws
        self.pr_bufs.append(b)
        return T(ap, b)

    def bcast_load(self, dst, dram_row_ap, n):
        self.P.dma("sp", dst, dram_row_ap.to_broadcast([128, n]))

    CONV_TILES = [(0, 508), (508, 1016), (1016, 1524), (1524, 2032), (2032, 2048), (2048, 2304)]

    def conv_chunk(self, wv, col0, vec_w_off, vec_b_off, dst, htall, stg_rot, dg_rot):
        P = self.P
        dg = dg_rot.next()
        for j in range(5):
            P.ts("pool", dg[:, j, :], self.identf, self.VEC[:, vec_w_off + 8 * j:vec_w_off + 8 * j + 1], None, ALU.mult)
        for (a, b) in self.CONV_TILES:
            s0, s1 = (0, NL) if a < NL else (NL, NT)
            ia, ib = max(a - 2, s0), min(b + 2, s1)
            w = b - a
            win = ib - ia
            j0 = ia - (a - 2)
            pt = self.psA.next()
            for k in range(8):
                P.mm(pt[:, 0:win], wv[:, k, col0:col0 + 128], htall[:, k, ia:ib], start=(k == 0), stop=(k == 7))
            stg = stg_rot.next()
            if j0 > 0:
                P.memset("pool", stg[:, 0:j0], 0.0)
            if j0 + win < w + 4:
                P.memset("pool", stg[:, j0 + win:w + 4], 0.0)
            P.copy("act", stg[:, j0:j0 + win], pt[:, 0:win])
            pc = self.psB.next()
            for j in range(5):
                P.mm(pc[:, 0:w], dg[:, j, :], stg[:, j:j + w], start=(j == 0), stop=(j == 4))
            P.act(dst[:, a:b], pc[:, 0:w], AF.Silu, bias=self.VEC[:, vec_b_off:vec_b_off + 1])

    def out_proj(self, l, wv, nk, gts, tiles):
        P = self.P
        for ti in tiles:
            t0, w = TILES[ti]
            v = 1 if ti == 4 else 0
            for dc in range(8):
                po = self.psB.next()
                for k in range(nk):
                    P.mm(po[:, :w], wv[:, k, dc * 128:(dc + 1) * 128], gts[k][:, t0:t0 + w],
                         start=(k == 0), stop=(k == nk - 1))
                P.stt("dve", self.xt[dc][ti], po[:, :w], self.MOD[l][:, 16 + dc, v:v + 1], self.xt[dc][ti],
                      ALU.mult, ALU.add)

    def odd_mixer(self, l):
        P = self.P
        d = self.d
        o = l // 2
        last = (l == self.depth - 1)
        tiles = [i for i in range(5) if not (last and i == 4)]
        self.norm_phase(l, 0)
        self.fence()
        htall = self.join([t for row in self.ht for t in row], self.HT.ap, "htall")
        Win = d["od_w_in"][o]
        self.bump, self.bump_end = 41 * 1024, 60 * 1024
        IGF = self.pbump([128, 18, 32], F32, "IGF")
        LFn = self.pbump([128, 2, 18, 8], F32, "LFn")
        BALL = self.pbump([128, 2, 18, 8], F32, "BALL")
        EC = self.pbump([128, 2, 18, 8], F32, "EC")
        THR = self.pbump([128, 2, 18, 8], F32, "THR")
        EBL = self.pbump([128, 2, 18, 8], F32, "EBL")
        WT = self.pbump([128, 2, 18, 8], F32, "WT")
        HNW = self.pbump([128, D], F32, "HNW")
        BIAS = self.pbump([128, 32], F32, "BIAS")
        MSK2 = self.pbump([128, 2, 128], F32, "MSK2")
        MSK = [MSK2[:, dd, :] for dd in range(2)]

        tsm = Rot([self.pbump([128, 2, 128], BF16, "sm%d" % i) for i in range(2)])
        tv2 = Rot([self.pbump([128, 2, 2, 129], BF16, "v2_%d" % i) for i in range(1)])
        tv3 = Rot([self.pbump([128, 2, 129], BF16, "v3_%d" % i) for i in range(1)])
        KTM = self.scr(7.5, [128, 128], BF16, "KTM")
        CBF = self.pbump([128, 129], BF16, "CBF")
        GN = self.pbump([128, 2, 128], BF16, "GN")
        CST = [self.pbump([128, 129], F32, "C_dir0")]
        T0 = self.scr(4.5, [128, 2, 128], F32, "T0")
        HS = self.scr(5.5, [128, 2, 128], F32, "HS")
        HOs = [self.scr(6.5 + 0.5 * i, [128, 256], BF16, "HO%d" % i) for i in range(2)]
        sg_rot = Rot([T0.re("p h v -> p (h v)"), HS.re("p h v -> p (h v)")])
        self.bcast_load(HNW, d["od_head_norm_w"][o:o + 1, :], D)
        P.dma("sp", BIAS[:, 0:16], d["od_i_bias"][o:o + 1, :].to_broadcast([128, 16]))
        P.dma("sp", BIAS[:, 16:32], d["od_f_bias"][o:o + 1, :].to_broadcast([128, 16]))
        for dd in range(2):
            P.ts("dve", MSK2[:, dd, :], self.tri[dd], 0.125, None, ALU.mult)
        for j in range(4):
            self.fence()
            stg_rot = Rot([self.scr(1.03125 * i, [128, 516], BF16, "stg%d" % i) for i in range(3)])
            dg_rot = Rot([self.scr(3.125, [128, 5, 128], BF16, "dg0")])
            QKT = self.pr(0, [128, 2, NT], BF16, "QKT")
            VA = self.pr(9, [128, 18, 2, 129], BF16, "VA")
            OG = self.pr(18.25, [128, 18, 256], BF16, "OG")
            SNAP = self.pr(27.25, [128, 18, 129], BF16, "SNAP")
            GTp = self.pr(32, [128, 2, NT], BF16, "GTp")
            slotA = self.wslot()
            wA = slotA[:, 0:4096].re("p (k n) -> p k n", k=8)
            P.dma("pool", wA[:, :, 0:128], Win[:, 128 * j:128 * j + 128].rearrange("(k p) n -> p k n", p=128))
            P.dma("pool", wA[:, :, 128:256], Win[:, 512 + 128 * j:512 + 128 * j + 128].rearrange("(k p) n -> p k n", p=128))
            P.dma("pool", wA[:, :, 256:512], Win[:, 1024 + 256 * j:1024 + 256 * j + 256].rearrange("(k p) n -> p k n", p=128))
            slotB = self.wslot()
            wB = slotB[:, 0:8 * 288].re("p (k n) -> p k n", k=8)
            P.dma("pool", wB[:, :, 0:256], Win[:, 2048 + 256 * j:2048 + 256 * j + 256].rearrange("(k p) n -> p k n", p=128))
            if j == 0:
                P.dma("pool", wB[:, :, 256:288], Win[:, 3072:3104].rearrange("(k p) n -> p k n", p=128))
            P.memset("pool", VA[:, :, :, 128:129], 1.0)
            if self.stop == "wload":
                continue
            cw = self.voff["od_conv_w"] + o * 40
            cb = self.voff["od_conv_b"] + o * 8
            self.conv_chunk(wA, 0, cw + j, cb + j, QKT[:, 0, :], htall, stg_rot, dg_rot)
            self.conv_chunk(wA, 128, cw + 4 + j, cb + 4 + j, QKT[:, 1, :], htall, stg_rot, dg_rot)
            if self.stop == "conv":
                continue
            for i in range(NCH):
                pt = self.psA.next()
                for k in range(8):
                    P.mm(pt[:, 0:256], htall[:, k, i * 128:(i + 1) * 128], wA[:, k, 256:512], start=(k == 0), stop=(k == 7))
                for k in range(8):
                    P.mm(pt[:, 256:512], htall[:, k, i * 128:(i + 1) * 128], wB[:, k, 0:256], start=(k == 0), stop=(k == 7))
                P.copy("act", VA[:, i, :, 0:128], pt[:, 0:256].re("p (h v) -> p h v", h=2))
                sgt = sg_rot.next()
                P.act(sgt, pt[:, 256:512], AF.Exp, scale=-1.0)
                P.act(sgt, sgt, AF.Ln, bias=1.0)
                P.act(OG[:, i, :], sgt, AF.Exp, scale=-1.0)
                if j == 0 and self.stop != "vo_nogate":
                    pg = self.psA.next()
                    for k in range(8):
                        P.mm(pg[:, 0:32], htall[:, k, i * 128:(i + 1) * 128], wB[:, k, 256:288], start=(k == 0), stop=(k == 7))
                    P.tt("dve", IGF[:, i, :], pg[:, 0:32], BIAS, ALU.add)
            if self.stop in ("vo", "vo_nogate"):
                continue
            if j == 0:
                FGv = IGF[:, :, 16:32].re("p c (d h) -> p d c h", d=2)
                IGv = IGF[:, :, 0:16].re("p c (d h) -> p d c h", d=2)
                P.act(LFn, FGv, AF.Exp, scale=-1.0)
                P.act(LFn, LFn, AF.Ln, bias=1.0)
                P.ts("dve", LFn, LFn, -1.0, None, ALU.mult)
                pb_ = self.psA.next()
                pbv = pb_[:, 0:288].re("p (d c h) -> p d c h", c=18, d=2)
                for dd in range(2):
                    P.mm(pb_[:, dd * 144:(dd + 1) * 144], self.tri[dd], LFn[:, dd].re("p c h -> p (c h)"))
                P.copy("dve", BALL, pbv)
                pe_ = self.psA.next()
                P.mm(pe_[:, 0:288], self.ones_f, LFn.re("p d c h -> p (d c h)"))
                P.act(EBL, pe_[:, 0:288].re("p (d c h) -> p d c h", c=18, d=2), AF.Exp)
                P.tt("dve", EC, IGv, BALL, ALU.subtract)
                P.act(EC, EC, AF.Exp)
                P.act(THR, BALL, AF.Exp, scale=-1.0)
                P.tt("dve", WT, EC, EBL, ALU.mult)
            h0 = 2 * j
            if self.stop == "inproj":
                continue

            def state_update(Cst, i, dd):
                ptk = self.psA.next()
                ptkb = ptk.v(ptk.ap.bitcast(BF16))
                P.transpose(ptkb[:, 0:128], QKT[:, 1, i * 128:(i + 1) * 128], self.ident)
                P.copy("act", KTM, ptkb[:, 0:128])
                v3 = tv3.next()
                P.tt("pool", v3, VA[:, i], WT[:, dd, i, h0:h0 + 2].re("p (h o) -> p h o", o=1).bc([128, 2, 129]), ALU.mult)
                pd = self.psA.next()
                pdv = pd[:, 0:258].re("p (h n) -> p h n", h=2)
                for h in range(2):
                    P.mm(pdv[:, h, :], KTM, v3[:, h, :])
                for h in range(2):
                    r = slice(64 * h, 64 * h + 64)
                    P.stt("dve", Cst[r, :], Cst[r, :], EBL[r, dd, i, h0 + h:h0 + h + 1], pdv[r, h, :], ALU.mult, ALU.add)

            Cst = CST[0]
            P.memset("dve", Cst, 0.0)
            orderA = [16, 17] + list(range(16))
            for n_, i in enumerate(orderA):
                P.ts("dve", SNAP[:, i, :], Cst, 0.125, None, ALU.mult)
                if n_ % 4 == 2:
                    self.ada.step(load=(n_ < 14))
                if n_ < len(orderA) - 1:
                    state_update(Cst, i, 0)
            if self.stop == "passA":
                continue
            P.memset("dve", Cst, 0.0)
            orderB = [17, 16] + list(range(15, -1, -1))
            self.fence()
            accb = self.scr(0, [128, 512], F32, "accb")
            HSs = [HS, accb[:, 0:256].re("p (h v) -> p h v", h=2)]
            junk2 = accb[:, 256:384]
            TP = accb.v(accb.ap[:, 384:512].bitcast(BF16))

            def lockstep(*gens):
                gens = [g for g in gens if g is not None]
                while gens:
                    for g in list(gens):
                        try:
                            next(g)
                        except StopIteration:
                            gens.remove(g)

            def head(n_, i, out):
                cs = slice(i * 128, (i + 1) * 128)
                HSc = HSs[n_ % 2]
                HO = HOs[n_ % 2]
                pst = []
                for h in range(2):
                    r = slice(64 * h, 64 * h + 64)
                    pb1 = self.psA.next()
                    P.mm(pb1[:, 0:128], QKT[r, 1, cs], QKT[r, 0, cs])
                    pst.append(pb1)
                P.ts("dve", CBF, Cst, 0.125, None, ALU.mult)
                yield
                Rr = self.stat.next()
                smh = [tsm.next(), tsm.next()]
                for h in range(2):
                    P.tt("dve", smh[h], pst[h][:, 0:128].re("p (o t) -> p o t", o=1).bc([128, 2, 128]), MSK2, ALU.mult)
                    yield
                v2b = tv2.next()
                P.tt("pool", v2b, VA[:, i].re("p (o h) n -> p o h n", o=1).bc([128, 2, 2, 129]),
                     EC[:, :, i, h0:h0 + 2].re("p d (h o) -> p d h o", o=1).bc([128, 2, 2, 129]), ALU.mult)
                yield
                P.tt("pool", HO, OG[:, i, :], HNW[:, h0 * 128:(h0 + 2) * 128], ALU.mult)
                yield
                poss = [[None, None], [None, None]]
                for h in range(2):
                    r = slice(64 * h, 64 * h + 64)
                    for dd in range(2):
                        po = self.psB2[h].next()
                        cb_ = SNAP[:, i, :] if dd == 0 else CBF
                        P.mm(po[:, 0:129], smh[h][:, dd, :], v2b[:, dd, h, :], start=True, stop=False)
                        P.mm(po[:, 0:129], QKT[r, 0, cs], cb_[r, :], start=False, stop=True)
                        yield
                        c = 2 * dd + h
                        P.act(Rr[:, c:c + 1], po[:, 128:129], AF.Abs)
                        yield
                        poss[h][dd] = po
                P.tt("dve", Rr[:, 0:4].re("p (d h) -> p d h", d=2), Rr[:, 0:4].re("p (d h) -> p d h", d=2),
                     THR[:, :, i, h0:h0 + 2], ALU.max)
                yield
                P.recip(Rr[:, 0:4], Rr[:, 0:4])
                yield
                for h in range(2):
                    P.act(T0[:, h, :], poss[h][0][:, 0:128], AF.Identity, scale=Rr[:, h:h + 1])
                    yield
                for h in range(2):
                    P.stt("dve", HSc[:, h, :], poss[h][1][:, 0:128], Rr[:, 2 + h:3 + h], T0[:, h, :], ALU.mult, ALU.add)
                    yield

                def tail():
                    for h in range(2):
                        P.act(junk2, HSc[:, h, :], AF.Square, accum=Rr[:, 4 + h:5 + h])
                        yield
                    P.act(Rr[:, 6:8], Rr[:, 4:6], AF.Ln, bias=self.epsc, scale=1.0 / 128)
                    yield
                    P.act(Rr[:, 6:8], Rr[:, 6:8], AF.Exp, scale=-0.5)
                    yield
                    P.tt("pool", TP.re("p (h v) -> p h v", h=2), HSc, Rr[:, 6:8].re("p (h o) -> p h o", o=1).bc([128, 2, 128]), ALU.mult)
                    yield
                    P.tt("pool", GN.re("p h v -> p (h v)"), TP, HO, ALU.mult)
                    yield
                    ptg = self.psA.next()
                    ptgb = ptg.v(ptg.ap.bitcast(BF16))
                    for h in range(2):
                        P.transpose(ptgb[:, h * 128:(h + 1) * 128], GN[:, h, :], self.ident)
                    yield
                    P.copy("act", GTp[:, :, cs], ptgb[:, 0:256].re("p (h t) -> p h t", h=2))
                    yield
                out.append(tail)

            self.psB2 = [Rot(self.banks[4:6]), Rot(self.banks[6:8])]
            pending = None
            for n_, i in enumerate(orderB):
                need_out = not (last and i >= 16)
                out = []
                hg = head(n_, i, out) if need_out else None
                lockstep(hg, pending() if pending is not None else None)
                pending = out[0] if out else None
                if n_ < len(orderB) - 1:
                    state_update(Cst, i, 1)
            if pending is not None:
                lockstep(pending())
            if self.stop in ("passB", "B1", "B2", "B3"):
                continue
            wo = self.load_w(d["od_w_out"][o][256 * j:256 * j + 256, :], 256, 1024)
            self.out_proj(l, wo, 2, [GTp[:, 0, :], GTp[:, 1, :]], tiles)

    def epilogue(self):
        P = self.P
        self.fence()
        ostg = Rot([self.pr(4 * i, [128, D], F32, "ostg%d" % i) for i in range(2)])
        junk = self.pr(8, [128, 512], BF16, "junk")
        self.FNW = self.pr(12, [128, D], F32, "FNW")
        P.dma("sp", self.FNW, self.d["final_norm_w"].to_broadcast([128, D]))
        toks = []
        for t in range(16):
            ti, j = divmod(t, 4)
            pa = self.psA.next()
            pb = self.psA.next()
            for c in range(8):
                dst = pa if c < 4 else pb
                cc = c % 4
                P.transpose(dst[:, cc * 128:(cc + 1) * 128], self.xt[c][ti][:, j * 128:(j + 1) * 128], self.identf)
            st = self.stat.next()
            P.act(junk, pa, AF.Square, accum=st[:, 0:1])
            P.act(junk, pb, AF.Square, accum=st[:, 1:2])
            P.tt("dve", st[:, 2:3], st[:, 0:1], st[:, 1:2], ALU.add)
            P.act(st[:, 3:4], st[:, 2:3], AF.Ln, bias=self.epsc, scale=1.0 / D)
            P.act(st[:, 4:5], st[:, 3:4], AF.Exp, scale=-0.5)
            o = ostg.next()
            P.stt("dve", o[:, 0:512], pa, st[:, 4:5], self.FNW[:, 0:512], ALU.mult, ALU.mult)
            P.stt("dve", o[:, 512:1024], pb, st[:, 4:5], self.FNW[:, 512:1024], ALU.mult, ALU.mult)
            toks.append(P.dma("sp", self.out[t * 128:(t + 1) * 128, :], o))
        P.wait_all_dma("sp", toks)


def host_consts():
    p = np.arange(128)
    ident = np.eye(128, dtype=np.float32)
    tri0 = (p[:, None] <= p[None, :]).astype(np.float32)
    tri1 = (p[:, None] >= p[None, :]).astype(np.float32)
    neg0 = np.where(p[:, None] <= p[None, :], 0.0, -30000.0).astype(np.float32)
    neg1 = np.where(p[:, None] >= p[None, :], 0.0, -30000.0).astype(np.float32)
    cst = np.concatenate([ident, tri0, tri1, neg0, neg1], axis=1)
    pos = np.arange(NL)
    row = (pos // 64).astype(np.float32)
    col = (pos % 64).astype(np.float32)
    half = 16
    inv = (1.0 / (10000.0 ** (np.arange(0, half, 2, dtype=np.float32) / half))).astype(np.float32)
    ar = row[None, :] * inv[:, None]
    ac = col[None, :] * inv[:, None]
    C = np.ones((32, NT), np.float32)
    S = np.zeros((32, NT), np.float32)
    C[0:8, :NL] = np.cos(ar); C[8:16, :NL] = np.cos(ar); C[16:24, :NL] = np.cos(ac); C[24:32, :NL] = np.cos(ac)
    S[0:8, :NL] = -np.sin(ar); S[8:16, :NL] = np.sin(ar); S[16:24, :NL] = -np.sin(ac); S[24:32, :NL] = np.sin(ac)
    import os
    if os.environ.get("NOROPE"):
        C[:] = 1.0
        S[:] = 0.0
    return cst, C, S


_NC_CACHE = {}


def make_in_maps(inputs, n):
    cst, C, S = host_consts()
    f = lambda a: np.ascontiguousarray(np.asarray(a, dtype=np.float32))
    shared = {
        "c_ctx": f(inputs["c_ctx"]).reshape(8, 128),
        "ada_w": f(inputs["ada_w"]), "ada_b": f(inputs["ada_b"]).reshape(192, 128),
        "norm_mix_w": f(inputs["norm_mix_w"]).reshape(32, 128), "norm_mlp_w": f(inputs["norm_mlp_w"]).reshape(32, 128),
        "mlp_w1": f(inputs["mlp_w1"]), "mlp_w2": f(inputs["mlp_w2"]),
        "ev_w_in": f(inputs["ev_w_in"]), "ev_q_norm_w": f(inputs["ev_q_norm_w"]).reshape(4, 128),
        "ev_w_uq": f(inputs["ev_w_uq"]), "ev_kv_norm_w": f(inputs["ev_kv_norm_w"]).reshape(4, 128),
        "ev_w_ukv": f(inputs["ev_w_ukv"]), "ev_conv_w": f(inputs["ev_conv_w"]).reshape(80, 128),
        "ev_conv_b": f(inputs["ev_conv_b"]).reshape(16, 128), "ev_dt_bias": f(inputs["ev_dt_bias"]).reshape(2, 16),
        "ev_a_log": f(inputs["ev_a_log"]).reshape(2, 16), "ev_d_skip": f(inputs["ev_d_skip"]),
        "ev_ssm_norm_w": f(inputs["ev_ssm_norm_w"]), "ev_w_out": f(inputs["ev_w_out"]),
        "od_w_in": f(inputs["od_w_in"]), "od_conv_w": f(inputs["od_conv_w"]).reshape(80, 128),
        "od_conv_b": f(inputs["od_conv_b"]).reshape(16, 128), "od_i_bias": f(inputs["od_i_bias"]).reshape(2, 16),
        "od_f_bias": f(inputs["od_f_bias"]).reshape(2, 16), "od_head_norm_w": f(inputs["od_head_norm_w"]),
        "od_w_out": f(inputs["od_w_out"]), "final_norm_w": f(inputs["final_norm_w"]).reshape(1, D),
        "cst": cst, "ropeC": C, "ropeS": S,
    }
    x = f(inputs["x"]); c = f(inputs["c"]); ctx = f(inputs["ctx"])
    maps = []
    for b in range(n):
        m = dict(shared)
        m["x"] = x[b]
        m["ctx"] = ctx[b]
        m["c"] = c[b].reshape(8, 128)
        maps.append(m)
    return maps


def kernel(**inputs):
    n = 8
    key = "full"
    if key not in _NC_CACHE:
        mk = MK()
        _NC_CACHE[key] = (mk, mk.P.build())
    mk, nc = _NC_CACHE[key]
    maps = make_in_maps(inputs, n)
    used = set(mk.d.keys())
    maps = [{k: v for k, v in m.items() if k in used} for m in maps]
    res = run_bass_kernel_spmd(nc, maps, core_ids=list(range(n)))
    return np.stack([r["out"] for r in res.results], axis=0).astype(np.float32)
```

```python
import bisect
from contextlib import ExitStack

import numpy as np
import concourse.bass as bass
import concourse.mybir as mybir
from concourse.bass_utils import run_bass_kernel_spmd

F32 = mybir.dt.float32
BF16 = mybir.dt.bfloat16
AF = mybir.ActivationFunctionType
ALU = mybir.AluOpType
AX = mybir.AxisListType

COMPUTE = ("pe", "act", "dve", "pool")
QUEUES = ("pe", "act", "dve", "pool", "sp")


class Buf:
    __slots__ = ("name", "w", "r", "dkey", "dcount", "psum")

    def __init__(self, name):
        self.name = name
        self.psum = False
        self.w = None
        self.r = []
        self.dkey = None
        self.dcount = 0


class T:
    __slots__ = ("ap", "buf")

    def __init__(self, ap, buf):
        self.ap = ap
        self.buf = buf

    def __getitem__(self, key):
        return T(self.ap[key], self.buf)

    def v(self, ap):
        return T(ap, self.buf)

    def sub(self, name, key):
        return T(self.ap[key], Buf(name))

    def bc(self, shape):
        return T(self.ap.to_broadcast(shape), self.buf)

    def re(self, s, **kw):
        return T(self.ap.rearrange(s, **kw), self.buf)


class Op:
    __slots__ = ("fn", "waits", "signal", "dma", "idx")

    def __init__(self, fn, waits, dma=None):
        self.fn = fn
        self.waits = waits
        self.signal = False
        self.dma = dma
        self.idx = 0


class Prog:
    def __init__(self):
        self.nc = bass.Bass("TRN2", target_bir_lowering=False)
        self.es = ExitStack()
        self.ops = {q: [] for q in QUEUES}
        self.count = {q: 0 for q in COMPUTE}
        self.known = {q: {} for q in QUEUES}
        self.snaps = {q: [(0, {})] for q in COMPUTE}
        self.dsnap = {}
        self.ndma = 0
        self.sbuf_used = 0
        self.psum_i = 0
        self.psum = []

    def dram(self, name, shape, dtype, kind):
        return self.nc.dram_tensor(name, list(shape), dtype, kind=kind).ap()

    def tile(self, name, shape, dtype):
        t = self.es.enter_context(self.nc.sbuf_tensor(name, list(shape), dtype))
        per = int(np.prod(shape[1:])) * (4 if dtype == F32 else 2)
        self.sbuf_used += per
        return T(t[:] if not isinstance(t, bass.AP) else t, Buf(name))

    def psum_tile(self, name, shape, dtype):
        t = self.es.enter_context(self.nc.psum_tensor(name, list(shape), dtype))
        b = Buf(name)
        b.psum = True
        return T(t[:] if not isinstance(t, bass.AP) else t, b)

    def _snapshot(self, key, count):
        if isinstance(key, str):
            sn = self.snaps[key]
            i = bisect.bisect_right(sn, count, key=lambda e: e[0]) - 1
            return sn[i][1]
        return self.dsnap.get((key, count), {})

    def _resolve(self, q, reads, writes):
        need = {}

        def add(tok, is_write_dep_on_write=False):
            if tok is None:
                return
            k, c = tok
            if need.get(k, 0) < c:
                need[k] = c

        def addw(w, pe_ok):
            if w is None:
                return
            if isinstance(w, list):
                for x in w:
                    add(x)
            elif not (pe_ok and w[0] == "pe"):
                add(w)

        for b in reads:
            addw(b.w, False)
            if b.psum:
                for tok in b.r:
                    if tok[0] != q:
                        add(tok)
        for b in writes:
            addw(b.w, q == "pe")
            for tok in b.r:
                add(tok)
        known = self.known[q]
        waits = []
        changed = False
        for k, c in need.items():
            if known.get(k, 0) >= c:
                continue
            waits.append((k, c))
        if waits:
            known = dict(known)
            for k, c in waits:
                snap = self._snapshot(k, c)
                for kk, cc in snap.items():
                    if known.get(kk, 0) < cc:
                        known[kk] = cc
                if known.get(k, 0) < c:
                    known[k] = c
            self.known[q] = known
            changed = True
        return waits, changed

    def op(self, q, fn, reads=(), writes=()):
        rb = [t.buf for t in reads if isinstance(t, T)]
        wb = [t.buf for t in writes if isinstance(t, T)]
        waits, changed = self._resolve(q, rb, wb)
        self.count[q] += 1
        idx = self.count[q]
        if changed:
            self.snaps[q].append((idx, self.known[q]))
        o = Op(fn, waits)
        o.idx = idx
        self.ops[q].append(o)
        tok = (q, idx)
        for b in rb:
            b.r.append(tok)
        for b in wb:
            b.w = tok
            b.r = []
        return o

    def dma(self, q, out, in_, owner=None, **kw):
        reads = [in_] if isinstance(in_, T) else []
        writes = [out] if isinstance(out, T) else []
        own = owner if owner is not None else (out if isinstance(out, T) else in_)
        ob = own.buf
        rb = [t.buf for t in reads]
        wb = [t.buf for t in writes]
        waits, changed = self._resolve(q, rb, wb)
        if q in COMPUTE:
            pass
        if ob.dkey is None:
            self.ndma += 1
            ob.dkey = self.ndma
        ob.dcount += 1
        tok = (ob.dkey, ob.dcount)
        self.dsnap[tok] = self.known[q]
        oap = out.ap if isinstance(out, T) else out
        iap = in_.ap if isinstance(in_, T) else in_
        o = Op(lambda e: e.dma_start(out=oap, in_=iap, **kw), waits, dma=ob.dkey)
        self.ops[q].append(o)
        for b in rb:
            b.r.append(tok)
        for b in wb:
            b.w = tok
            b.r = []
        return tok

    def wait_all_dma(self, q, toks):
        o = Op(None, list(toks))
        self.ops[q].append(o)

    def build(self):
        nc = self.nc
        sig = {q: set() for q in COMPUTE}
        for q in QUEUES:
            for o in self.ops[q]:
                for k, c in o.waits:
                    if isinstance(k, str):
                        sig[k].add(c)
        sigmap = {}
        for q in COMPUTE:
            s = sorted(sig[q])
            sigmap[q] = {c: i + 1 for i, c in enumerate(s)}
        for q in COMPUTE:
            for o in self.ops[q]:
                if o.dma is None and o.fn is not None and o.idx in sigmap[q]:
                    o.signal = True
        sems = {q: self.es.enter_context(nc.semaphore("s_" + q)) for q in COMPUTE}
        dsems = {k: self.es.enter_context(nc.semaphore("d%d" % k)) for k in range(1, self.ndma + 1)}
        self.nsig = {q: len(sigmap[q]) for q in COMPUTE}

        def emit(q, eng):
            for o in self.ops[q]:
                for k, c in o.waits:
                    if isinstance(k, str):
                        eng.wait_ge(sems[k], sigmap[k][c])
                    else:
                        eng.wait_ge(dsems[k], 16 * c)
                if o.fn is None:
                    continue
                ins = o.fn(eng)
                if o.dma is not None:
                    ins.then_inc(dsems[o.dma], 16)
                elif o.signal:
                    ins.then_inc(sems[q], 1)

        with nc.Block() as block:
            @block.tensor
            def _(e):
                emit("pe", e)

            @block.scalar
            def _(e):
                emit("act", e)

            @block.vector
            def _(e):
                emit("dve", e)

            @block.gpsimd
            def _(e):
                emit("pool", e)

            @block.sync
            def _(e):
                emit("sp", e)
        self.es.close()
        return nc

    @staticmethod
    def _a(x):
        return x.ap if isinstance(x, T) else x

    def mm(self, out, lhsT, rhs, start=True, stop=True):
        o, l, r = out.ap, lhsT.ap, rhs.ap
        return self.op("pe", lambda e: e.matmul(o, l, r, start=start, stop=stop),
                       reads=(lhsT, rhs), writes=(out,))

    def transpose(self, out, in_, ident):
        o, i, d = out.ap, in_.ap, ident.ap
        return self.op("pe", lambda e: e.transpose(o, i, d), reads=(in_, ident), writes=(out,))

    def act(self, out, in_, func, bias=0.0, scale=1.0, accum=None, q="act"):
        o, i = out.ap, in_.ap
        b, s = self._a(bias), self._a(scale)
        kw = {}
        if accum is not None:
            kw["accum_out"] = accum.ap
        wr = (out,) if accum is None else (out, accum)
        return self.op(q, lambda e: e.activation(o, i, func, bias=b, scale=s, **kw),
                       reads=(in_, bias, scale), writes=wr)

    def tt(self, q, out, in0, in1, op):
        o, a, b = out.ap, in0.ap, in1.ap
        return self.op(q, lambda e: e.tensor_tensor(o, a, b, op), reads=(in0, in1), writes=(out,))

    def ts(self, q, out, in0, s1, s2, op0, op1=None, accum=None):
        o, a = out.ap, in0.ap
        x1, x2 = self._a(s1), self._a(s2)
        kw = {}
        if op1 is not None:
            kw["op1"] = op1
        if accum is not None:
            kw["accum_out"] = accum.ap
        wr = (out,) if accum is None else (out, accum)
        return self.op(q, lambda e: e.tensor_scalar(o, a, x1, x2, op0, **kw),
                       reads=(in0, s1, s2), writes=wr)

    def stt(self, q, out, in0, scalar, in1, op0, op1):
        o, a, b = out.ap, in0.ap, in1.ap
        s = self._a(scalar)
        return self.op(q, lambda e: e.scalar_tensor_tensor(o, a, s, b, op0, op1),
                       reads=(in0, scalar, in1), writes=(out,))

    def copy(self, q, out, in_):
        o, i = out.ap, in_.ap
        if q == "act":
            return self.op(q, lambda e: e.copy(o, i), reads=(in_,), writes=(out,))
        return self.op(q, lambda e: e.tensor_copy(o, i), reads=(in_,), writes=(out,))

    def memset(self, q, out, val):
        o = out.ap
        return self.op(q, lambda e: e.memset(o, val), writes=(out,))

    def reduce(self, q, out, in_, op, axis=AX.X):
        o, i = out.ap, in_.ap
        return self.op(q, lambda e: e.tensor_reduce(o, i, axis, op), reads=(in_,), writes=(out,))

    def recip(self, out, in_):
        o, i = out.ap, in_.ap
        return self.op("dve", lambda e: e.reciprocal(o, i), reads=(in_,), writes=(out,))
import ml_dtypes

D = 1024
NL = 2048
NCX = 256
NT = NL + NCX
EPS = 1e-6
TILES = [(0, 512), (512, 512), (1024, 512), (1536, 512), (2048, 256)]
NCH = 18


class Rot:
    def __init__(self, items):
        self.items = items
        self.i = 0

    def next(self):
        t = self.items[self.i % len(self.items)]
        self.i += 1
        return t


class MK:
    def __init__(self, depth=4, mixers=("even", "odd"), dbg=None, stop=None):
        self.stop = stop
        self.depth = depth
        self.mixers = mixers
        P = self.P = Prog()
        self.dbg = dbg
        self.declare_io()
        self.alloc()
        self.prologue()
        for l in range(depth):
            self.layer(l)
        self.epilogue()

    def declare_io(self):
        P = self.P
        I = "ExternalInput"
        d = {}
        d["x"] = P.dram("x", [NL, D], F32, I)
        d["ctx"] = P.dram("ctx", [NCX, D], F32, I)
        d["c"] = P.dram("c", [8, 128], F32, I)
        d["c_ctx"] = P.dram("c_ctx", [8, 128], F32, I)
        d["ada_w"] = P.dram("ada_w", [4, D, 6 * D], F32, I)
        d["ada_b"] = P.dram("ada_b", [4 * 48, 128], F32, I)
        d["norm_mix_w"] = P.dram("norm_mix_w", [32, 128], F32, I)
        d["norm_mlp_w"] = P.dram("norm_mlp_w", [32, 128], F32, I)
        d["mlp_w1"] = P.dram("mlp_w1", [4, D, 4 * D], F32, I)
        d["mlp_w2"] = P.dram("mlp_w2", [4, 4 * D, D], F32, I)
        d["ev_w_in"] = P.dram("ev_w_in", [2, D, 2096], F32, I)
        d["ev_q_norm_w"] = P.dram("ev_q_norm_w", [4, 128], F32, I)
        d["ev_w_uq"] = P.dram("ev_w_uq", [2, 256, 768], F32, I)
        d["ev_kv_norm_w"] = P.dram("ev_kv_norm_w", [4, 128], F32, I)
        d["ev_w_ukv"] = P.dram("ev_w_ukv", [2, 256, 1024], F32, I)
        d["ev_conv_w"] = P.dram("ev_conv_w", [80, 128], F32, I)
        d["ev_conv_b"] = P.dram("ev_conv_b", [16, 128], F32, I)
        d["ev_dt_bias"] = P.dram("ev_dt_bias", [2, 16], F32, I)
        d["ev_a_log"] = P.dram("ev_a_log", [2, 16], F32, I)
        d["ev_d_skip"] = P.dram("ev_d_skip", [2, 8], F32, I)
        d["ev_ssm_norm_w"] = P.dram("ev_ssm_norm_w", [2, 512], F32, I)
        d["ev_w_out"] = P.dram("ev_w_out", [2, D, D], F32, I)
        d["od_w_in"] = P.dram("od_w_in", [2, D, 3104], F32, I)
        d["od_conv_w"] = P.dram("od_conv_w", [80, 128], F32, I)
        d["od_conv_b"] = P.dram("od_conv_b", [16, 128], F32, I)
        d["od_i_bias"] = P.dram("od_i_bias", [2, 16], F32, I)
        d["od_f_bias"] = P.dram("od_f_bias", [2, 16], F32, I)
        d["od_head_norm_w"] = P.dram("od_head_norm_w", [2, D], F32, I)
        d["od_w_out"] = P.dram("od_w_out", [2, D, D], F32, I)
        d["final_norm_w"] = P.dram("final_norm_w", [1, D], F32, I)
        d["cst"] = P.dram("cst", [128, 5 * 128], F32, I)
        d["ropeC"] = P.dram("ropeC", [32, NT], F32, I)
        d["ropeS"] = P.dram("ropeS", [32, NT], F32, I)
        self.d = d
        self.out = P.dram("out", [NL, D], F32, "ExternalOutput")
        zs = P.nc.dram_tensor("zscr", [NT, 512], BF16, kind="Internal").ap()
        self.zscr = zs
        self.zch = [T(zs[i * 128:(i + 1) * 128, :], Buf("zch%d" % i)) for i in range(NCH)]

    def alloc(self):
        P = self.P
        self.XT = P.tile("XT", [128, 8, NT], F32)
        self.xt = [[self.XT.sub("xt%d_%d" % (c, i), (slice(None), c, slice(t0, t0 + w)))
                    for i, (t0, w) in enumerate(TILES)] for c in range(8)]
        self.PR = P.tile("PR", [128, 49152], BF16)
        self.pr_bufs = []
        self.pr_fence = []
        self.arena = [P.tile("wa%d" % i, [128, 4096], BF16) for i in range(3)]
        self.arot = Rot(self.arena)
        self.SCR = P.tile("SCR", [128, 4096], BF16)
        self.cst = P.tile("cst_sb", [128, 5 * 128], F32)
        self.identf = self.cst[:, 0:128]
        self.tri = [self.cst[:, 128:256], self.cst[:, 256:384]]
        self.neg = [self.cst[:, 384:512], self.cst[:, 512:640]]
        self.ident = P.tile("ident", [128, 128], BF16)
        self.ones_bf = P.tile("ones_bf", [128, 128], BF16)
        self.ones_f = P.tile("ones_f", [128, 128], F32)
        self.epsc = P.tile("epsc", [128, 1], F32)
        self.VEC = P.tile("VEC", [128, 512], F32)
        self.SV = P.tile("SV", [128, 8, 2], BF16)
        self.MOD = [P.tile("MOD%d" % l, [128, 48, 2], F32) for l in range(4)]
        self.AB = [P.tile("AB%d" % l, [128, 2, 8, 2], F32) for l in range(4)]
        self.stat = Rot([P.tile("stat%d" % i, [128, 8], F32) for i in range(4)])
        banks = [P.psum_tile("pb%d" % i, [128, 512], F32) for i in range(8)]
        self.banks = banks
        self.psA = Rot(banks[0:4])
        self.psB = Rot(banks[4:7])
        self.psC = Rot(banks[7:8])
        self.scr_bufs = []
        self.scr_fence = []

    def _carve(self, base, off, shape, dtype, name, bufs, fence):
        n = int(np.prod(shape[1:]))
        nb = n * (4 if dtype == F32 else 2)
        assert off % 4 == 0 and off + nb <= base.ap.shape[1] * 2, (name, off, nb)
        ap = base.ap[0:shape[0], off // 2: (off + nb) // 2]
        if dtype == F32:
            ap = ap.bitcast(F32)
        if len(shape) > 2:
            names = " ".join("a%d" % i for i in range(len(shape) - 1))
            kw = {"a%d" % i: shape[i + 1] for i in range(len(shape) - 2)}
            ap = ap.rearrange("p (%s) -> p %s" % (names, names), **kw)
        b = Buf(name)
        b.r = list(fence)
        bufs.append(b)
        return T(ap, b)

    def pr(self, off_kib, shape, dtype, name):
        return self._carve(self.PR, int(off_kib * 1024), shape, dtype, name, self.pr_bufs, self.pr_fence)

    def pbump(self, shape, dtype, name):
        n = int(np.prod(shape[1:])) * (4 if dtype == F32 else 2)
        off = (self.bump + 31) // 32 * 32
        assert off + n <= self.bump_end, (name, off, n, self.bump_end)
        self.bump = off + n
        return self._carve(self.PR, off, shape, dtype, name, self.pr_bufs, self.pr_fence)

    def scr(self, off_kib, shape, dtype, name):
        return self._carve(self.SCR, int(off_kib * 1024), shape, dtype, name, self.scr_bufs, self.scr_fence)

    def fence(self):
        for bufs, attr in ((self.pr_bufs, "pr_fence"), (self.scr_bufs, "scr_fence")):
            toks = {}
            for b in bufs:
                ws = b.w if isinstance(b.w, list) else ([b.w] if b.w else [])
                for tok in ws + b.r:
                    if toks.get(tok[0], 0) < tok[1]:
                        toks[tok[0]] = tok[1]
            for k, c in getattr(self, attr):
                if toks.get(k, 0) < c:
                    toks[k] = c
            setattr(self, attr, list(toks.items()))
            for b in bufs:
                if len(b.r) > 8:
                    m = {}
                    for k, c in b.r:
                        if m.get(k, 0) < c:
                            m[k] = c
                    b.r = list(m.items())

    def subs(self, t, name):
        out = []
        for c in range(t.ap.shape[1]):
            row = []
            for i, (t0, w) in enumerate(TILES):
                b = Buf("%s%d_%d" % (name, c, i))
                b.r = list(t.buf.r)
                self.pr_bufs.append(b)
                row.append(T(t.ap[:, c, t0:t0 + w], b))
            out.append(row)
        return out

    def norm_scratch(self):
        self.sq = Rot([self.scr(i, [128, 512], BF16, "sq%d" % i) for i in range(2)])
        self.rs = Rot([self.scr(2, [128, 512], F32, "rs0")])
        self.tmpf = Rot([self.scr(4 + 2 * i, [128, 512], F32, "tmpf%d" % i) for i in range(2)])

    def wslot(self):
        return self.arot.next()

    def load_w(self, dram_ap, rows, cols, slot=None):
        P = self.P
        k = rows // 128
        s = slot if slot is not None else self.wslot()
        v = s[:, 0:k * cols].re("p (k n) -> p k n", k=k)
        P.dma("pool", v, dram_ap.rearrange("(k p) n -> p k n", p=128))
        return v

    def prologue(self):
        P = self.P
        d = self.d
        P.dma("sp", self.cst, d["cst"])
        P.copy("dve", self.ident, self.identf)
        P.memset("dve", self.ones_bf, 1.0)
        P.memset("dve", self.ones_f, 1.0)
        P.memset("dve", self.epsc, EPS)
        rows = [("ada_b", 192), ("norm_mix_w", 32), ("norm_mlp_w", 32), ("final_norm_w_rows", 8), ("c", 8),
                ("c_ctx", 8), ("ev_conv_w", 80), ("ev_conv_b", 16), ("od_conv_w", 80), ("od_conv_b", 16),
                ("ev_q_norm_w", 4), ("ev_kv_norm_w", 4)]
        self.voff = {}
        off = 0
        for n, k in rows:
            self.voff[n] = off
            off += k
        assert off <= 512
        stg = [self.pr(16 + 0.5 * i, [128, 128], F32, "vstg%d" % i) for i in range(4)]
        for s in stg:
            P.memset("dve", s, 0.0)
        for n, k in rows:
            src = d["final_norm_w"].rearrange("o (r p) -> (o r) p", p=128) if n == "final_norm_w_rows" else d[n]
            o = self.voff[n]
            done = 0
            while done < k:
                ti, r0 = divmod(o + done, 128)
                m = min(k - done, 128 - r0)
                P.dma("sp", stg[ti][r0:r0 + m, :], src[done:done + m, :])
                done += m
        for i in range(4):
            pt = self.psC.next()
            P.transpose(pt[:, 0:128], stg[i], self.identf)
            P.copy("dve", self.VEC[:, i * 128:(i + 1) * 128], pt[:, 0:128])
        oc, occ = self.voff["c"], self.voff["c_ctx"]
        P.act(self.SV[:, :, 0], self.VEC[:, oc:oc + 8], AF.Silu)
        P.act(self.SV[:, :, 1], self.VEC[:, occ:occ + 8], AF.Silu)
        xs = Rot([self.pr(4 * i, [128, D], F32, "xstg%d" % i) for i in range(4)])
        for ti, (t0, w) in enumerate(TILES):
            subs = []
            for j in range(w // 128):
                s = xs.next()
                tok = t0 + j * 128
                src = d["x"][tok:tok + 128, :] if tok < NL else d["ctx"][tok - NL:tok - NL + 128, :]
                P.dma("sp", s, src)
                subs.append(s)
            for c in range(8):
                pt = self.psA.next()
                for j, s in enumerate(subs):
                    P.transpose(pt[:, j * 128:(j + 1) * 128], s[:, c * 128:(c + 1) * 128], self.identf)
                eng = "act" if c % 2 else "dve"
                P.copy(eng, self.xt[c][ti], pt[:, 0:w])
        for q in range(12):
            self.ada_piece(0, q)
        self.ada_finish(0)

    def ada_load(self, l, q, slot=None):
        return self.load_w(self.d["ada_w"][l][:, q * 512:(q + 1) * 512], 1024, 512, slot=slot)

    def ada_mm(self, l, q, wv):
        P = self.P
        pt = self.psA.next()
        av = pt[:, 0:8].re("p (j v) -> p j v", v=2)
        for i in range(4):
            for k in range(8):
                P.mm(av[:, i, :], wv[:, k, i * 128:(i + 1) * 128], self.SV[:, k, :], start=(k == 0), stop=(k == 7))
        P.copy("act", self.MOD[l][:, q * 4:(q + 1) * 4, :], av)

    def ada_piece(self, l, q):
        self.ada_mm(l, q, self.ada_load(l, q))

    def ada_finish(self, l):
        P = self.P
        ob = self.voff["ada_b"] + l * 48
        P.tt("dve", self.MOD[l], self.MOD[l], self.VEC[:, ob:ob + 48].re("p (j o) -> p j o", o=1).bc([128, 48, 2]), ALU.add)
        for which, (scj, nwn) in enumerate(((1, "norm_mix_w"), (4, "norm_mlp_w"))):
            on = self.voff[nwn] + l * 8
            P.stt("dve", self.AB[l][:, which], self.MOD[l][:, scj * 8:(scj + 1) * 8, :], 1.0,
                  self.VEC[:, on:on + 8].re("p (j o) -> p j o", o=1).bc([128, 8, 2]), ALU.add, ALU.mult)

    class AdaStream:
        def __init__(self, mk, l):
            self.mk, self.l = mk, l
            self.q_loaded, self.q_done, self.wv = 0, 0, None
            self.slot = None

        def step(self, load=True):
            mk, l = self.mk, self.l
            if l is None:
                return
            if self.wv is not None:
                mk.ada_mm(l, self.q_done, self.wv)
                self.q_done += 1
                self.wv = None
            if load and self.q_loaded < 12:
                self.wv = mk.ada_load(l, self.q_loaded, slot=self.slot)
                self.q_loaded += 1

        def finish(self):
            if self.l is None:
                return
            while self.q_done < 12:
                self.step()
            self.mk.ada_finish(self.l)

    def norm_phase(self, l, which, tiles=None):
        P = self.P
        self.norm_scratch()
        self.HT = self.pr(60, [128, 8, NT], BF16, "HT")
        self.ht = self.subs(self.HT, "ht")
        shj = 0 if which == 0 else 3
        for ti, (t0, w) in enumerate(TILES):
            if tiles is not None and ti not in tiles:
                continue
            v = 1 if ti == 4 else 0
            ssp = self.psC.next()
            for c in range(8):
                sq = self.sq.next()
                if c % 4 == 3:
                    P.act(sq[:, :w], self.xt[c][ti], AF.Square)
                else:
                    P.tt("pool", sq[:, :w], self.xt[c][ti], self.xt[c][ti], ALU.mult)
                P.mm(ssp[:, :w], self.ones_bf, sq[:, :w], start=(c == 0), stop=(c == 7))
            rs = self.rs.next()
            P.act(rs[:, :w], ssp[:, :w], AF.Ln, bias=self.epsc, scale=1.0 / D)
            P.act(rs[:, :w], rs[:, :w], AF.Exp, scale=-0.5)
            for c in range(8):
                tmp = self.tmpf.next()
                P.tt("dve", tmp[:, :w], self.xt[c][ti], rs[:, :w], ALU.mult)
                P.act(self.ht[c][ti], tmp[:, :w], AF.Identity,
                      bias=self.MOD[l][:, shj * 8 + c, v:v + 1], scale=self.AB[l][:, which, c, v:v + 1])

    def mlp(self, l):
        P = self.P
        d = self.d
        last = (l == self.depth - 1)
        tiles = [i for i in range(5) if not (last and i == 4)]
        self.fence()
        self.norm_phase(l, 1, tiles)
        self.fence()
        self.relu = Rot([self.scr(i, [128, 512], BF16, "relu%d" % i) for i in range(3)])
        hid = [self.subs(self.pr(18 * b, [128, 4, NT], BF16, "hid%d" % b), "hid%d_" % b) for b in range(2)]
        nxt = None
        for g in range(8):
            w1 = self.load_w(d["mlp_w1"][l][:, g * 512:(g + 1) * 512], 1024, 512)
            w2 = self.load_w(d["mlp_w2"][l][g * 512:(g + 1) * 512, :], 512, 1024)
            hb = hid[g % 2]
            for ti in tiles:
                t0, w = TILES[ti]
                for hc in range(4):
                    pt = self.psA.next()
                    for k in range(8):
                        P.mm(pt[:, :w], w1[:, k, hc * 128:(hc + 1) * 128], self.ht[k][ti],
                             start=(k == 0), stop=(k == 7))
                    r = self.relu.next()
                    P.act(r[:, :w], pt[:, :w], AF.Relu)
                    P.tt("pool", hb[hc][ti], r[:, :w], r[:, :w], ALU.mult)
            if nxt is not None and g < 6:
                self.ada_piece(nxt, 2 * g)
            for ti in tiles:
                t0, w = TILES[ti]
                v = 1 if ti == 4 else 0
                for dc in range(8):
                    po = self.psB.next()
                    for hc in range(4):
                        P.mm(po[:, :w], w2[:, hc, dc * 128:(dc + 1) * 128], hb[hc][ti],
                             start=(hc == 0), stop=(hc == 3))
                    P.stt("dve", self.xt[dc][ti], po[:, :w], self.MOD[l][:, 40 + dc, v:v + 1], self.xt[dc][ti],
                          ALU.mult, ALU.add)
            if nxt is not None and g < 6:
                self.ada_piece(nxt, 2 * g + 1)
        if nxt is not None:
            self.ada_finish(nxt)

    def layer(self, l):
        if getattr(self, "halt", False):
            return
        self.fence()
        self.ada = MK.AdaStream(self, l + 1 if l + 1 < self.depth else None)
        if l % 2 == 0 and "even" in self.mixers:
            self.even_mixer(l)
        if l % 2 == 1 and "odd" in self.mixers:
            self.odd_mixer(l)
        if getattr(self, "halt", False):
            return
        self.ada.finish()
        self.mlp(l)

    def even_mixer(self, l):
        P = self.P
        d = self.d
        e = l // 2
        last = (l == self.depth - 1)
        tiles = [i for i in range(5) if not (last and i == 4)]
        do_ssd = "nossd" not in self.mixers
        do_attn = "noattn" not in self.mixers
        self.norm_phase(l, 0)
        self.fence()
        htall = self.join([t for row in self.ht for t in row], self.HT.ap, "htall")
        Win = d["ev_w_in"][e]
        XSB = self.pr(0, [128, 4, NT], BF16, "XSB")
        BCB = self.pr(18, [128, 4, NT], BF16, "BCB")
        CQT = self.pr(36, [128, 2, NT], BF16, "CQT")
        CKVT = self.pr(45, [128, 2, NT], BF16, "CKVT")
        KRAB = self.pr(54, [128, NT], BF16, "KRAB")
        self.bump, self.bump_end = int(58.5 * 1024), 60 * 1024
        DT = self.pbump([128, 18, 16], F32, "DT")
        A_b = self.pbump([128, 16], F32, "A_b")
        DTB = self.pbump([128, 16], F32, "DTB")
        DSK = self.pbump([128, 8], F32, "DSK")
        P.dma("sp", A_b, d["ev_a_log"][e:e + 1, :].to_broadcast([128, 16]))
        P.dma("sp", DTB, d["ev_dt_bias"][e:e + 1, :].to_broadcast([128, 16]))
        P.dma("sp", DSK, d["ev_d_skip"][e:e + 1, :].to_broadcast([128, 8]))
        P.act(A_b, A_b, AF.Exp)
        P.ts("dve", A_b, A_b, -1.0, None, ALU.mult)
        stg_rot = Rot([self.scr(1.03125 * i, [128, 516], BF16, "stg%d" % i) for i in range(3)])
        dg_rot = Rot([self.scr(3.125, [128, 5, 128], BF16, "dg0")])
        zs_rot = Rot([self.scr(4.5 + i, [128, 512], BF16, "zs%d" % i) for i in range(2)])
        cw = self.voff["ev_conv_w"] + e * 40
        cb = self.voff["ev_conv_b"] + e * 8
        wz = self.load_w(Win[:, 544:1056], 1024, 512)
        slot_dk = self.wslot()
        wdk = slot_dk[:, 0:8 * 112].re("p (k n) -> p k n", k=8)
        P.memset("pool", wdk[:, :, 48:80], 0.0)
        P.dma("pool", wdk[:, :, 0:16], Win[:, 2080:2096].rearrange("(k p) n -> p k n", p=128))
        P.dma("pool", wdk[:, :, 80:112], Win[:, 512:544].rearrange("(k p) n -> p k n", p=128))
        for b_ in range(2):
            for hf in range(2):
                so = 512 + 16 * b_ + 8 * (1 - hf)
                do = 16 + 16 * b_ + 8 * hf
                P.dma("pool", wdk[:, :, do:do + 8], Win[:, so:so + 8].rearrange("(k p) n -> p k n", p=128))
        for i in range(NCH):
            cs = slice(i * 128, (i + 1) * 128)
            pz = self.psA.next()
            for k in range(8):
                P.mm(pz, htall[:, k, cs], wz[:, k, :], start=(k == 0), stop=(k == 7))
            zs = zs_rot.next()
            P.act(zs, pz, AF.Silu)
            P.dma("sp", self.zch[i], zs, owner=zs)
            pd = self.psA.next()
            for k in range(8):
                P.mm(pd[:, 0:16], htall[:, k, cs], wdk[:, k, 0:16], start=(k == 0), stop=(k == 7))
            P.tt("dve", DT[:, i, :], pd[:, 0:16], DTB, ALU.add)
        P.act(DT, DT, AF.Exp)
        P.act(DT, DT, AF.Ln, bias=1.0)
        for ti, (t0, w) in enumerate(TILES):
            ts_ = slice(t0, t0 + w)
            pk = self.psA.next()
            for k in range(8):
                P.mm(pk[0:96, :w], wdk[:, k, 16:112], htall[:, k, ts_], start=(k == 0), stop=(k == 7))
            P.copy("act", KRAB[0:96, ts_], pk[0:96, :w])
        for half in range(2):
            wx = self.load_w(Win[:, 1056 + 512 * half:1056 + 512 * half + 512], 1024, 512)
            for cc in range(4):
                c = half * 4 + cc
                dst = XSB[:, c, :] if c < 4 else BCB[:, c - 4, :]
                self.conv_chunk(wx, cc * 128, cw + c, cb + c, dst, htall, stg_rot, dg_rot)
        self.fence()
        wl = self.load_w(Win[:, 0:512], 1024, 512)
        rawl = [self.scr(2 * i, [128, 512], F32, "rawl%d" % i) for i in range(2)]
        sqs = [self.scr(4 + i, [128, 512], BF16, "sql%d" % i) for i in range(2)]
        rsl = self.scr(6, [128, 512], F32, "rsl")
        for ti, (t0, w) in enumerate(TILES):
            ts_ = slice(t0, t0 + w)
            for lat in range(2):
                dstT = CQT if lat == 0 else CKVT
                nwo = self.voff["ev_q_norm_w" if lat == 0 else "ev_kv_norm_w"] + e * 2
                ssp = self.psC.next()
                for c in range(2):
                    pl = self.psA.next()
                    for k in range(8):
                        P.mm(pl[:, :w], wl[:, k, (lat * 2 + c) * 128:(lat * 2 + c + 1) * 128], htall[:, k, ts_],
                             start=(k == 0), stop=(k == 7))
                    P.copy("act", rawl[c][:, :w], pl[:, :w])
                    P.act(sqs[c][:, :w], rawl[c][:, :w], AF.Square)
                    P.mm(ssp[:, :w], self.ones_bf, sqs[c][:, :w], start=(c == 0), stop=(c == 1))
                P.act(rsl[:, :w], ssp[:, :w], AF.Ln, bias=self.epsc, scale=1.0 / 256)
                P.act(rsl[:, :w], rsl[:, :w], AF.Exp, scale=-0.5)
                for c in range(2):
                    P.stt("dve", dstT[:, c, ts_], rawl[c][:, :w], self.VEC[:, nwo + c:nwo + c + 1], rsl[:, :w],
                          ALU.mult, ALU.mult)
        self.fence()
        if do_ssd:
            self.ssd_scan(l, e, XSB, BCB, DT, A_b, DSK, last)
            wo = self.load_w(d["ev_w_out"][e][512:1024, :], 512, 1024)
            self.out_proj(l, wo, 4, [XSB[:, c, :] for c in range(4)], tiles)
        self.fence()
        if do_attn:
            self.attention(l, e, CQT, CKVT, KRAB, last, tiles)

    def ssd_scan(self, l, e, XSB, BCB, DT, A_b, DSK, last):
        P = self.P
        d = self.d
        self.bump, self.bump_end = 60 * 1024, 96 * 1024
        SNAP = self.pbump([128, 18, 512], BF16, "SSNAP")
        RA = self.pbump([128, 8, 128], F32, "RA")
        WTt = self.pbump([128, 8, 128], BF16, "WTt")
        CBM = [self.pbump([128, 2, 128], BF16, "CBM%d" % i) for i in range(2)]
        XS_TM = self.pbump([128, 8, 64], BF16, "XS_TM")
        BM_TM = self.pbump([128, 2, 128], BF16, "BM_TM")
        XP = [self.pbump([128, 8, 64], BF16, "XP%d" % i) for i in range(2)]
        H = self.pbump([128, 8, 64], F32, "Hst")
        HBF = self.pbump([128, 512], BF16, "HBF")
        SNW = self.pbump([128, 512], F32, "SNW")
        GNb = self.pbump([128, 512], BF16, "GNb")
        TMP = self.scr(0, [128, 512], F32, "ssd_tmp")
        ACCs = [self.scr(2 + 2 * i, [128, 512], F32, "ssd_acc%d" % i) for i in range(2)]
        zin = Rot([self.scr(6 + i, [128, 512], BF16, "zin%d" % i) for i in range(2)])
        junk = GNb[:, 0:256]
        P.dma("sp", SNW, d["ev_ssm_norm_w"][e:e + 1, :].to_broadcast([128, 512]))

        sl = self.wslot()
        def slv(k):
            return sl.v(sl.ap[:, k * 576:(k + 1) * 576].bitcast(F32).rearrange("p (d c h) -> p d c h", d=2, c=18))
        DTA, NACa, EACa, TOEa, CDa, DTOE = slv(0), slv(1), slv(2), slv(3), slv(4), slv(5)
        for dd in range(2):
            P.tt("dve", DTA[:, dd], DT[:, :, dd * 8:dd * 8 + 8],
                 A_b[:, dd * 8:dd * 8 + 8].re("p (o h) -> p o h", o=1).bc([128, 18, 8]), ALU.mult)
        pa = self.psA.next()
        pb = self.psA.next()
        for dd in range(2):
            P.mm(pa[:, dd * 144:(dd + 1) * 144], self.tri[dd], DTA[:, dd].re("p c h -> p (c h)"))
        P.mm(pb[:, 0:288], self.ones_f, DTA.re("p d c h -> p (d c h)"))
        pav = pa[:, 0:288].re("p (d c h) -> p d c h", d=2, c=18)
        pbv = pb[:, 0:288].re("p (d c h) -> p d c h", d=2, c=18)
        P.act(NACa, pav, AF.Copy, scale=-1.0)
        P.act(EACa, pav, AF.Exp)
        P.act(CDa, pbv, AF.Exp)
        P.tt("dve", TOEa, pbv, NACa, ALU.add)
        P.act(TOEa, TOEa, AF.Exp)
        for dd in range(2):
            P.tt("dve", DTOE[:, dd], TOEa[:, dd], DT[:, :, dd * 8:dd * 8 + 8], ALU.mult)

        sl2 = self.wslot()
        RA2 = sl2.v(sl2.ap[:, 0:2048].bitcast(F32).rearrange("p (h t) -> p h t", h=8))
        WTt2 = sl2.v(sl2.ap[:, 2048:3072].rearrange("p (h t) -> p h t", h=8))
        RAs, WTts = [RA, RA2], [WTt, WTt2]

        def prep(dd, i, need_decay):
            cs = slice(i * 128, (i + 1) * 128)
            dt = DT[:, i, dd * 8:dd * 8 + 8]
            dtA, NAC, EAC, TOE, CD = (DTA[:, dd, i, :], NACa[:, dd, i, :], EACa[:, dd, i, :], TOEa[:, dd, i, :], CDa[:, dd, i, :])
            r = dict(dt=dt, dtA=dtA, NAC=NAC, EAC=EAC, TOE=TOE, CD=CD, cs=cs, DTOE=DTOE[:, dd, i, :])
            return r

        def decay_a(dd, pr_):
            RAd = RAs[dd]
            P.tt("pool", RAd, self.tri[dd].re("p (o t) -> p o t", o=1).bc([128, 8, 128]),
                 pr_["dtA"].re("p (h o) -> p h o", o=1).bc([128, 8, 128]), ALU.mult)
            pA = [self.psA.next(), self.psA.next()]
            for hf in range(2):
                P.mm(pA[hf], self.ones_f, RAd[:, 4 * hf:4 * hf + 4, :].re("p h t -> p (h t)"))
            return pA

        def decay_b(dd, pr_, pA):
            RAd = RAs[dd]
            for hf in range(2):
                P.tt("dve", RAd[:, 4 * hf:4 * hf + 4, :], pA[hf].re("p (h t) -> p h t", h=4),
                     pr_["NAC"][:, 4 * hf:4 * hf + 4].re("p (h o) -> p h o", o=1).bc([128, 4, 128]), ALU.add)
            P.act(RAd, RAd, AF.Relu, scale=-1.0)
            P.act(RAd, RAd, AF.Exp, scale=-1.0)

        def transposes(i, need_x=True):
            cs = slice(i * 128, (i + 1) * 128)
            pt = self.psA.next()
            ptb = pt.v(pt.ap.bitcast(BF16))
            for c in range(4):
                P.transpose(ptb[:, c * 128:(c + 1) * 128], XSB[:, c, cs], self.ident)
            P.copy("act", XS_TM.re("p h v -> p (h v)"), ptb[:, 0:512])
            pt2 = self.psA.next()
            ptb2 = pt2.v(pt2.ap.bitcast(BF16))
            for g in range(2):
                P.transpose(ptb2[:, g * 128:(g + 1) * 128], BCB[:, g, cs], self.ident)
            P.copy("act", BM_TM.re("p g n -> p (g n)"), ptb2[:, 0:256])

        def state_update(pr_, xp, XPP):
            P.tt("pool", XPP, XS_TM, pr_["DTOE"].re("p (h o) -> p h o", o=1).bc([128, 8, 64]), ALU.mult)
            ph = self.psB.next()
            for g in range(2):
                P.mm(ph[:, g * 256:(g + 1) * 256], BM_TM[:, g, :], XPP[:, 4 * g:4 * g + 4, :].re("p h v -> p (h v)"))
            P.tt("dve", H, H, pr_["CD"].re("p (h o) -> p h o", o=1).bc([128, 8, 64]), ALU.mult)
            P.tt("dve", H.re("p h v -> p (h v)"), H.re("p h v -> p (h v)"), ph, ALU.add)

        P.memset("dve", H, 0.0)
        orderA = [16, 17] + list(range(16))
        for n_, i in enumerate(orderA):
            P.copy("dve", SNAP[:, i, :], H.re("p h v -> p (h v)"))
            if n_ == len(orderA) - 1:
                break
            pr_ = prep(0, i, False)
            transposes(i)
            state_update(pr_, None, XP[1])
        P.memset("dve", H, 0.0)
        orderB = [17, 16] + list(range(15, -1, -1))

        def head(n_, i, need_out):
            cs = slice(i * 128, (i + 1) * 128)
            ACC = ACCs[n_ % 2]
            transposes(i)
            zt = None
            if need_out:
                zt = zin.next()
                P.dma("sp", zt, self.zch[i], owner=zt)
                P.copy("dve", HBF, H.re("p h v -> p (h v)"))
                pcb = self.psA.next()
                for g in range(2):
                    P.mm(pcb[:, g * 128:(g + 1) * 128], BCB[:, g, cs], BCB[:, 2 + g, cs])
                pcv = pcb[:, 0:256].re("p (g t) -> p g t", g=2)
                for dd in range(2):
                    P.tt("dve", CBM[dd], pcv, self.tri[dd].re("p (o t) -> p o t", o=1).bc([128, 2, 128]), ALU.mult)
                P.tt("pool", ACC.re("p (h v) -> p h v", h=8), XS_TM, DSK.re("p (h o) -> p h o", o=1).bc([128, 8, 64]), ALU.mult)
            prs = [prep(dd, i, need_out) for dd in range(2)]
            for dd in range(2):
                P.tt("pool", XP[dd], XS_TM, prs[dd]["dt"].re("p (h o) -> p h o", o=1).bc([128, 8, 64]), ALU.mult)
            if need_out:
                pAs = [decay_a(dd, prs[dd]) for dd in range(2)]
                for dd in range(2):
                    decay_b(dd, prs[dd], pAs[dd])
                pys = []
                for dd in range(2):
                    for g in range(2):
                        P.tt("dve", WTts[dd][:, 4 * g:4 * g + 4, :], RAs[dd][:, 4 * g:4 * g + 4, :],
                             CBM[dd][:, g:g + 1, :].bc([128, 4, 128]), ALU.mult)
                    py = self.psB.next()
                    for h in range(8):
                        P.mm(py[:, h * 64:(h + 1) * 64], WTts[dd][:, h, :], XP[dd][:, h, :])
                    pys.append(py)
                for dd in range(2):
                    pyi = self.psB.next() if dd == 0 else self.psC.next()
                    hsrc = SNAP[:, i, :] if dd == 0 else HBF
                    for g in range(2):
                        P.mm(pyi[:, g * 256:(g + 1) * 256], BCB[:, 2 + g, cs], hsrc[:, g * 256:(g + 1) * 256])
                    P.tt("dve", TMP.re("p (h v) -> p h v", h=8), pyi.re("p (h v) -> p h v", h=8),
                         prs[dd]["EAC"].re("p (h o) -> p h o", o=1).bc([128, 8, 64]), ALU.mult)
                    P.tt("dve", TMP, TMP, pys[dd], ALU.add)
                    P.tt("pool", ACC, ACC, TMP, ALU.add)
            if n_ < len(orderB) - 1:
                state_update(prs[1], XP[1], XP[0])
            if not need_out:
                return None

            def tail():
                Rr = self.stat.next()
                P.tt("pool", ACC, ACC, zt, ALU.mult)
                for g in range(2):
                    P.act(junk, ACC[:, g * 256:(g + 1) * 256], AF.Square, accum=Rr[:, g:g + 1])
                P.act(Rr[:, 2:4], Rr[:, 0:2], AF.Ln, bias=self.epsc, scale=1.0 / 256)
                P.act(Rr[:, 2:4], Rr[:, 2:4], AF.Exp, scale=-0.5)
                for g in range(2):
                    gs = slice(g * 256, (g + 1) * 256)
                    P.stt("dve", GNb[:, gs], ACC[:, gs], Rr[:, 2 + g:3 + g], SNW[:, gs], ALU.mult, ALU.mult)
                pg = self.psA.next()
                pgb = pg.v(pg.ap.bitcast(BF16))
                for c in range(4):
                    P.transpose(pgb[:, c * 128:(c + 1) * 128], GNb[:, c * 128:(c + 1) * 128], self.ident)
                P.copy("act", XSB[:, :, cs], pgb[:, 0:512].re("p (c t) -> p c t", c=4))
            return tail

        pending = None
        for n_, i in enumerate(orderB):
            need_out = not (last and i >= 16)
            t_ = head(n_, i, need_out)
            if pending is not None:
                pending()
            pending = t_
        if pending is not None:
            pending()

    def attention(self, l, e, CQT, CKVT, KRAB, last, tiles):
        P = self.P
        d = self.d
        SCALE = 96.0 ** -0.5
        CT = self.pr(0, [128, NT], F32, "ropeCT")
        ST = self.pr(9, [128, NT], F32, "ropeST")
        GTA = self.pr(18, [128, 4, NT], BF16, "GTA")
        self.bump, self.bump_end = 60 * 1024, 96 * 1024
        QTs = [self.pbump([128, NT], BF16, "QT%d" % i) for i in range(2)]
        KTs = [self.pbump([128, NT], BF16, "KT%d" % i) for i in range(2)]
        VA = [self.pbump([128, 18, 128], BF16, "VA%d" % i) for i in range(2)]
        pts = Rot([self.pbump([128, 512], BF16, "PT%d" % i) for i in range(5)])
        T1 = self.scr(0, [128, 512], F32, "aT1")
        T2 = self.scr(2, [128, 512], F32, "aT2")
        RD = self.scr(4, [128, 512], F32, "aRD")
        T2s = self.scr(6, [128, 512], F32, "aT2s")
        P.memset("dve", CT[0:64, :], 1.0)
        P.dma("sp", CT[64:96, :], d["ropeC"])
        P.dma("sp", ST[64:96, :], d["ropeS"])
        P.dma("sp", ST[0:32, :], d["ropeS"])
        P.memset("dve", ST[32:64, :], 0.0)
        P.memset("pool", VA[0][:, :, 64:128], 1.0)
        P.memset("pool", VA[1][:, :, 0:64], 1.0)
        s1 = self.wslot()
        WUQ = s1[:, 0:1536].re("p (k n) -> p k n", k=2)
        WUQP = s1[:, 1536:3072].re("p (k n) -> p k n", k=2)
        P.memset("pool", WUQP, 0.0)
        P.dma("pool", WUQ, d["ev_w_uq"][e].rearrange("(k p) n -> p k n", p=128))
        src5 = d["ev_w_uq"][e].rearrange("(k p) (h f) -> p k h f", p=128, f=96)
        dst5 = WUQP.re("p k (h f) -> p k h f", f=96)
        for k in range(2):
            for b_ in range(2):
                for hf in range(2):
                    so = 64 + 16 * b_ + 8 * (1 - hf)
                    do = 64 + 16 * b_ + 8 * hf
                    P.dma("pool", dst5[:, k, :, do:do + 8], src5[:, k, :, so:so + 8])
        WUKV = self.load_w(d["ev_w_ukv"][e], 256, 1024)
        for ti, (t0, w) in enumerate(TILES):
            ts_ = slice(t0, t0 + w)
            P.tt("dve", T2[0:32, :w], KRAB[0:32, ts_], ST[0:32, ts_], ALU.mult)
            P.copy("act", T2s[64:96, :w], T2[0:32, :w])
            P.tt("dve", T1[64:96, :w], KRAB[64:96, ts_], CT[64:96, ts_], ALU.mult)
            P.tt("pool", KTs[0][64:96, ts_], T1[64:96, :w], T2s[64:96, :w], ALU.add)
            P.tt("pool", KTs[1][64:96, ts_], T1[64:96, :w], T2s[64:96, :w], ALU.add)

        def prep(h):
            par = h % 2
            QT, KT = QTs[par], KTs[par]
            for ti, (t0, w) in enumerate(TILES):
                ts_ = slice(t0, t0 + w)
                pa_ = self.psA.next()
                pb_ = self.psA.next()
                for k in range(2):
                    P.mm(pa_[0:96, :w], WUQ[:, k, h * 96:(h + 1) * 96], CQT[:, k, ts_], start=(k == 0), stop=(k == 1))
                for k in range(2):
                    P.mm(pb_[0:96, :w], WUQP[:, k, h * 96:(h + 1) * 96], CQT[:, k, ts_], start=(k == 0), stop=(k == 1))
                P.tt("dve", T1[0:96, :w], pa_[0:96, :w], CT[0:96, ts_], ALU.mult)
                P.tt("dve", T2[0:96, :w], pb_[0:96, :w], ST[0:96, ts_], ALU.mult)
                P.tt("pool", QT[0:96, ts_], T1[0:96, :w], T2[0:96, :w], ALU.add)
                pk = self.psA.next()
                for k in range(2):
                    P.mm(pk[0:64, :w], WUKV[:, k, h * 128:h * 128 + 64], CKVT[:, k, ts_], start=(k == 0), stop=(k == 1))
                P.copy("dve", KT[0:64, ts_], pk[0:64, :w])
            va = VA[par]
            vo = 64 * par
            for i0 in range(0, NCH, 8):
                n = min(8, NCH - i0)
                pv = self.psA.next()
                for jj in range(n):
                    i = i0 + jj
                    for k in range(2):
                        P.mm(pv[:, jj * 64:(jj + 1) * 64], CKVT[:, k, i * 128:(i + 1) * 128],
                             WUKV[:, k, h * 128 + 64:h * 128 + 128], start=(k == 0), stop=(k == 1))
                P.copy("dve", va[:, i0:i0 + n, vo:vo + 64], pv[:, 0:n * 64].re("p (c v) -> p c v", c=n))

        def attend(h):
            par = h % 2
            QT, KT, va = QTs[par], KTs[par], VA[par]
            orow = slice(64 * par, 64 * par + 64)
            drow = slice(64 * (1 - par), 64 * (1 - par) + 64)
            for ti in tiles:
                t0, w = TILES[ti]
                ts_ = slice(t0, t0 + w)
                chunks = list(range(NCH)) if ti < 4 else [16, 17]
                po = self.psB.next()
                pend = []

                def do_pv(idx, i, ptile):
                    P.mm(po[:, :w], va[:, i, :], ptile[:, :w], start=(idx == 0), stop=(idx == len(chunks) - 1))

                for idx, i in enumerate(chunks):
                    ps_ = self.psA.next()
                    P.mm(ps_[:, :w], KT[0:96, i * 128:(i + 1) * 128], QT[0:96, ts_])
                    ptile = pts.next()
                    P.act(ptile[:, :w], ps_[:, :w], AF.Exp, scale=SCALE)
                    pend.append((idx, i, ptile))
                    if len(pend) > 2:
                        do_pv(*pend.pop(0))
                while pend:
                    do_pv(*pend.pop(0))
                P.recip(RD[orow, :w], po[drow, :w])
                P.tt("dve", GTA[orow, h // 2, ts_], po[orow, :w], RD[orow, :w], ALU.mult)

        self.ada.slot = self.wslot()
        prep(0)
        for h in range(8):
            if h + 1 < 8:
                prep(h + 1)
            self.ada.step()
            attend(h)
            if h < 4:
                self.ada.step()
        if self.ada.wv is not None:
            self.ada.step(load=False)
        self.ada.slot = None
        if self.stop == "attn_end":
            self.halt = True
            return
        wo = self.load_w(d["ev_w_out"][e][0:512, :], 512, 1024)
        self.out_proj(l, wo, 4, [GTA[:, c, :] for c in range(4)], tiles)

    def join(self, tlist, ap, name):
        b = Buf(name)
        ws = []
        for t in tlist:
            w = t.buf.w
            if w is None:
                continue
            ws.extend(w if isinstance(w, list) else [w])
        b.w = ws
        self.pr_bufs.append(b)
        return T(ap, b)

    def bcast_load(self, dst, dram_row_ap, n):
        self.P.dma("sp", dst, dram_row_ap.to_broadcast([128, n]))

    CONV_TILES = [(0, 508), (508, 1016), (1016, 1524), (1524, 2032), (2032, 2048), (2048, 2304)]

    def conv_chunk(self, wv, col0, vec_w_off, vec_b_off, dst, htall, stg_rot, dg_rot):
        P = self.P
        dg = dg_rot.next()
        for j in range(5):
            P.ts("pool", dg[:, j, :], self.identf, self.VEC[:, vec_w_off + 8 * j:vec_w_off + 8 * j + 1], None, ALU.mult)
        for (a, b) in self.CONV_TILES:
            s0, s1 = (0, NL) if a < NL else (NL, NT)
            ia, ib = max(a - 2, s0), min(b + 2, s1)
            w = b - a
            win = ib - ia
            j0 = ia - (a - 2)
            pt = self.psA.next()
            for k in range(8):
                P.mm(pt[:, 0:win], wv[:, k, col0:col0 + 128], htall[:, k, ia:ib], start=(k == 0), stop=(k == 7))
            stg = stg_rot.next()
            if j0 > 0:
                P.memset("pool", stg[:, 0:j0], 0.0)
            if j0 + win < w + 4:
                P.memset("pool", stg[:, j0 + win:w + 4], 0.0)
            P.copy("act", stg[:, j0:j0 + win], pt[:, 0:win])
            pc = self.psB.next()
            for j in range(5):
                P.mm(pc[:, 0:w], dg[:, j, :], stg[:, j:j + w], start=(j == 0), stop=(j == 4))
            P.act(dst[:, a:b], pc[:, 0:w], AF.Silu, bias=self.VEC[:, vec_b_off:vec_b_off + 1])

    def out_proj(self, l, wv, nk, gts, tiles):
        P = self.P
        for ti in tiles:
            t0, w = TILES[ti]
            v = 1 if ti == 4 else 0
            for dc in range(8):
                po = self.psB.next()
                for k in range(nk):
                    P.mm(po[:, :w], wv[:, k, dc * 128:(dc + 1) * 128], gts[k][:, t0:t0 + w],
                         start=(k == 0), stop=(k == nk - 1))
                P.stt("dve", self.xt[dc][ti], po[:, :w], self.MOD[l][:, 16 + dc, v:v + 1], self.xt[dc][ti],
                      ALU.mult, ALU.add)

    def odd_mixer(self, l):
        P = self.P
        d = self.d
        o = l // 2
        last = (l == self.depth - 1)
        tiles = [i for i in range(5) if not (last and i == 4)]
        self.norm_phase(l, 0)
        self.fence()
        htall = self.join([t for row in self.ht for t in row], self.HT.ap, "htall")
        Win = d["od_w_in"][o]
        self.bump, self.bump_end = 41 * 1024, 60 * 1024
        IGF = self.pbump([128, 18, 32], F32, "IGF")
        LFn = self.pbump([128, 2, 18, 8], F32, "LFn")
        BALL = self.pbump([128, 2, 18, 8], F32, "BALL")
        EC = self.pbump([128, 2, 18, 8], F32, "EC")
        THR = self.pbump([128, 2, 18, 8], F32, "THR")
        EBL = self.pbump([128, 2, 18, 8], F32, "EBL")
        WT = self.pbump([128, 2, 18, 8], F32, "WT")
        HNW = self.pbump([128, D], F32, "HNW")
        BIAS = self.pbump([128, 32], F32, "BIAS")
        MSK2 = self.pbump([128, 2, 128], F32, "MSK2")
        MSK = [MSK2[:, dd, :] for dd in range(2)]

        tsm = Rot([self.pbump([128, 2, 128], BF16, "sm%d" % i) for i in range(2)])
        tv2 = Rot([self.pbump([128, 2, 2, 129], BF16, "v2_%d" % i) for i in range(1)])
        tv3 = Rot([self.pbump([128, 2, 129], BF16, "v3_%d" % i) for i in range(1)])
        KTM = self.scr(7.5, [128, 128], BF16, "KTM")
        CBF = self.pbump([128, 129], BF16, "CBF")
        GN = self.pbump([128, 2, 128], BF16, "GN")
        CST = [self.pbump([128, 129], F32, "C_dir0")]
        T0 = self.scr(4.5, [128, 2, 128], F32, "T0")
        HS = self.scr(5.5, [128, 2, 128], F32, "HS")
        HOs = [self.scr(6.5 + 0.5 * i, [128, 256], BF16, "HO%d" % i) for i in range(2)]
        sg_rot = Rot([T0.re("p h v -> p (h v)"), HS.re("p h v -> p (h v)")])
        self.bcast_load(HNW, d["od_head_norm_w"][o:o + 1, :], D)
        P.dma("sp", BIAS[:, 0:16], d["od_i_bias"][o:o + 1, :].to_broadcast([128, 16]))
        P.dma("sp", BIAS[:, 16:32], d["od_f_bias"][o:o + 1, :].to_broadcast([128, 16]))
        for dd in range(2):
            P.ts("dve", MSK2[:, dd, :], self.tri[dd], 0.125, None, ALU.mult)
        for j in range(4):
            self.fence()
            stg_rot = Rot([self.scr(1.03125 * i, [128, 516], BF16, "stg%d" % i) for i in range(3)])
            dg_rot = Rot([self.scr(3.125, [128, 5, 128], BF16, "dg0")])
            QKT = self.pr(0, [128, 2, NT], BF16, "QKT")
            VA = self.pr(9, [128, 18, 2, 129], BF16, "VA")
            OG = self.pr(18.25, [128, 18, 256], BF16, "OG")
            SNAP = self.pr(27.25, [128, 18, 129], BF16, "SNAP")
            GTp = self.pr(32, [128, 2, NT], BF16, "GTp")
            slotA = self.wslot()
            wA = slotA[:, 0:4096].re("p (k n) -> p k n", k=8)
            P.dma("pool", wA[:, :, 0:128], Win[:, 128 * j:128 * j + 128].rearrange("(k p) n -> p k n", p=128))
            P.dma("pool", wA[:, :, 128:256], Win[:, 512 + 128 * j:512 + 128 * j + 128].rearrange("(k p) n -> p k n", p=128))
            P.dma("pool", wA[:, :, 256:512], Win[:, 1024 + 256 * j:1024 + 256 * j + 256].rearrange("(k p) n -> p k n", p=128))
            slotB = self.wslot()
            wB = slotB[:, 0:8 * 288].re("p (k n) -> p k n", k=8)
            P.dma("pool", wB[:, :, 0:256], Win[:, 2048 + 256 * j:2048 + 256 * j + 256].rearrange("(k p) n -> p k n", p=128))
            if j == 0:
                P.dma("pool", wB[:, :, 256:288], Win[:, 3072:3104].rearrange("(k p) n -> p k n", p=128))
            P.memset("pool", VA[:, :, :, 128:129], 1.0)
            if self.stop == "wload":
                continue
            cw = self.voff["od_conv_w"] + o * 40
            cb = self.voff["od_conv_b"] + o * 8
            self.conv_chunk(wA, 0, cw + j, cb + j, QKT[:, 0, :], htall, stg_rot, dg_rot)
            self.conv_chunk(wA, 128, cw + 4 + j, cb + 4 + j, QKT[:, 1, :], htall, stg_rot, dg_rot)
            if self.stop == "conv":
                continue
            for i in range(NCH):
                pt = self.psA.next()
                for k in range(8):
                    P.mm(pt[:, 0:256], htall[:, k, i * 128:(i + 1) * 128], wA[:, k, 256:512], start=(k == 0), stop=(k == 7))
                for k in range(8):
                    P.mm(pt[:, 256:512], htall[:, k, i * 128:(i + 1) * 128], wB[:, k, 0:256], start=(k == 0), stop=(k == 7))
                P.copy("act", VA[:, i, :, 0:128], pt[:, 0:256].re("p (h v) -> p h v", h=2))
                sgt = sg_rot.next()
                P.act(sgt, pt[:, 256:512], AF.Exp, scale=-1.0)
                P.act(sgt, sgt, AF.Ln, bias=1.0)
                P.act(sgt, sgt, AF.Exp, scale=-1.0)
                P.tt("pool", OG[:, i, :], sgt, HNW[:, 2 * j * 128:(2 * j + 2) * 128], ALU.mult)
                if j == 0 and self.stop != "vo_nogate":
                    pg = self.psA.next()
                    for k in range(8):
                        P.mm(pg[:, 0:32], htall[:, k, i * 128:(i + 1) * 128], wB[:, k, 256:288], start=(k == 0), stop=(k == 7))
                    P.tt("dve", IGF[:, i, :], pg[:, 0:32], BIAS, ALU.add)
            if self.stop in ("vo", "vo_nogate"):
                continue
            if j == 0:
                FGv = IGF[:, :, 16:32].re("p c (d h) -> p d c h", d=2)
                IGv = IGF[:, :, 0:16].re("p c (d h) -> p d c h", d=2)
                P.act(LFn, FGv, AF.Exp, scale=-1.0)
                P.act(LFn, LFn, AF.Ln, bias=1.0)
                P.ts("dve", LFn, LFn, -1.0, None, ALU.mult)
                pb_ = self.psA.next()
                pbv = pb_[:, 0:288].re("p (d c h) -> p d c h", c=18, d=2)
                for dd in range(2):
                    P.mm(pb_[:, dd * 144:(dd + 1) * 144], self.tri[dd], LFn[:, dd].re("p c h -> p (c h)"))
                P.copy("dve", BALL, pbv)
                pe_ = self.psA.next()
                P.mm(pe_[:, 0:288], self.ones_f, LFn.re("p d c h -> p (d c h)"))
                P.act(EBL, pe_[:, 0:288].re("p (d c h) -> p d c h", c=18, d=2), AF.Exp)
                P.tt("dve", EC, IGv, BALL, ALU.subtract)
                P.act(EC, EC, AF.Exp)
                P.act(THR, BALL, AF.Exp, scale=-1.0)
                P.tt("dve", WT, EC, EBL, ALU.mult)
            h0 = 2 * j
            if self.stop == "inproj":
                continue

            def state_update(Cst, i, dd):
                ptk = self.psA.next()
                ptkb = ptk.v(ptk.ap.bitcast(BF16))
                P.transpose(ptkb[:, 0:128], QKT[:, 1, i * 128:(i + 1) * 128], self.ident)
                P.copy("act", KTM, ptkb[:, 0:128])
                v3 = tv3.next()
                P.tt("pool", v3, VA[:, i], WT[:, dd, i, h0:h0 + 2].re("p (h o) -> p h o", o=1).bc([128, 2, 129]), ALU.mult)
                pd = self.psA.next()
                pdv = pd[:, 0:258].re("p (h n) -> p h n", h=2)
                for h in range(2):
                    P.mm(pdv[:, h, :], KTM, v3[:, h, :])
                for h in range(2):
                    r = slice(64 * h, 64 * h + 64)
                    P.stt("dve", Cst[r, :], Cst[r, :], EBL[r, dd, i, h0 + h:h0 + h + 1], pdv[r, h, :], ALU.mult, ALU.add)

            Cst = CST[0]
            P.memset("dve", Cst, 0.0)
            orderA = [16, 17] + list(range(16))
            for n_, i in enumerate(orderA):
                P.ts("dve", SNAP[:, i, :], Cst, 0.125, None, ALU.mult)
                if n_ % 4 == 2:
                    self.ada.step(load=(n_ < 14))
                if n_ < len(orderA) - 1:
                    state_update(Cst, i, 0)
            if self.stop == "passA":
                continue
            P.memset("dve", Cst, 0.0)
            orderB = [17, 16] + list(range(15, -1, -1))
            self.fence()
            accb = self.scr(0, [128, 512], F32, "accb")
            HSs = [HS, accb[:, 0:256].re("p (h v) -> p h v", h=2)]
            junk2 = accb[:, 256:384]
            TP = accb.v(accb.ap[:, 384:512].bitcast(BF16))

            def lockstep(*gens):
                gens = [g for g in gens if g is not None]
                while gens:
                    for g in list(gens):
                        try:
                            next(g)
                        except StopIteration:
                            gens.remove(g)

            def head(n_, i, out):
                cs = slice(i * 128, (i + 1) * 128)
                HSc = HSs[n_ % 2]
                HO = HOs[n_ % 2]
                pst = []
                for h in range(2):
                    r = slice(64 * h, 64 * h + 64)
                    pb1 = self.psA.next()
                    P.mm(pb1[:, 0:128], QKT[r, 1, cs], QKT[r, 0, cs])
                    pst.append(pb1)
                P.ts("dve", CBF, Cst, 0.125, None, ALU.mult)
                yield
                Rr = self.stat.next()
                smh = [tsm.next(), tsm.next()]
                for h in range(2):
                    P.tt("dve", smh[h], pst[h][:, 0:128].re("p (o t) -> p o t", o=1).bc([128, 2, 128]), MSK2, ALU.mult)
                    yield
                v2b = tv2.next()
                P.tt("pool", v2b, VA[:, i].re("p (o h) n -> p o h n", o=1).bc([128, 2, 2, 129]),
                     EC[:, :, i, h0:h0 + 2].re("p d (h o) -> p d h o", o=1).bc([128, 2, 2, 129]), ALU.mult)
                yield
                poss = [[None, None], [None, None]]
                for h in range(2):
                    r = slice(64 * h, 64 * h + 64)
                    for dd in range(2):
                        po = self.psB2[h].next()
                        cb_ = SNAP[:, i, :] if dd == 0 else CBF
                        P.mm(po[:, 0:129], smh[h][:, dd, :], v2b[:, dd, h, :], start=True, stop=False)
                        P.mm(po[:, 0:129], QKT[r, 0, cs], cb_[r, :], start=False, stop=True)
                        yield
                        c = 2 * dd + h
                        P.act(Rr[:, c:c + 1], po[:, 128:129], AF.Abs)
                        yield
                        poss[h][dd] = po
                P.tt("dve", Rr[:, 0:4].re("p (d h) -> p d h", d=2), Rr[:, 0:4].re("p (d h) -> p d h", d=2),
                     THR[:, :, i, h0:h0 + 2], ALU.max)
                yield
                P.recip(Rr[:, 0:4], Rr[:, 0:4])
                yield
                for h in range(2):
                    P.act(T0[:, h, :], poss[h][0][:, 0:128], AF.Identity, scale=Rr[:, h:h + 1])
                    yield
                for h in range(2):
                    P.stt("dve", HSc[:, h, :], poss[h][1][:, 0:128], Rr[:, 2 + h:3 + h], T0[:, h, :], ALU.mult, ALU.add)
                    yield

                def tail():
                    for h in range(2):
                        P.act(junk2, HSc[:, h, :], AF.Square, accum=Rr[:, 4 + h:5 + h])
                        yield
                    P.act(Rr[:, 6:8], Rr[:, 4:6], AF.Ln, bias=self.epsc, scale=1.0 / 128)
                    yield
                    P.act(Rr[:, 6:8], Rr[:, 6:8], AF.Exp, scale=-0.5)
                    yield
                    P.tt("pool", TP.re("p (h v) -> p h v", h=2), HSc, Rr[:, 6:8].re("p (h o) -> p h o", o=1).bc([128, 2, 128]), ALU.mult)
                    yield
                    P.tt("pool", GN.re("p h v -> p (h v)"), TP, OG[:, i, :], ALU.mult)
                    yield
                    ptg = self.psA.next()
                    ptgb = ptg.v(ptg.ap.bitcast(BF16))
                    for h in range(2):
                        P.transpose(ptgb[:, h * 128:(h + 1) * 128], GN[:, h, :], self.ident)
                    yield
                    P.copy("act", GTp[:, :, cs], ptgb[:, 0:256].re("p (h t) -> p h t", h=2))
                    yield
                out.append(tail)

            self.psB2 = [Rot(self.banks[4:6]), Rot(self.banks[6:8])]
            pending = None
            for n_, i in enumerate(orderB):
                need_out = not (last and i >= 16)
                out = []
                hg = head(n_, i, out) if need_out else None
                lockstep(hg, pending() if pending is not None else None)
                pending = out[0] if out else None
                if n_ < len(orderB) - 1:
                    state_update(Cst, i, 1)
            if pending is not None:
                lockstep(pending())
            if self.stop in ("passB", "B1", "B2", "B3"):
                continue
            wo = self.load_w(d["od_w_out"][o][256 * j:256 * j + 256, :], 256, 1024)
            self.out_proj(l, wo, 2, [GTp[:, 0, :], GTp[:, 1, :]], tiles)

    def epilogue(self):
        P = self.P
        self.fence()
        ostg = Rot([self.pr(4 * i, [128, D], F32, "ostg%d" % i) for i in range(2)])
        junk = self.pr(8, [128, 512], BF16, "junk")
        self.FNW = self.pr(12, [128, D], F32, "FNW")
        P.dma("sp", self.FNW, self.d["final_norm_w"].to_broadcast([128, D]))
        toks = []
        for t in range(16):
            ti, j = divmod(t, 4)
            pa = self.psA.next()
            pb = self.psA.next()
            for c in range(8):
                dst = pa if c < 4 else pb
                cc = c % 4
                P.transpose(dst[:, cc * 128:(cc + 1) * 128], self.xt[c][ti][:, j * 128:(j + 1) * 128], self.identf)
            st = self.stat.next()
            P.act(junk, pa, AF.Square, accum=st[:, 0:1])
            P.act(junk, pb, AF.Square, accum=st[:, 1:2])
            P.tt("dve", st[:, 2:3], st[:, 0:1], st[:, 1:2], ALU.add)
            P.act(st[:, 3:4], st[:, 2:3], AF.Ln, bias=self.epsc, scale=1.0 / D)
            P.act(st[:, 4:5], st[:, 3:4], AF.Exp, scale=-0.5)
            o = ostg.next()
            P.stt("dve", o[:, 0:512], pa, st[:, 4:5], self.FNW[:, 0:512], ALU.mult, ALU.mult)
            P.stt("dve", o[:, 512:1024], pb, st[:, 4:5], self.FNW[:, 512:1024], ALU.mult, ALU.mult)
            toks.append(P.dma("sp", self.out[t * 128:(t + 1) * 128, :], o))
        P.wait_all_dma("sp", toks)


def host_consts():
    p = np.arange(128)
    ident = np.eye(128, dtype=np.float32)
    tri0 = (p[:, None] <= p[None, :]).astype(np.float32)
    tri1 = (p[:, None] >= p[None, :]).astype(np.float32)
    neg0 = np.where(p[:, None] <= p[None, :], 0.0, -30000.0).astype(np.float32)
    neg1 = np.where(p[:, None] >= p[None, :], 0.0, -30000.0).astype(np.float32)
    cst = np.concatenate([ident, tri0, tri1, neg0, neg1], axis=1)
    pos = np.arange(NL)
    row = (pos // 64).astype(np.float32)
    col = (pos % 64).astype(np.float32)
    half = 16
    inv = (1.0 / (10000.0 ** (np.arange(0, half, 2, dtype=np.float32) / half))).astype(np.float32)
    ar = row[None, :] * inv[:, None]
    ac = col[None, :] * inv[:, None]
    C = np.ones((32, NT), np.float32)
    S = np.zeros((32, NT), np.float32)
    C[0:8, :NL] = np.cos(ar); C[8:16, :NL] = np.cos(ar); C[16:24, :NL] = np.cos(ac); C[24:32, :NL] = np.cos(ac)
    S[0:8, :NL] = -np.sin(ar); S[8:16, :NL] = np.sin(ar); S[16:24, :NL] = -np.sin(ac); S[24:32, :NL] = np.sin(ac)
    import os
    if os.environ.get("NOROPE"):
        C[:] = 1.0
        S[:] = 0.0
    return cst, C, S


_NC_CACHE = {}


def make_in_maps(inputs, n):
    cst, C, S = host_consts()
    f = lambda a: np.ascontiguousarray(np.asarray(a, dtype=np.float32))
    shared = {
        "c_ctx": f(inputs["c_ctx"]).reshape(8, 128),
        "ada_w": f(inputs["ada_w"]), "ada_b": f(inputs["ada_b"]).reshape(192, 128),
        "norm_mix_w": f(inputs["norm_mix_w"]).reshape(32, 128), "norm_mlp_w": f(inputs["norm_mlp_w"]).reshape(32, 128),
        "mlp_w1": f(inputs["mlp_w1"]), "mlp_w2": f(inputs["mlp_w2"]),
        "ev_w_in": f(inputs["ev_w_in"]), "ev_q_norm_w": f(inputs["ev_q_norm_w"]).reshape(4, 128),
        "ev_w_uq": f(inputs["ev_w_uq"]), "ev_kv_norm_w": f(inputs["ev_kv_norm_w"]).reshape(4, 128),
        "ev_w_ukv": f(inputs["ev_w_ukv"]), "ev_conv_w": f(inputs["ev_conv_w"]).reshape(80, 128),
        "ev_conv_b": f(inputs["ev_conv_b"]).reshape(16, 128), "ev_dt_bias": f(inputs["ev_dt_bias"]).reshape(2, 16),
        "ev_a_log": f(inputs["ev_a_log"]).reshape(2, 16), "ev_d_skip": f(inputs["ev_d_skip"]),
        "ev_ssm_norm_w": f(inputs["ev_ssm_norm_w"]), "ev_w_out": f(inputs["ev_w_out"]),
        "od_w_in": f(inputs["od_w_in"]), "od_conv_w": f(inputs["od_conv_w"]).reshape(80, 128),
        "od_conv_b": f(inputs["od_conv_b"]).reshape(16, 128), "od_i_bias": f(inputs["od_i_bias"]).reshape(2, 16),
        "od_f_bias": f(inputs["od_f_bias"]).reshape(2, 16), "od_head_norm_w": f(inputs["od_head_norm_w"]),
        "od_w_out": f(inputs["od_w_out"]), "final_norm_w": f(inputs["final_norm_w"]).reshape(1, D),
        "cst": cst, "ropeC": C, "ropeS": S,
    }
    x = f(inputs["x"]); c = f(inputs["c"]); ctx = f(inputs["ctx"])
    maps = []
    for b in range(n):
        m = dict(shared)
        m["x"] = x[b]
        m["ctx"] = ctx[b]
        m["c"] = c[b].reshape(8, 128)
        maps.append(m)
    return maps


def kernel(**inputs):
    n = 8
    key = "full"
    if key not in _NC_CACHE:
        mk = MK()
        _NC_CACHE[key] = (mk, mk.P.build())
    mk, nc = _NC_CACHE[key]
    maps = make_in_maps(inputs, n)
    used = set(mk.d.keys())
    maps = [{k: v for k, v in m.items() if k in used} for m in maps]
    res = run_bass_kernel_spmd(nc, maps, core_ids=list(range(n)))
    return np.stack([r["out"] for r in res.results], axis=0).astype(np.float32)
```

```python
import bisect
from contextlib import ExitStack

import numpy as np
import concourse.bass as bass
import concourse.mybir as mybir
from concourse.bass_utils import run_bass_kernel_spmd

F32 = mybir.dt.float32
BF16 = mybir.dt.bfloat16
AF = mybir.ActivationFunctionType
ALU = mybir.AluOpType
AX = mybir.AxisListType

COMPUTE = ("pe", "act", "dve", "pool")
QUEUES = ("pe", "act", "dve", "pool", "sp")


class Buf:
    __slots__ = ("name", "w", "r", "dkey", "dcount", "psum")

    def __init__(self, name):
        self.name = name
        self.psum = False
        self.w = None
        self.r = []
        self.dkey = None
        self.dcount = 0


class T:
    __slots__ = ("ap", "buf")

    def __init__(self, ap, buf):
        self.ap = ap
        self.buf = buf

    def __getitem__(self, key):
        return T(self.ap[key], self.buf)

    def v(self, ap):
        return T(ap, self.buf)

    def sub(self, name, key):
        return T(self.ap[key], Buf(name))

    def bc(self, shape):
        return T(self.ap.to_broadcast(shape), self.buf)

    def re(self, s, **kw):
        return T(self.ap.rearrange(s, **kw), self.buf)


class Op:
    __slots__ = ("fn", "waits", "signal", "dma", "idx")

    def __init__(self, fn, waits, dma=None):
        self.fn = fn
        self.waits = waits
        self.signal = False
        self.dma = dma
        self.idx = 0


class Prog:
    def __init__(self):
        self.nc = bass.Bass("TRN2", target_bir_lowering=False)
        self.es = ExitStack()
        self.ops = {q: [] for q in QUEUES}
        self.count = {q: 0 for q in COMPUTE}
        self.known = {q: {} for q in QUEUES}
        self.snaps = {q: [(0, {})] for q in COMPUTE}
        self.dsnap = {}
        self.ndma = 0
        self.sbuf_used = 0
        self.psum_i = 0
        self.psum = []

    def dram(self, name, shape, dtype, kind):
        return self.nc.dram_tensor(name, list(shape), dtype, kind=kind).ap()

    def tile(self, name, shape, dtype):
        t = self.es.enter_context(self.nc.sbuf_tensor(name, list(shape), dtype))
        per = int(np.prod(shape[1:])) * (4 if dtype == F32 else 2)
        self.sbuf_used += per
        return T(t[:] if not isinstance(t, bass.AP) else t, Buf(name))

    def psum_tile(self, name, shape, dtype):
        t = self.es.enter_context(self.nc.psum_tensor(name, list(shape), dtype))
        b = Buf(name)
        b.psum = True
        return T(t[:] if not isinstance(t, bass.AP) else t, b)

    def _snapshot(self, key, count):
        if isinstance(key, str):
            sn = self.snaps[key]
            i = bisect.bisect_right(sn, count, key=lambda e: e[0]) - 1
            return sn[i][1]
        return self.dsnap.get((key, count), {})

    def _resolve(self, q, reads, writes):
        need = {}

        def add(tok, is_write_dep_on_write=False):
            if tok is None:
                return
            k, c = tok
            if need.get(k, 0) < c:
                need[k] = c

        def addw(w, pe_ok):
            if w is None:
                return
            if isinstance(w, list):
                for x in w:
                    add(x)
            elif not (pe_ok and w[0] == "pe"):
                add(w)

        for b in reads:
            addw(b.w, False)
            if b.psum:
                for tok in b.r:
                    if tok[0] != q:
                        add(tok)
        for b in writes:
            addw(b.w, q == "pe")
            for tok in b.r:
                add(tok)
        known = self.known[q]
        waits = []
        changed = False
        for k, c in need.items():
            if known.get(k, 0) >= c:
                continue
            waits.append((k, c))
        if waits:
            known = dict(known)
            for k, c in waits:
                snap = self._snapshot(k, c)
                for kk, cc in snap.items():
                    if known.get(kk, 0) < cc:
                        known[kk] = cc
                if known.get(k, 0) < c:
                    known[k] = c
            self.known[q] = known
            changed = True
        return waits, changed

    def op(self, q, fn, reads=(), writes=()):
        rb = [t.buf for t in reads if isinstance(t, T)]
        wb = [t.buf for t in writes if isinstance(t, T)]
        waits, changed = self._resolve(q, rb, wb)
        self.count[q] += 1
        idx = self.count[q]
        if changed:
            self.snaps[q].append((idx, self.known[q]))
        o = Op(fn, waits)
        o.idx = idx
        self.ops[q].append(o)
        tok = (q, idx)
        for b in rb:
            b.r.append(tok)
        for b in wb:
            b.w = tok
            b.r = []
        return o

    def dma(self, q, out, in_, owner=None, **kw):
        reads = [in_] if isinstance(in_, T) else []
        writes = [out] if isinstance(out, T) else []
        own = owner if owner is not None else (out if isinstance(out, T) else in_)
        ob = own.buf
        rb = [t.buf for t in reads]
        wb = [t.buf for t in writes]
        waits, changed = self._resolve(q, rb, wb)
        if q in COMPUTE:
            pass
        if ob.dkey is None:
            self.ndma += 1
            ob.dkey = self.ndma
        ob.dcount += 1
        tok = (ob.dkey, ob.dcount)
        self.dsnap[tok] = self.known[q]
        oap = out.ap if isinstance(out, T) else out
        iap = in_.ap if isinstance(in_, T) else in_
        o = Op(lambda e: e.dma_start(out=oap, in_=iap, **kw), waits, dma=ob.dkey)
        self.ops[q].append(o)
        for b in rb:
            b.r.append(tok)
        for b in wb:
            b.w = tok
            b.r = []
        return tok

    def wait_all_dma(self, q, toks):
        o = Op(None, list(toks))
        self.ops[q].append(o)

    def build(self):
        nc = self.nc
        sig = {q: set() for q in COMPUTE}
        for q in QUEUES:
            for o in self.ops[q]:
                for k, c in o.waits:
                    if isinstance(k, str):
                        sig[k].add(c)
        sigmap = {}
        for q in COMPUTE:
            s = sorted(sig[q])
            sigmap[q] = {c: i + 1 for i, c in enumerate(s)}
        for q in COMPUTE:
            for o in self.ops[q]:
                if o.dma is None and o.fn is not None and o.idx in sigmap[q]:
                    o.signal = True
        sems = {q: self.es.enter_context(nc.semaphore("s_" + q)) for q in COMPUTE}
        dsems = {k: self.es.enter_context(nc.semaphore("d%d" % k)) for k in range(1, self.ndma + 1)}
        self.nsig = {q: len(sigmap[q]) for q in COMPUTE}

        def emit(q, eng):
            for o in self.ops[q]:
                for k, c in o.waits:
                    if isinstance(k, str):
                        eng.wait_ge(sems[k], sigmap[k][c])
                    else:
                        eng.wait_ge(dsems[k], 16 * c)
                if o.fn is None:
                    continue
                ins = o.fn(eng)
                if o.dma is not None:
                    ins.then_inc(dsems[o.dma], 16)
                elif o.signal:
                    ins.then_inc(sems[q], 1)

        with nc.Block() as block:
            @block.tensor
            def _(e):
                emit("pe", e)

            @block.scalar
            def _(e):
                emit("act", e)

            @block.vector
            def _(e):
                emit("dve", e)

            @block.gpsimd
            def _(e):
                emit("pool", e)

            @block.sync
            def _(e):
                emit("sp", e)
        self.es.close()
        return nc

    @staticmethod
    def _a(x):
        return x.ap if isinstance(x, T) else x

    def mm(self, out, lhsT, rhs, start=True, stop=True):
        o, l, r = out.ap, lhsT.ap, rhs.ap
        return self.op("pe", lambda e: e.matmul(o, l, r, start=start, stop=stop),
                       reads=(lhsT, rhs), writes=(out,))

    def transpose(self, out, in_, ident):
        o, i, d = out.ap, in_.ap, ident.ap
        return self.op("pe", lambda e: e.transpose(o, i, d), reads=(in_, ident), writes=(out,))

    def act(self, out, in_, func, bias=0.0, scale=1.0, accum=None, q="act"):
        o, i = out.ap, in_.ap
        b, s = self._a(bias), self._a(scale)
        kw = {}
        if accum is not None:
            kw["accum_out"] = accum.ap
        wr = (out,) if accum is None else (out, accum)
        return self.op(q, lambda e: e.activation(o, i, func, bias=b, scale=s, **kw),
                       reads=(in_, bias, scale), writes=wr)

    def tt(self, q, out, in0, in1, op):
        o, a, b = out.ap, in0.ap, in1.ap
        return self.op(q, lambda e: e.tensor_tensor(o, a, b, op), reads=(in0, in1), writes=(out,))

    def ts(self, q, out, in0, s1, s2, op0, op1=None, accum=None):
        o, a = out.ap, in0.ap
        x1, x2 = self._a(s1), self._a(s2)
        kw = {}
        if op1 is not None:
            kw["op1"] = op1
        if accum is not None:
            kw["accum_out"] = accum.ap
        wr = (out,) if accum is None else (out, accum)
        return self.op(q, lambda e: e.tensor_scalar(o, a, x1, x2, op0, **kw),
                       reads=(in0, s1, s2), writes=wr)

    def stt(self, q, out, in0, scalar, in1, op0, op1):
        o, a, b = out.ap, in0.ap, in1.ap
        s = self._a(scalar)
        return self.op(q, lambda e: e.scalar_tensor_tensor(o, a, s, b, op0, op1),
                       reads=(in0, scalar, in1), writes=(out,))

    def copy(self, q, out, in_):
        o, i = out.ap, in_.ap
        if q == "act":
            return self.op(q, lambda e: e.copy(o, i), reads=(in_,), writes=(out,))
        return self.op(q, lambda e: e.tensor_copy(o, i), reads=(in_,), writes=(out,))

    def memset(self, q, out, val):
        o = out.ap
        return self.op(q, lambda e: e.memset(o, val), writes=(out,))

    def reduce(self, q, out, in_, op, axis=AX.X):
        o, i = out.ap, in_.ap
        return self.op(q, lambda e: e.tensor_reduce(o, i, axis, op), reads=(in_,), writes=(out,))

    def recip(self, out, in_):
        o, i = out.ap, in_.ap
        return self.op("dve", lambda e: e.reciprocal(o, i), reads=(in_,), writes=(out,))
import ml_dtypes

D = 1024
NL = 2048
NCX = 256
NT = NL + NCX
EPS = 1e-6
TILES = [(0, 512), (512, 512), (1024, 512), (1536, 512), (2048, 256)]
NCH = 18


class Rot:
    def __init__(self, items):
        self.items = items
        self.i = 0

    def next(self):
        t = self.items[self.i % len(self.items)]
        self.i += 1
        return t


class MK:
    def __init__(self, depth=4, mixers=("even", "odd"), dbg=None, stop=None):
        self.stop = stop
        self.depth = depth
        self.mixers = mixers
        P = self.P = Prog()
        self.dbg = dbg
        self.declare_io()
        self.alloc()
        self.prologue()
        for l in range(depth):
            self.layer(l)
        self.epilogue()

    def declare_io(self):
        P = self.P
        I = "ExternalInput"
        d = {}
        d["x"] = P.dram("x", [NL, D], F32, I)
        d["ctx"] = P.dram("ctx", [NCX, D], F32, I)
        d["c"] = P.dram("c", [8, 128], F32, I)
        d["c_ctx"] = P.dram("c_ctx", [8, 128], F32, I)
        d["ada_w"] = P.dram("ada_w", [4, D, 6 * D], F32, I)
        d["ada_b"] = P.dram("ada_b", [4 * 48, 128], F32, I)
        d["norm_mix_w"] = P.dram("norm_mix_w", [32, 128], F32, I)
        d["norm_mlp_w"] = P.dram("norm_mlp_w", [32, 128], F32, I)
        d["mlp_w1"] = P.dram("mlp_w1", [4, D, 4 * D], F32, I)
        d["mlp_w2"] = P.dram("mlp_w2", [4, 4 * D, D], F32, I)
        d["ev_w_in"] = P.dram("ev_w_in", [2, D, 2096], F32, I)
        d["ev_q_norm_w"] = P.dram("ev_q_norm_w", [4, 128], F32, I)
        d["ev_w_uq"] = P.dram("ev_w_uq", [2, 256, 768], F32, I)
        d["ev_kv_norm_w"] = P.dram("ev_kv_norm_w", [4, 128], F32, I)
        d["ev_w_ukv"] = P.dram("ev_w_ukv", [2, 256, 1024], F32, I)
        d["ev_conv_w"] = P.dram("ev_conv_w", [80, 128], F32, I)
        d["ev_conv_b"] = P.dram("ev_conv_b", [16, 128], F32, I)
        d["ev_dt_bias"] = P.dram("ev_dt_bias", [2, 16], F32, I)
        d["ev_a_log"] = P.dram("ev_a_log", [2, 16], F32, I)
        d["ev_d_skip"] = P.dram("ev_d_skip", [2, 8], F32, I)
        d["ev_ssm_norm_w"] = P.dram("ev_ssm_norm_w", [2, 512], F32, I)
        d["ev_w_out"] = P.dram("ev_w_out", [2, D, D], F32, I)
        d["od_w_in"] = P.dram("od_w_in", [2, D, 3104], F32, I)
        d["od_conv_w"] = P.dram("od_conv_w", [80, 128], F32, I)
        d["od_conv_b"] = P.dram("od_conv_b", [16, 128], F32, I)
        d["od_i_bias"] = P.dram("od_i_bias", [2, 16], F32, I)
        d["od_f_bias"] = P.dram("od_f_bias", [2, 16], F32, I)
        d["od_head_norm_w"] = P.dram("od_head_norm_w", [2, D], F32, I)
        d["od_w_out"] = P.dram("od_w_out", [2, D, D], F32, I)
        d["final_norm_w"] = P.dram("final_norm_w", [1, D], F32, I)
        d["cst"] = P.dram("cst", [128, 5 * 128], F32, I)
        d["ropeC"] = P.dram("ropeC", [32, NT], F32, I)
        d["ropeS"] = P.dram("ropeS", [32, NT], F32, I)
        self.d = d
        self.out = P.dram("out", [NL, D], F32, "ExternalOutput")
        zs = P.nc.dram_tensor("zscr", [NT, 512], BF16, kind="Internal").ap()
        self.zscr = zs
        self.zch = [T(zs[i * 128:(i + 1) * 128, :], Buf("zch%d" % i)) for i in range(NCH)]

    def alloc(self):
        P = self.P
        self.XT = P.tile("XT", [128, 8, NT], F32)
        self.xt = [[self.XT.sub("xt%d_%d" % (c, i), (slice(None), c, slice(t0, t0 + w)))
                    for i, (t0, w) in enumerate(TILES)] for c in range(8)]
        self.PR = P.tile("PR", [128, 49152], BF16)
        self.pr_bufs = []
        self.pr_fence = []
        self.arena = [P.tile("wa%d" % i, [128, 4096], BF16) for i in range(3)]
        self.arot = Rot(self.arena)
        self.SCR = P.tile("SCR", [128, 4096], BF16)
        self.cst = P.tile("cst_sb", [128, 5 * 128], F32)
        self.identf = self.cst[:, 0:128]
        self.tri = [self.cst[:, 128:256], self.cst[:, 256:384]]
        self.neg = [self.cst[:, 384:512], self.cst[:, 512:640]]
        self.ident = P.tile("ident", [128, 128], BF16)
        self.ones_bf = P.tile("ones_bf", [128, 128], BF16)
        self.ones_f = P.tile("ones_f", [128, 128], F32)
        self.epsc = P.tile("epsc", [128, 1], F32)
        self.VEC = P.tile("VEC", [128, 512], F32)
        self.SV = P.tile("SV", [128, 8, 2], BF16)
        self.MOD = [P.tile("MOD%d" % l, [128, 48, 2], F32) for l in range(4)]
        self.AB = [P.tile("AB%d" % l, [128, 2, 8, 2], F32) for l in range(4)]
        self.stat = Rot([P.tile("stat%d" % i, [128, 8], F32) for i in range(4)])
        banks = [P.psum_tile("pb%d" % i, [128, 512], F32) for i in range(8)]
        self.banks = banks
        self.psA = Rot(banks[0:4])
        self.psB = Rot(banks[4:7])
        self.psC = Rot(banks[7:8])
        self.scr_bufs = []
        self.scr_fence = []

    def _carve(self, base, off, shape, dtype, name, bufs, fence):
        n = int(np.prod(shape[1:]))
        nb = n * (4 if dtype == F32 else 2)
        assert off % 4 == 0 and off + nb <= base.ap.shape[1] * 2, (name, off, nb)
        ap = base.ap[0:shape[0], off // 2: (off + nb) // 2]
        if dtype == F32:
            ap = ap.bitcast(F32)
        if len(shape) > 2:
            names = " ".join("a%d" % i for i in range(len(shape) - 1))
            kw = {"a%d" % i: shape[i + 1] for i in range(len(shape) - 2)}
            ap = ap.rearrange("p (%s) -> p %s" % (names, names), **kw)
        b = Buf(name)
        b.r = list(fence)
        bufs.append(b)
        return T(ap, b)

    def pr(self, off_kib, shape, dtype, name):
        return self._carve(self.PR, int(off_kib * 1024), shape, dtype, name, self.pr_bufs, self.pr_fence)

    def pbump(self, shape, dtype, name):
        n = int(np.prod(shape[1:])) * (4 if dtype == F32 else 2)
        off = (self.bump + 31) // 32 * 32
        assert off + n <= self.bump_end, (name, off, n, self.bump_end)
        self.bump = off + n
        return self._carve(self.PR, off, shape, dtype, name, self.pr_bufs, self.pr_fence)

    def scr(self, off_kib, shape, dtype, name):
        return self._carve(self.SCR, int(off_kib * 1024), shape, dtype, name, self.scr_bufs, self.scr_fence)

    def fence(self):
        for bufs, attr in ((self.pr_bufs, "pr_fence"), (self.scr_bufs, "scr_fence")):
            toks = {}
            for b in bufs:
                ws = b.w if isinstance(b.w, list) else ([b.w] if b.w else [])
                for tok in ws + b.r:
                    if toks.get(tok[0], 0) < tok[1]:
                        toks[tok[0]] = tok[1]
            for k, c in getattr(self, attr):
                if toks.get(k, 0) < c:
                    toks[k] = c
            setattr(self, attr, list(toks.items()))
            for b in bufs:
                if len(b.r) > 8:
                    m = {}
                    for k, c in b.r:
                        if m.get(k, 0) < c:
                            m[k] = c
                    b.r = list(m.items())

    def subs(self, t, name):
        out = []
        for c in range(t.ap.shape[1]):
            row = []
            for i, (t0, w) in enumerate(TILES):
                b = Buf("%s%d_%d" % (name, c, i))
                b.r = list(t.buf.r)
                self.pr_bufs.append(b)
                row.append(T(t.ap[:, c, t0:t0 + w], b))
            out.append(row)
        return out

    def norm_scratch(self):
        self.sq = Rot([self.scr(i, [128, 512], BF16, "sq%d" % i) for i in range(2)])
        self.rs = Rot([self.scr(2, [128, 512], F32, "rs0")])
        self.tmpf = Rot([self.scr(4 + 2 * i, [128, 512], F32, "tmpf%d" % i) for i in range(2)])

    def wslot(self):
        return self.arot.next()

    def load_w(self, dram_ap, rows, cols, slot=None):
        P = self.P
        k = rows // 128
        s = slot if slot is not None else self.wslot()
        v = s[:, 0:k * cols].re("p (k n) -> p k n", k=k)
        P.dma("pool", v, dram_ap.rearrange("(k p) n -> p k n", p=128))
        return v

    def prologue(self):
        P = self.P
        d = self.d
        P.dma("sp", self.cst, d["cst"])
        P.copy("dve", self.ident, self.identf)
        P.memset("dve", self.ones_bf, 1.0)
        P.memset("dve", self.ones_f, 1.0)
        P.memset("dve", self.epsc, EPS)
        rows = [("ada_b", 192), ("norm_mix_w", 32), ("norm_mlp_w", 32), ("final_norm_w_rows", 8), ("c", 8),
                ("c_ctx", 8), ("ev_conv_w", 80), ("ev_conv_b", 16), ("od_conv_w", 80), ("od_conv_b", 16),
                ("ev_q_norm_w", 4), ("ev_kv_norm_w", 4)]
        self.voff = {}
        off = 0
        for n, k in rows:
            self.voff[n] = off
            off += k
        assert off <= 512
        stg = [self.pr(16 + 0.5 * i, [128, 128], F32, "vstg%d" % i) for i in range(4)]
        for s in stg:
            P.memset("dve", s, 0.0)
        for n, k in rows:
            src = d["final_norm_w"].rearrange("o (r p) -> (o r) p", p=128) if n == "final_norm_w_rows" else d[n]
            o = self.voff[n]
            done = 0
            while done < k:
                ti, r0 = divmod(o + done, 128)
                m = min(k - done, 128 - r0)
                P.dma("sp", stg[ti][r0:r0 + m, :], src[done:done + m, :])
                done += m
        for i in range(4):
            pt = self.psC.next()
            P.transpose(pt[:, 0:128], stg[i], self.identf)
            P.copy("dve", self.VEC[:, i * 128:(i + 1) * 128], pt[:, 0:128])
        oc, occ = self.voff["c"], self.voff["c_ctx"]
        P.act(self.SV[:, :, 0], self.VEC[:, oc:oc + 8], AF.Silu)
        P.act(self.SV[:, :, 1], self.VEC[:, occ:occ + 8], AF.Silu)
        xs = Rot([self.pr(4 * i, [128, D], F32, "xstg%d" % i) for i in range(4)])
        for ti, (t0, w) in enumerate(TILES):
            subs = []
            for j in range(w // 128):
                s = xs.next()
                tok = t0 + j * 128
                src = d["x"][tok:tok + 128, :] if tok < NL else d["ctx"][tok - NL:tok - NL + 128, :]
                P.dma("sp", s, src)
                subs.append(s)
            for c in range(8):
                pt = self.psA.next()
                for j, s in enumerate(subs):
                    P.transpose(pt[:, j * 128:(j + 1) * 128], s[:, c * 128:(c + 1) * 128], self.identf)
                eng = "act" if c % 2 else "dve"
                P.copy(eng, self.xt[c][ti], pt[:, 0:w])
        for q in range(12):
            self.ada_piece(0, q)
        self.ada_finish(0)

    def ada_load(self, l, q, slot=None):
        return self.load_w(self.d["ada_w"][l][:, q * 512:(q + 1) * 512], 1024, 512, slot=slot)

    def ada_mm(self, l, q, wv):
        P = self.P
        pt = self.psA.next()
        av = pt[:, 0:8].re("p (j v) -> p j v", v=2)
        for i in range(4):
            for k in range(8):
                P.mm(av[:, i, :], wv[:, k, i * 128:(i + 1) * 128], self.SV[:, k, :], start=(k == 0), stop=(k == 7))
        P.copy("act", self.MOD[l][:, q * 4:(q + 1) * 4, :], av)

    def ada_piece(self, l, q):
        self.ada_mm(l, q, self.ada_load(l, q))

    def ada_finish(self, l):
        P = self.P
        ob = self.voff["ada_b"] + l * 48
        P.tt("dve", self.MOD[l], self.MOD[l], self.VEC[:, ob:ob + 48].re("p (j o) -> p j o", o=1).bc([128, 48, 2]), ALU.add)
        for which, (scj, nwn) in enumerate(((1, "norm_mix_w"), (4, "norm_mlp_w"))):
            on = self.voff[nwn] + l * 8
            P.stt("dve", self.AB[l][:, which], self.MOD[l][:, scj * 8:(scj + 1) * 8, :], 1.0,
                  self.VEC[:, on:on + 8].re("p (j o) -> p j o", o=1).bc([128, 8, 2]), ALU.add, ALU.mult)

    class AdaStream:
        def __init__(self, mk, l):
            self.mk, self.l = mk, l
            self.q_loaded, self.q_done, self.wv = 0, 0, None
            self.slot = None

        def step(self, load=True):
            mk, l = self.mk, self.l
            if l is None:
                return
            if self.wv is not None:
                mk.ada_mm(l, self.q_done, self.wv)
                self.q_done += 1
                self.wv = None
            if load and self.q_loaded < 12:
                self.wv = mk.ada_load(l, self.q_loaded, slot=self.slot)
                self.q_loaded += 1

        def finish(self):
            if self.l is None:
                return
            while self.q_done < 12:
                self.step()
            self.mk.ada_finish(self.l)

    def norm_phase(self, l, which, tiles=None):
        P = self.P
        self.norm_scratch()
        self.HT = self.pr(60, [128, 8, NT], BF16, "HT")
        self.ht = self.subs(self.HT, "ht")
        shj = 0 if which == 0 else 3
        for ti, (t0, w) in enumerate(TILES):
            if tiles is not None and ti not in tiles:
                continue
            v = 1 if ti == 4 else 0
            ssp = self.psC.next()
            for c in range(8):
                sq = self.sq.next()
                if c % 4 == 3:
                    P.act(sq[:, :w], self.xt[c][ti], AF.Square)
                else:
                    P.tt("pool", sq[:, :w], self.xt[c][ti], self.xt[c][ti], ALU.mult)
                P.mm(ssp[:, :w], self.ones_bf, sq[:, :w], start=(c == 0), stop=(c == 7))
            rs = self.rs.next()
            P.act(rs[:, :w], ssp[:, :w], AF.Ln, bias=self.epsc, scale=1.0 / D)
            P.act(rs[:, :w], rs[:, :w], AF.Exp, scale=-0.5)
            for c in range(8):
                tmp = self.tmpf.next()
                P.tt("dve", tmp[:, :w], self.xt[c][ti], rs[:, :w], ALU.mult)
                P.act(self.ht[c][ti], tmp[:, :w], AF.Identity,
                      bias=self.MOD[l][:, shj * 8 + c, v:v + 1], scale=self.AB[l][:, which, c, v:v + 1])

    def mlp(self, l):
        P = self.P
        d = self.d
        last = (l == self.depth - 1)
        tiles = [i for i in range(5) if not (last and i == 4)]
        self.fence()
        self.norm_phase(l, 1, tiles)
        self.fence()
        self.relu = Rot([self.scr(i, [128, 512], BF16, "relu%d" % i) for i in range(3)])
        hid = [self.subs(self.pr(18 * b, [128, 4, NT], BF16, "hid%d" % b), "hid%d_" % b) for b in range(2)]
        nxt = None
        for g in range(8):
            w1 = self.load_w(d["mlp_w1"][l][:, g * 512:(g + 1) * 512], 1024, 512)
            w2 = self.load_w(d["mlp_w2"][l][g * 512:(g + 1) * 512, :], 512, 1024)
            hb = hid[g % 2]
            for ti in tiles:
                t0, w = TILES[ti]
                for hc in range(4):
                    pt = self.psA.next()
                    for k in range(8):
                        P.mm(pt[:, :w], w1[:, k, hc * 128:(hc + 1) * 128], self.ht[k][ti],
                             start=(k == 0), stop=(k == 7))
                    r = self.relu.next()
                    P.act(r[:, :w], pt[:, :w], AF.Relu)
                    P.tt("pool", hb[hc][ti], r[:, :w], r[:, :w], ALU.mult)
            if nxt is not None and g < 6:
                self.ada_piece(nxt, 2 * g)
            for ti in tiles:
                t0, w = TILES[ti]
                v = 1 if ti == 4 else 0
                for dc in range(8):
                    po = self.psB.next()
                    for hc in range(4):
                        P.mm(po[:, :w], w2[:, hc, dc * 128:(dc + 1) * 128], hb[hc][ti],
                             start=(hc == 0), stop=(hc == 3))
                    P.stt("dve", self.xt[dc][ti], po[:, :w], self.MOD[l][:, 40 + dc, v:v + 1], self.xt[dc][ti],
                          ALU.mult, ALU.add)
            if nxt is not None and g < 6:
                self.ada_piece(nxt, 2 * g + 1)
        if nxt is not None:
            self.ada_finish(nxt)

    def layer(self, l):
        if getattr(self, "halt", False):
            return
        self.fence()
        self.ada = MK.AdaStream(self, l + 1 if l + 1 < self.depth else None)
        if l % 2 == 0 and "even" in self.mixers:
            self.even_mixer(l)
        if l % 2 == 1 and "odd" in self.mixers:
            self.odd_mixer(l)
        if getattr(self, "halt", False):
            return
        self.ada.finish()
        self.mlp(l)

    def even_mixer(self, l):
        P = self.P
        d = self.d
        e = l // 2
        last = (l == self.depth - 1)
        tiles = [i for i in range(5) if not (last and i == 4)]
        do_ssd = "nossd" not in self.mixers
        do_attn = "noattn" not in self.mixers
        self.norm_phase(l, 0)
        self.fence()
        htall = self.join([t for row in self.ht for t in row], self.HT.ap, "htall")
        Win = d["ev_w_in"][e]
        XSB = self.pr(0, [128, 4, NT], BF16, "XSB")
        BCB = self.pr(18, [128, 4, NT], BF16, "BCB")
        CQT = self.pr(36, [128, 2, NT], BF16, "CQT")
        CKVT = self.pr(45, [128, 2, NT], BF16, "CKVT")
        KRAB = self.pr(54, [128, NT], BF16, "KRAB")
        self.bump, self.bump_end = int(58.5 * 1024), 60 * 1024
        DT = self.pbump([128, 18, 16], F32, "DT")
        A_b = self.pbump([128, 16], F32, "A_b")
        DTB = self.pbump([128, 16], F32, "DTB")
        DSK = self.pbump([128, 8], F32, "DSK")
        P.dma("sp", A_b, d["ev_a_log"][e:e + 1, :].to_broadcast([128, 16]))
        P.dma("sp", DTB, d["ev_dt_bias"][e:e + 1, :].to_broadcast([128, 16]))
        P.dma("sp", DSK, d["ev_d_skip"][e:e + 1, :].to_broadcast([128, 8]))
        P.act(A_b, A_b, AF.Exp)
        P.ts("dve", A_b, A_b, -1.0, None, ALU.mult)
        stg_rot = Rot([self.scr(1.03125 * i, [128, 516], BF16, "stg%d" % i) for i in range(3)])
        dg_rot = Rot([self.scr(3.125, [128, 5, 128], BF16, "dg0")])
        zs_rot = Rot([self.scr(4.5 + i, [128, 512], BF16, "zs%d" % i) for i in range(2)])
        cw = self.voff["ev_conv_w"] + e * 40
        cb = self.voff["ev_conv_b"] + e * 8
        wz = self.load_w(Win[:, 544:1056], 1024, 512)
        slot_dk = self.wslot()
        wdk = slot_dk[:, 0:8 * 112].re("p (k n) -> p k n", k=8)
        P.memset("pool", wdk[:, :, 48:80], 0.0)
        P.dma("pool", wdk[:, :, 0:16], Win[:, 2080:2096].rearrange("(k p) n -> p k n", p=128))
        P.dma("pool", wdk[:, :, 80:112], Win[:, 512:544].rearrange("(k p) n -> p k n", p=128))
        for b_ in range(2):
            for hf in range(2):
                so = 512 + 16 * b_ + 8 * (1 - hf)
                do = 16 + 16 * b_ + 8 * hf
                P.dma("pool", wdk[:, :, do:do + 8], Win[:, so:so + 8].rearrange("(k p) n -> p k n", p=128))
        for i in range(NCH):
            cs = slice(i * 128, (i + 1) * 128)
            pz = self.psA.next()
            for k in range(8):
                P.mm(pz, htall[:, k, cs], wz[:, k, :], start=(k == 0), stop=(k == 7))
            zs = zs_rot.next()
            P.act(zs, pz, AF.Silu)
            P.dma("sp", self.zch[i], zs, owner=zs)
            pd = self.psA.next()
            for k in range(8):
                P.mm(pd[:, 0:16], htall[:, k, cs], wdk[:, k, 0:16], start=(k == 0), stop=(k == 7))
            P.tt("dve", DT[:, i, :], pd[:, 0:16], DTB, ALU.add)
        P.act(DT, DT, AF.Exp)
        P.act(DT, DT, AF.Ln, bias=1.0)
        for ti, (t0, w) in enumerate(TILES):
            ts_ = slice(t0, t0 + w)
            pk = self.psA.next()
            for k in range(8):
                P.mm(pk[0:96, :w], wdk[:, k, 16:112], htall[:, k, ts_], start=(k == 0), stop=(k == 7))
            P.copy("act", KRAB[0:96, ts_], pk[0:96, :w])
        for half in range(2):
            wx = self.load_w(Win[:, 1056 + 512 * half:1056 + 512 * half + 512], 1024, 512)
            for cc in range(4):
                c = half * 4 + cc
                dst = XSB[:, c, :] if c < 4 else BCB[:, c - 4, :]
                self.conv_chunk(wx, cc * 128, cw + c, cb + c, dst, htall, stg_rot, dg_rot)
        self.fence()
        wl = self.load_w(Win[:, 0:512], 1024, 512)
        rawl = [self.scr(2 * i, [128, 512], F32, "rawl%d" % i) for i in range(2)]
        sqs = [self.scr(4 + i, [128, 512], BF16, "sql%d" % i) for i in range(2)]
        rsl = self.scr(6, [128, 512], F32, "rsl")
        for ti, (t0, w) in enumerate(TILES):
            ts_ = slice(t0, t0 + w)
            for lat in range(2):
                dstT = CQT if lat == 0 else CKVT
                nwo = self.voff["ev_q_norm_w" if lat == 0 else "ev_kv_norm_w"] + e * 2
                ssp = self.psC.next()
                for c in range(2):
                    pl = self.psA.next()
                    for k in range(8):
                        P.mm(pl[:, :w], wl[:, k, (lat * 2 + c) * 128:(lat * 2 + c + 1) * 128], htall[:, k, ts_],
                             start=(k == 0), stop=(k == 7))
                    P.copy("act", rawl[c][:, :w], pl[:, :w])
                    P.act(sqs[c][:, :w], rawl[c][:, :w], AF.Square)
                    P.mm(ssp[:, :w], self.ones_bf, sqs[c][:, :w], start=(c == 0), stop=(c == 1))
                P.act(rsl[:, :w], ssp[:, :w], AF.Ln, bias=self.epsc, scale=1.0 / 256)
                P.act(rsl[:, :w], rsl[:, :w], AF.Exp, scale=-0.5)
                for c in range(2):
                    P.stt("dve", dstT[:, c, ts_], rawl[c][:, :w], self.VEC[:, nwo + c:nwo + c + 1], rsl[:, :w],
                          ALU.mult, ALU.mult)
        self.fence()
        if do_ssd:
            self.ssd_scan(l, e, XSB, BCB, DT, A_b, DSK, last)
            wo = self.load_w(d["ev_w_out"][e][512:1024, :], 512, 1024)
            self.out_proj(l, wo, 4, [XSB[:, c, :] for c in range(4)], tiles)
        self.fence()
        if do_attn:
            self.attention(l, e, CQT, CKVT, KRAB, last, tiles)

    def ssd_scan(self, l, e, XSB, BCB, DT, A_b, DSK, last):
        P = self.P
        d = self.d
        self.bump, self.bump_end = 60 * 1024, 96 * 1024
        SNAP = self.pbump([128, 18, 512], BF16, "SSNAP")
        RA = self.pbump([128, 8, 128], F32, "RA")
        WTt = self.pbump([128, 8, 128], BF16, "WTt")
        CBM = [self.pbump([128, 2, 128], BF16, "CBM%d" % i) for i in range(2)]
        XS_TM = self.pbump([128, 8, 64], BF16, "XS_TM")
        BM_TM = self.pbump([128, 2, 128], BF16, "BM_TM")
        XP = [self.pbump([128, 8, 64], BF16, "XP%d" % i) for i in range(2)]
        H = self.pbump([128, 8, 64], F32, "Hst")
        HBF = self.pbump([128, 512], BF16, "HBF")
        SNW = self.pbump([128, 512], F32, "SNW")
        GNb = self.pbump([128, 512], BF16, "GNb")
        TMP = self.scr(0, [128, 512], F32, "ssd_tmp")
        ACCs = [self.scr(2 + 2 * i, [128, 512], F32, "ssd_acc%d" % i) for i in range(2)]
        zin = Rot([self.scr(6 + i, [128, 512], BF16, "zin%d" % i) for i in range(2)])
        junk = GNb[:, 0:256]
        P.dma("sp", SNW, d["ev_ssm_norm_w"][e:e + 1, :].to_broadcast([128, 512]))

        sl = self.wslot()
        def slv(k):
            return sl.v(sl.ap[:, k * 576:(k + 1) * 576].bitcast(F32).rearrange("p (d c h) -> p d c h", d=2, c=18))
        DTA, NACa, EACa, TOEa, CDa, DTOE = slv(0), slv(1), slv(2), slv(3), slv(4), slv(5)
        for dd in range(2):
            P.tt("dve", DTA[:, dd], DT[:, :, dd * 8:dd * 8 + 8],
                 A_b[:, dd * 8:dd * 8 + 8].re("p (o h) -> p o h", o=1).bc([128, 18, 8]), ALU.mult)
        pa = self.psA.next()
        pb = self.psA.next()
        for dd in range(2):
            P.mm(pa[:, dd * 144:(dd + 1) * 144], self.tri[dd], DTA[:, dd].re("p c h -> p (c h)"))
        P.mm(pb[:, 0:288], self.ones_f, DTA.re("p d c h -> p (d c h)"))
        pav = pa[:, 0:288].re("p (d c h) -> p d c h", d=2, c=18)
        pbv = pb[:, 0:288].re("p (d c h) -> p d c h", d=2, c=18)
        P.act(NACa, pav, AF.Copy, scale=-1.0)
        P.act(EACa, pav, AF.Exp)
        P.act(CDa, pbv, AF.Exp)
        P.tt("dve", TOEa, pbv, NACa, ALU.add)
        P.act(TOEa, TOEa, AF.Exp)
        for dd in range(2):
            P.tt("dve", DTOE[:, dd], TOEa[:, dd], DT[:, :, dd * 8:dd * 8 + 8], ALU.mult)

        sl2 = self.wslot()
        RA2 = sl2.v(sl2.ap[:, 0:2048].bitcast(F32).rearrange("p (h t) -> p h t", h=8))
        WTt2 = sl2.v(sl2.ap[:, 2048:3072].rearrange("p (h t) -> p h t", h=8))
        RAs, WTts = [RA, RA2], [WTt, WTt2]

        def prep(dd, i, need_decay):
            cs = slice(i * 128, (i + 1) * 128)
            dt = DT[:, i, dd * 8:dd * 8 + 8]
            dtA, NAC, EAC, TOE, CD = (DTA[:, dd, i, :], NACa[:, dd, i, :], EACa[:, dd, i, :], TOEa[:, dd, i, :], CDa[:, dd, i, :])
            r = dict(dt=dt, dtA=dtA, NAC=NAC, EAC=EAC, TOE=TOE, CD=CD, cs=cs, DTOE=DTOE[:, dd, i, :])
            return r

        def decay_a(dd, pr_):
            RAd = RAs[dd]
            P.tt("pool", RAd, self.tri[dd].re("p (o t) -> p o t", o=1).bc([128, 8, 128]),
                 pr_["dtA"].re("p (h o) -> p h o", o=1).bc([128, 8, 128]), ALU.mult)
            pA = [self.psA.next(), self.psA.next()]
            for hf in range(2):
                P.mm(pA[hf], self.ones_f, RAd[:, 4 * hf:4 * hf + 4, :].re("p h t -> p (h t)"))
            return pA

        def decay_b(dd, pr_, pA):
            RAd = RAs[dd]
            for hf in range(2):
                P.tt("dve", RAd[:, 4 * hf:4 * hf + 4, :], pA[hf].re("p (h t) -> p h t", h=4),
                     pr_["NAC"][:, 4 * hf:4 * hf + 4].re("p (h o) -> p h o", o=1).bc([128, 4, 128]), ALU.add)
            P.act(RAd, RAd, AF.Relu, scale=-1.0)
            P.act(RAd, RAd, AF.Exp, scale=-1.0)

        def transposes(i, need_x=True):
            cs = slice(i * 128, (i + 1) * 128)
            pt = self.psA.next()
            ptb = pt.v(pt.ap.bitcast(BF16))
            for c in range(4):
                P.transpose(ptb[:, c * 128:(c + 1) * 128], XSB[:, c, cs], self.ident)
            P.copy("act", XS_TM.re("p h v -> p (h v)"), ptb[:, 0:512])
            pt2 = self.psA.next()
            ptb2 = pt2.v(pt2.ap.bitcast(BF16))
            for g in range(2):
                P.transpose(ptb2[:, g * 128:(g + 1) * 128], BCB[:, g, cs], self.ident)
            P.copy("act", BM_TM.re("p g n -> p (g n)"), ptb2[:, 0:256])

        def state_update(pr_, xp, XPP):
            P.tt("pool", XPP, XS_TM, pr_["DTOE"].re("p (h o) -> p h o", o=1).bc([128, 8, 64]), ALU.mult)
            ph = self.psB.next()
            for g in range(2):
                P.mm(ph[:, g * 256:(g + 1) * 256], BM_TM[:, g, :], XPP[:, 4 * g:4 * g + 4, :].re("p h v -> p (h v)"))
            P.tt("dve", H, H, pr_["CD"].re("p (h o) -> p h o", o=1).bc([128, 8, 64]), ALU.mult)
            P.tt("dve", H.re("p h v -> p (h v)"), H.re("p h v -> p (h v)"), ph, ALU.add)

        P.memset("dve", H, 0.0)
        orderA = [16, 17] + list(range(16))
        for n_, i in enumerate(orderA):
            P.copy("dve", SNAP[:, i, :], H.re("p h v -> p (h v)"))
            if n_ == len(orderA) - 1:
                break
            pr_ = prep(0, i, False)
            transposes(i)
            state_update(pr_, None, XP[1])
        P.memset("dve", H, 0.0)
        orderB = [17, 16] + list(range(15, -1, -1))

        def head(n_, i, need_out):
            cs = slice(i * 128, (i + 1) * 128)
            ACC = ACCs[n_ % 2]
            transposes(i)
            zt = None
            if need_out:
                zt = zin.next()
                P.dma("sp", zt, self.zch[i], owner=zt)
                P.copy("dve", HBF, H.re("p h v -> p (h v)"))
                pcb = self.psA.next()
                for g in range(2):
                    P.mm(pcb[:, g * 128:(g + 1) * 128], BCB[:, g, cs], BCB[:, 2 + g, cs])
                pcv = pcb[:, 0:256].re("p (g t) -> p g t", g=2)
                for dd in range(2):
                    P.tt("dve", CBM[dd], pcv, self.tri[dd].re("p (o t) -> p o t", o=1).bc([128, 2, 128]), ALU.mult)
                P.tt("pool", ACC.re("p (h v) -> p h v", h=8), XS_TM, DSK.re("p (h o) -> p h o", o=1).bc([128, 8, 64]), ALU.mult)
            prs = [prep(dd, i, need_out) for dd in range(2)]
            for dd in range(2):
                P.tt("pool", XP[dd], XS_TM, prs[dd]["dt"].re("p (h o) -> p h o", o=1).bc([128, 8, 64]), ALU.mult)
            if need_out:
                pAs = [decay_a(dd, prs[dd]) for dd in range(2)]
                for dd in range(2):
                    decay_b(dd, prs[dd], pAs[dd])
                pys = []
                for dd in range(2):
                    for g in range(2):
                        P.tt("dve", WTts[dd][:, 4 * g:4 * g + 4, :], RAs[dd][:, 4 * g:4 * g + 4, :],
                             CBM[dd][:, g:g + 1, :].bc([128, 4, 128]), ALU.mult)
                    py = self.psB.next()
                    for h in range(8):
                        P.mm(py[:, h * 64:(h + 1) * 64], WTts[dd][:, h, :], XP[dd][:, h, :])
                    pys.append(py)
                for dd in range(2):
                    pyi = self.psB.next() if dd == 0 else self.psC.next()
                    hsrc = SNAP[:, i, :] if dd == 0 else HBF
                    for g in range(2):
                        P.mm(pyi[:, g * 256:(g + 1) * 256], BCB[:, 2 + g, cs], hsrc[:, g * 256:(g + 1) * 256])
                    P.tt("dve", TMP.re("p (h v) -> p h v", h=8), pyi.re("p (h v) -> p h v", h=8),
                         prs[dd]["EAC"].re("p (h o) -> p h o", o=1).bc([128, 8, 64]), ALU.mult)
                    P.tt("dve", TMP, TMP, pys[dd], ALU.add)
                    P.tt("pool", ACC, ACC, TMP, ALU.add)
            if n_ < len(orderB) - 1:
                state_update(prs[1], XP[1], XP[0])
            if not need_out:
                return None

            def tail():
                Rr = self.stat.next()
                P.tt("pool", ACC, ACC, zt, ALU.mult)
                for g in range(2):
                    P.act(junk, ACC[:, g * 256:(g + 1) * 256], AF.Square, accum=Rr[:, g:g + 1])
                P.act(Rr[:, 2:4], Rr[:, 0:2], AF.Ln, bias=self.epsc, scale=1.0 / 256)
                P.act(Rr[:, 2:4], Rr[:, 2:4], AF.Exp, scale=-0.5)
                for g in range(2):
                    gs = slice(g * 256, (g + 1) * 256)
                    P.stt("dve", GNb[:, gs], ACC[:, gs], Rr[:, 2 + g:3 + g], SNW[:, gs], ALU.mult, ALU.mult)
                pg = self.psA.next()
                pgb = pg.v(pg.ap.bitcast(BF16))
                for c in range(4):
                    P.transpose(pgb[:, c * 128:(c + 1) * 128], GNb[:, c * 128:(c + 1) * 128], self.ident)
                P.copy("act", XSB[:, :, cs], pgb[:, 0:512].re("p (c t) -> p c t", c=4))
            return tail

        pending = None
        for n_, i in enumerate(orderB):
            need_out = not (last and i >= 16)
            t_ = head(n_, i, need_out)
            if pending is not None:
                pending()
            pending = t_
        if pending is not None:
            pending()

    def attention(self, l, e, CQT, CKVT, KRAB, last, tiles):
        P = self.P
        d = self.d
        SCALE = 96.0 ** -0.5
        CT = self.pr(0, [128, NT], F32, "ropeCT")
        ST = self.pr(9, [128, NT], F32, "ropeST")
        GTA = self.pr(18, [128, 4, NT], BF16, "GTA")
        self.bump, self.bump_end = 60 * 1024, 96 * 1024
        QTs = [self.pbump([128, NT], BF16, "QT%d" % i) for i in range(2)]
        KTs = [self.pbump([128, NT], BF16, "KT%d" % i) for i in range(2)]
        VA = [self.pbump([128, 18, 128], BF16, "VA%d" % i) for i in range(2)]
        pts = Rot([self.pbump([128, 512], BF16, "PT%d" % i) for i in range(5)])
        T1 = self.scr(0, [128, 512], F32, "aT1")
        T2 = self.scr(2, [128, 512], F32, "aT2")
        RD = self.scr(4, [128, 512], F32, "aRD")
        T2s = self.scr(6, [128, 512], F32, "aT2s")
        P.memset("dve", CT[0:64, :], 1.0)
        P.dma("sp", CT[64:96, :], d["ropeC"])
        P.dma("sp", ST[64:96, :], d["ropeS"])
        P.dma("sp", ST[0:32, :], d["ropeS"])
        P.memset("dve", ST[32:64, :], 0.0)
        P.memset("pool", VA[0][:, :, 64:128], 1.0)
        P.memset("pool", VA[1][:, :, 0:64], 1.0)
        s1 = self.wslot()
        WUQ = s1[:, 0:1536].re("p (k n) -> p k n", k=2)
        WUQP = s1[:, 1536:3072].re("p (k n) -> p k n", k=2)
        P.memset("pool", WUQP, 0.0)
        P.dma("pool", WUQ, d["ev_w_uq"][e].rearrange("(k p) n -> p k n", p=128))
        src5 = d["ev_w_uq"][e].rearrange("(k p) (h f) -> p k h f", p=128, f=96)
        dst5 = WUQP.re("p k (h f) -> p k h f", f=96)
        for k in range(2):
            for b_ in range(2):
                for hf in range(2):
                    so = 64 + 16 * b_ + 8 * (1 - hf)
                    do = 64 + 16 * b_ + 8 * hf
                    P.dma("pool", dst5[:, k, :, do:do + 8], src5[:, k, :, so:so + 8])
        WUKV = self.load_w(d["ev_w_ukv"][e], 256, 1024)
        for ti, (t0, w) in enumerate(TILES):
            ts_ = slice(t0, t0 + w)
            P.tt("dve", T2[0:32, :w], KRAB[0:32, ts_], ST[0:32, ts_], ALU.mult)
            P.copy("act", T2s[64:96, :w], T2[0:32, :w])
            P.tt("dve", T1[64:96, :w], KRAB[64:96, ts_], CT[64:96, ts_], ALU.mult)
            P.tt("pool", KTs[0][64:96, ts_], T1[64:96, :w], T2s[64:96, :w], ALU.add)
            P.tt("pool", KTs[1][64:96, ts_], T1[64:96, :w], T2s[64:96, :w], ALU.add)

        def prep(h):
            par = h % 2
            QT, KT = QTs[par], KTs[par]
            for ti, (t0, w) in enumerate(TILES):
                ts_ = slice(t0, t0 + w)
                pa_ = self.psA.next()
                pb_ = self.psA.next()
                for k in range(2):
                    P.mm(pa_[0:96, :w], WUQ[:, k, h * 96:(h + 1) * 96], CQT[:, k, ts_], start=(k == 0), stop=(k == 1))
                for k in range(2):
                    P.mm(pb_[0:96, :w], WUQP[:, k, h * 96:(h + 1) * 96], CQT[:, k, ts_], start=(k == 0), stop=(k == 1))
                P.tt("dve", T1[0:96, :w], pa_[0:96, :w], CT[0:96, ts_], ALU.mult)
                P.tt("dve", T2[0:96, :w], pb_[0:96, :w], ST[0:96, ts_], ALU.mult)
                P.tt("pool", QT[0:96, ts_], T1[0:96, :w], T2[0:96, :w], ALU.add)
                pk = self.psA.next()
                for k in range(2):
                    P.mm(pk[0:64, :w], WUKV[:, k, h * 128:h * 128 + 64], CKVT[:, k, ts_], start=(k == 0), stop=(k == 1))
                P.copy("dve", KT[0:64, ts_], pk[0:64, :w])
            va = VA[par]
            vo = 64 * par
            for i0 in range(0, NCH, 8):
                n = min(8, NCH - i0)
                pv = self.psA.next()
                for jj in range(n):
                    i = i0 + jj
                    for k in range(2):
                        P.mm(pv[:, jj * 64:(jj + 1) * 64], CKVT[:, k, i * 128:(i + 1) * 128],
                             WUKV[:, k, h * 128 + 64:h * 128 + 128], start=(k == 0), stop=(k == 1))
                P.copy("dve", va[:, i0:i0 + n, vo:vo + 64], pv[:, 0:n * 64].re("p (c v) -> p c v", c=n))

        def attend(h):
            par = h % 2
            QT, KT, va = QTs[par], KTs[par], VA[par]
            orow = slice(64 * par, 64 * par + 64)
            drow = slice(64 * (1 - par), 64 * (1 - par) + 64)
            for ti in tiles:
                t0, w = TILES[ti]
                ts_ = slice(t0, t0 + w)
                chunks = list(range(NCH)) if ti < 4 else [16, 17]
                po = self.psB.next()
                pend = []

                def do_pv(idx, i, ptile):
                    P.mm(po[:, :w], va[:, i, :], ptile[:, :w], start=(idx == 0), stop=(idx == len(chunks) - 1))

                for idx, i in enumerate(chunks):
                    ps_ = self.psA.next()
                    P.mm(ps_[:, :w], KT[0:96, i * 128:(i + 1) * 128], QT[0:96, ts_])
                    ptile = pts.next()
                    P.act(ptile[:, :w], ps_[:, :w], AF.Exp, scale=SCALE)
                    pend.append((idx, i, ptile))
                    if len(pend) > 2:
                        do_pv(*pend.pop(0))
                while pend:
                    do_pv(*pend.pop(0))
                P.recip(RD[orow, :w], po[drow, :w])
                P.tt("dve", GTA[orow, h // 2, ts_], po[orow, :w], RD[orow, :w], ALU.mult)

        self.ada.slot = self.wslot()
        prep(0)
        for h in range(8):
            if h + 1 < 8:
                prep(h + 1)
            self.ada.step()
            attend(h)
            if h < 4:
                self.ada.step()
        if self.ada.wv is not None:
            self.ada.step(load=False)
        self.ada.slot = None
        if self.stop == "attn_end":
            self.halt = True
            return
        wo = self.load_w(d["ev_w_out"][e][0:512, :], 512, 1024)
        self.out_proj(l, wo, 4, [GTA[:, c, :] for c in range(4)], tiles)

    def join(self, tlist, ap, name):
        b = Buf(name)
        ws = []
        for t in tlist:
            w = t.buf.w
            if w is None:
                continue
            ws.extend(w if isinstance(w, list) else [w])
        b.w = ws
        self.pr_bufs.append(b)
        return T(ap, b)

    def bcast_load(self, dst, dram_row_ap, n):
        self.P.dma("sp", dst, dram_row_ap.to_broadcast([128, n]))

    CONV_TILES = [(0, 508), (508, 1016), (1016, 1524), (1524, 2032), (2032, 2048), (2048, 2304)]

    def conv_chunk(self, wv, col0, vec_w_off, vec_b_off, dst, htall, stg_rot, dg_rot):
        P = self.P
        dg = dg_rot.next()
        for j in range(5):
            P.ts("pool", dg[:, j, :], self.identf, self.VEC[:, vec_w_off + 8 * j:vec_w_off + 8 * j + 1], None, ALU.mult)
        for (a, b) in self.CONV_TILES:
            s0, s1 = (0, NL) if a < NL else (NL, NT)
            ia, ib = max(a - 2, s0), min(b + 2, s1)
            w = b - a
            win = ib - ia
            j0 = ia - (a - 2)
            pt = self.psA.next()
            for k in range(8):
                P.mm(pt[:, 0:win], wv[:, k, col0:col0 + 128], htall[:, k, ia:ib], start=(k == 0), stop=(k == 7))
            stg = stg_rot.next()
            if j0 > 0:
                P.memset("pool", stg[:, 0:j0], 0.0)
            if j0 + win < w + 4:
                P.memset("pool", stg[:, j0 + win:w + 4], 0.0)
            P.copy("act", stg[:, j0:j0 + win], pt[:, 0:win])
            pc = self.psB.next()
            for j in range(5):
                P.mm(pc[:, 0:w], dg[:, j, :], stg[:, j:j + w], start=(j == 0), stop=(j == 4))
            P.act(dst[:, a:b], pc[:, 0:w], AF.Silu, bias=self.VEC[:, vec_b_off:vec_b_off + 1])

    def out_proj(self, l, wv, nk, gts, tiles):
        P = self.P
        for ti in tiles:
            t0, w = TILES[ti]
            v = 1 if ti == 4 else 0
            for dc in range(8):
                po = self.psB.next()
                for k in range(nk):
                    P.mm(po[:, :w], wv[:, k, dc * 128:(dc + 1) * 128], gts[k][:, t0:t0 + w],
                         start=(k == 0), stop=(k == nk - 1))
                P.stt("dve", self.xt[dc][ti], po[:, :w], self.MOD[l][:, 16 + dc, v:v + 1], self.xt[dc][ti],
                      ALU.mult, ALU.add)

    def odd_mixer(self, l):
        P = self.P
        d = self.d
        o = l // 2
        last = (l == self.depth - 1)
        tiles = [i for i in range(5) if not (last and i == 4)]
        self.norm_phase(l, 0)
        self.fence()
        htall = self.join([t for row in self.ht for t in row], self.HT.ap, "htall")
        Win = d["od_w_in"][o]
        self.bump, self.bump_end = 41 * 1024, 60 * 1024
        IGF = self.pbump([128, 18, 32], F32, "IGF")
        LFn = self.pbump([128, 2, 18, 8], F32, "LFn")
        BALL = self.pbump([128, 2, 18, 8], F32, "BALL")
        EC = self.pbump([128, 2, 18, 8], F32, "EC")
        THR = self.pbump([128, 2, 18, 8], F32, "THR")
        EBL = self.pbump([128, 2, 18, 8], F32, "EBL")
        WT = self.pbump([128, 2, 18, 8], F32, "WT")
        HNW = self.pbump([128, D], F32, "HNW")
        BIAS = self.pbump([128, 32], F32, "BIAS")
        MSK2 = self.pbump([128, 2, 128], F32, "MSK2")
        MSK = [MSK2[:, dd, :] for dd in range(2)]

        tsm = Rot([self.pbump([128, 2, 128], BF16, "sm%d" % i) for i in range(2)])
        tv2 = Rot([self.pbump([128, 2, 2, 129], BF16, "v2_%d" % i) for i in range(1)])
        tv3 = Rot([self.pbump([128, 2, 129], BF16, "v3_%d" % i) for i in range(1)])
        KTM = self.scr(7.5, [128, 128], BF16, "KTM")
        CBF = self.pbump([128, 129], BF16, "CBF")
        GN = self.pbump([128, 2, 128], BF16, "GN")
        CST = [self.pbump([128, 129], F32, "C_dir0")]
        T0 = self.scr(4.5, [128, 2, 128], F32, "T0")
        HS = self.scr(5.5, [128, 2, 128], F32, "HS")
        HOs = [self.scr(6.5 + 0.5 * i, [128, 256], BF16, "HO%d" % i) for i in range(2)]
        sg_rot = Rot([T0.re("p h v -> p (h v)"), HS.re("p h v -> p (h v)")])
        self.bcast_load(HNW, d["od_head_norm_w"][o:o + 1, :], D)
        P.dma("sp", BIAS[:, 0:16], d["od_i_bias"][o:o + 1, :].to_broadcast([128, 16]))
        P.dma("sp", BIAS[:, 16:32], d["od_f_bias"][o:o + 1, :].to_broadcast([128, 16]))
        for dd in range(2):
            P.ts("dve", MSK2[:, dd, :], self.tri[dd], 0.125, None, ALU.mult)
        stg_rot = Rot([self.scr(1.03125 * i, [128, 516], BF16, "stg%d" % i) for i in range(3)])
        dg_rot = Rot([self.scr(3.125, [128, 5, 128], BF16, "dg0")])
        accb = self.scr(0, [128, 512], F32, "accb")
        for j in range(4):
            QKT = self.pr(0, [128, 2, NT], BF16, "QKT")
            VA = self.pr(9, [128, 18, 2, 129], BF16, "VA")
            OG = self.pr(18.25, [128, 18, 256], BF16, "OG")
            SNAP = self.pr(27.25, [128, 18, 129], BF16, "SNAP")
            GTp = self.pr(32, [128, 2, NT], BF16, "GTp")
            slotA = self.wslot()
            wA = slotA[:, 0:4096].re("p (k n) -> p k n", k=8)
            P.dma("pool", wA[:, :, 0:128], Win[:, 128 * j:128 * j + 128].rearrange("(k p) n -> p k n", p=128))
            P.dma("pool", wA[:, :, 128:256], Win[:, 512 + 128 * j:512 + 128 * j + 128].rearrange("(k p) n -> p k n", p=128))
            P.dma("pool", wA[:, :, 256:512], Win[:, 1024 + 256 * j:1024 + 256 * j + 256].rearrange("(k p) n -> p k n", p=128))
            slotB = self.wslot()
            wB = slotB[:, 0:8 * 288].re("p (k n) -> p k n", k=8)
            P.dma("pool", wB[:, :, 0:256], Win[:, 2048 + 256 * j:2048 + 256 * j + 256].rearrange("(k p) n -> p k n", p=128))
            if j == 0:
                P.dma("pool", wB[:, :, 256:288], Win[:, 3072:3104].rearrange("(k p) n -> p k n", p=128))
            P.memset("pool", VA[:, :, :, 128:129], 1.0)
            if self.stop == "wload":
                continue
            cw = self.voff["od_conv_w"] + o * 40
            cb = self.voff["od_conv_b"] + o * 8
            self.conv_chunk(wA, 0, cw + j, cb + j, QKT[:, 0, :], htall, stg_rot, dg_rot)
            self.conv_chunk(wA, 128, cw + 4 + j, cb + 4 + j, QKT[:, 1, :], htall, stg_rot, dg_rot)
            if self.stop == "conv":
                continue
            for i in range(NCH):
                pt = self.psA.next()
                for k in range(8):
                    P.mm(pt[:, 0:256], htall[:, k, i * 128:(i + 1) * 128], wA[:, k, 256:512], start=(k == 0), stop=(k == 7))
                for k in range(8):
                    P.mm(pt[:, 256:512], htall[:, k, i * 128:(i + 1) * 128], wB[:, k, 0:256], start=(k == 0), stop=(k == 7))
                P.copy("act", VA[:, i, :, 0:128], pt[:, 0:256].re("p (h v) -> p h v", h=2))
                sgt = sg_rot.next()
                P.act(sgt, pt[:, 256:512], AF.Exp, scale=-1.0)
                P.act(sgt, sgt, AF.Ln, bias=1.0)
                P.act(OG[:, i, :], sgt, AF.Exp, scale=-1.0)
                if j == 0 and self.stop != "vo_nogate":
                    pg = self.psA.next()
                    for k in range(8):
                        P.mm(pg[:, 0:32], htall[:, k, i * 128:(i + 1) * 128], wB[:, k, 256:288], start=(k == 0), stop=(k == 7))
                    P.tt("dve", IGF[:, i, :], pg[:, 0:32], BIAS, ALU.add)
            if self.stop in ("vo", "vo_nogate"):
                continue
            if j == 0:
                FGv = IGF[:, :, 16:32].re("p c (d h) -> p d c h", d=2)
                IGv = IGF[:, :, 0:16].re("p c (d h) -> p d c h", d=2)
                P.act(LFn, FGv, AF.Exp, scale=-1.0)
                P.act(LFn, LFn, AF.Ln, bias=1.0)
                P.ts("dve", LFn, LFn, -1.0, None, ALU.mult)
                pb_ = self.psA.next()
                pbv = pb_[:, 0:288].re("p (d c h) -> p d c h", c=18, d=2)
                for dd in range(2):
                    P.mm(pb_[:, dd * 144:(dd + 1) * 144], self.tri[dd], LFn[:, dd].re("p c h -> p (c h)"))
                P.copy("dve", BALL, pbv)
                pe_ = self.psA.next()
                P.mm(pe_[:, 0:288], self.ones_f, LFn.re("p d c h -> p (d c h)"))
                P.act(EBL, pe_[:, 0:288].re("p (d c h) -> p d c h", c=18, d=2), AF.Exp)
                P.tt("dve", EC, IGv, BALL, ALU.subtract)
                P.act(EC, EC, AF.Exp)
                P.act(THR, BALL, AF.Exp, scale=-1.0)
                P.tt("dve", WT, EC, EBL, ALU.mult)
            h0 = 2 * j
            if self.stop == "inproj":
                continue

            def state_update(Cst, i, dd):
                ptk = self.psA.next()
                ptkb = ptk.v(ptk.ap.bitcast(BF16))
                P.transpose(ptkb[:, 0:128], QKT[:, 1, i * 128:(i + 1) * 128], self.ident)
                P.copy("act", KTM, ptkb[:, 0:128])
                v3 = tv3.next()
                P.tt("pool", v3, VA[:, i], WT[:, dd, i, h0:h0 + 2].re("p (h o) -> p h o", o=1).bc([128, 2, 129]), ALU.mult)
                pd = self.psA.next()
                pdv = pd[:, 0:258].re("p (h n) -> p h n", h=2)
                for h in range(2):
                    P.mm(pdv[:, h, :], KTM, v3[:, h, :])
                for h in range(2):
                    r = slice(64 * h, 64 * h + 64)
                    P.stt("dve", Cst[r, :], Cst[r, :], EBL[r, dd, i, h0 + h:h0 + h + 1], pdv[r, h, :], ALU.mult, ALU.add)

            Cst = CST[0]
            P.memset("dve", Cst, 0.0)
            orderA = [16, 17] + list(range(16))
            for n_, i in enumerate(orderA):
                P.ts("dve", SNAP[:, i, :], Cst, 0.125, None, ALU.mult)
                if n_ % 4 == 2:
                    self.ada.step(load=(n_ < 14))
                if n_ < len(orderA) - 1:
                    state_update(Cst, i, 0)
            if self.stop == "passA":
                continue
            P.memset("dve", Cst, 0.0)
            orderB = [17, 16] + list(range(15, -1, -1))
            HSs = [HS, accb[:, 0:256].re("p (h v) -> p h v", h=2)]
            junk2 = accb[:, 256:384]
            TP = accb.v(accb.ap[:, 384:512].bitcast(BF16))

            def lockstep(*gens):
                gens = [g for g in gens if g is not None]
                while gens:
                    for g in list(gens):
                        try:
                            next(g)
                        except StopIteration:
                            gens.remove(g)

            def head(n_, i, out):
                cs = slice(i * 128, (i + 1) * 128)
                HSc = HSs[n_ % 2]
                HO = HOs[n_ % 2]
                pst = []
                for h in range(2):
                    r = slice(64 * h, 64 * h + 64)
                    pb1 = self.psA.next()
                    P.mm(pb1[:, 0:128], QKT[r, 1, cs], QKT[r, 0, cs])
                    pst.append(pb1)
                P.ts("dve", CBF, Cst, 0.125, None, ALU.mult)
                yield
                Rr = self.stat.next()
                smh = [tsm.next(), tsm.next()]
                for h in range(2):
                    P.tt("dve", smh[h], pst[h][:, 0:128].re("p (o t) -> p o t", o=1).bc([128, 2, 128]), MSK2, ALU.mult)
                    yield
                v2b = tv2.next()
                P.tt("pool", v2b, VA[:, i].re("p (o h) n -> p o h n", o=1).bc([128, 2, 2, 129]),
                     EC[:, :, i, h0:h0 + 2].re("p d (h o) -> p d h o", o=1).bc([128, 2, 2, 129]), ALU.mult)
                yield
                P.tt("pool", HO, OG[:, i, :], HNW[:, h0 * 128:(h0 + 2) * 128], ALU.mult)
                yield
                poss = [[None, None], [None, None]]
                for h in range(2):
                    r = slice(64 * h, 64 * h + 64)
                    for dd in range(2):
                        po = self.psB2[h].next()
                        cb_ = SNAP[:, i, :] if dd == 0 else CBF
                        P.mm(po[:, 0:129], smh[h][:, dd, :], v2b[:, dd, h, :], start=True, stop=False)
                        P.mm(po[:, 0:129], QKT[r, 0, cs], cb_[r, :], start=False, stop=True)
                        yield
                        c = 2 * dd + h
                        P.act(Rr[:, c:c + 1], po[:, 128:129], AF.Abs)
                        yield
                        poss[h][dd] = po
                P.tt("dve", Rr[:, 0:4].re("p (d h) -> p d h", d=2), Rr[:, 0:4].re("p (d h) -> p d h", d=2),
                     THR[:, :, i, h0:h0 + 2], ALU.max)
                yield
                P.recip(Rr[:, 0:4], Rr[:, 0:4])
                yield
                for h in range(2):
                    P.act(T0[:, h, :], poss[h][0][:, 0:128], AF.Identity, scale=Rr[:, h:h + 1])
                    yield
                for h in range(2):
                    P.stt("dve", HSc[:, h, :], poss[h][1][:, 0:128], Rr[:, 2 + h:3 + h], T0[:, h, :], ALU.mult, ALU.add)
                    yield

                def tail():
                    for h in range(2):
                        P.act(junk2, HSc[:, h, :], AF.Square, accum=Rr[:, 4 + h:5 + h])
                        yield
                    P.act(Rr[:, 6:8], Rr[:, 4:6], AF.Ln, bias=self.epsc, scale=1.0 / 128)
                    yield
                    P.act(Rr[:, 6:8], Rr[:, 6:8], AF.Exp, scale=-0.5)
                    yield
                    P.tt("pool", TP.re("p (h v) -> p h v", h=2), HSc, Rr[:, 6:8].re("p (h o) -> p h o", o=1).bc([128, 2, 128]), ALU.mult)
                    yield
                    P.tt("pool", GN.re("p h v -> p (h v)"), TP, HO, ALU.mult)
                    yield
                    ptg = self.psA.next()
                    ptgb = ptg.v(ptg.ap.bitcast(BF16))
                    for h in range(2):
                        P.transpose(ptgb[:, h * 128:(h + 1) * 128], GN[:, h, :], self.ident)
                    yield
                    P.copy("act", GTp[:, :, cs], ptgb[:, 0:256].re("p (h t) -> p h t", h=2))
                    yield
                out.append(tail)

            self.psB2 = [Rot(self.banks[4:6]), Rot(self.banks[6:8])]
            pending = None
            for n_, i in enumerate(orderB):
                need_out = not (last and i >= 16)
                out = []
                hg = head(n_, i, out) if need_out else None
                lockstep(hg, pending() if pending is not None else None)
                pending = out[0] if out else None
                if n_ < len(orderB) - 1:
                    state_update(Cst, i, 1)
            if pending is not None:
                lockstep(pending())
            if self.stop in ("passB", "B1", "B2", "B3"):
                continue
            wo = self.load_w(d["od_w_out"][o][256 * j:256 * j + 256, :], 256, 1024)
            self.out_proj(l, wo, 2, [GTp[:, 0, :], GTp[:, 1, :]], tiles)

    def epilogue(self):
        P = self.P
        self.fence()
        ostg = Rot([self.pr(4 * i, [128, D], F32, "ostg%d" % i) for i in range(2)])
        junk = self.pr(8, [128, 512], BF16, "junk")
        self.FNW = self.pr(12, [128, D], F32, "FNW")
        P.dma("sp", self.FNW, self.d["final_norm_w"].to_broadcast([128, D]))
        toks = []
        for t in range(16):
            ti, j = divmod(t, 4)
            pa = self.psA.next()
            pb = self.psA.next()
            for c in range(8):
                dst = pa if c < 4 else pb
                cc = c % 4
                P.transpose(dst[:, cc * 128:(cc + 1) * 128], self.xt[c][ti][:, j * 128:(j + 1) * 128], self.identf)
            st = self.stat.next()
            P.act(junk, pa, AF.Square, accum=st[:, 0:1])
            P.act(junk, pb, AF.Square, accum=st[:, 1:2])
            P.tt("dve", st[:, 2:3], st[:, 0:1], st[:, 1:2], ALU.add)
            P.act(st[:, 3:4], st[:, 2:3], AF.Ln, bias=self.epsc, scale=1.0 / D)
            P.act(st[:, 4:5], st[:, 3:4], AF.Exp, scale=-0.5)
            o = ostg.next()
            P.stt("dve", o[:, 0:512], pa, st[:, 4:5], self.FNW[:, 0:512], ALU.mult, ALU.mult)
            P.stt("dve", o[:, 512:1024], pb, st[:, 4:5], self.FNW[:, 512:1024], ALU.mult, ALU.mult)
            toks.append(P.dma("sp", self.out[t * 128:(t + 1) * 128, :], o))
        P.wait_all_dma("sp", toks)


def host_consts():
    p = np.arange(128)
    ident = np.eye(128, dtype=np.float32)
    tri0 = (p[:, None] <= p[None, :]).astype(np.float32)
    tri1 = (p[:, None] >= p[None, :]).astype(np.float32)
    neg0 = np.where(p[:, None] <= p[None, :], 0.0, -30000.0).astype(np.float32)
    neg1 = np.where(p[:, None] >= p[None, :], 0.0, -30000.0).astype(np.float32)
    cst = np.concatenate([ident, tri0, tri1, neg0, neg1], axis=1)
    pos = np.arange(NL)
    row = (pos // 64).astype(np.float32)
    col = (pos % 64).astype(np.float32)
    half = 16
    inv = (1.0 / (10000.0 ** (np.arange(0, half, 2, dtype=np.float32) / half))).astype(np.float32)
    ar = row[None, :] * inv[:, None]
    ac = col[None, :] * inv[:, None]
    C = np.ones((32, NT), np.float32)
    S = np.zeros((32, NT), np.float32)
    C[0:8, :NL] = np.cos(ar); C[8:16, :NL] = np.cos(ar); C[16:24, :NL] = np.cos(ac); C[24:32, :NL] = np.cos(ac)
    S[0:8, :NL] = -np.sin(ar); S[8:16, :NL] = np.sin(ar); S[16:24, :NL] = -np.sin(ac); S[24:32, :NL] = np.sin(ac)
    import os
    if os.environ.get("NOROPE"):
        C[:] = 1.0
        S[:] = 0.0
    return cst, C, S


_NC_CACHE = {}


def make_in_maps(inputs, n):
    cst, C, S = host_consts()
    f = lambda a: np.ascontiguousarray(np.asarray(a, dtype=np.float32))
    shared = {
        "c_ctx": f(inputs["c_ctx"]).reshape(8, 128),
        "ada_w": f(inputs["ada_w"]), "ada_b": f(inputs["ada_b"]).reshape(192, 128),
        "norm_mix_w": f(inputs["norm_mix_w"]).reshape(32, 128), "norm_mlp_w": f(inputs["norm_mlp_w"]).reshape(32, 128),
        "mlp_w1": f(inputs["mlp_w1"]), "mlp_w2": f(inputs["mlp_w2"]),
        "ev_w_in": f(inputs["ev_w_in"]), "ev_q_norm_w": f(inputs["ev_q_norm_w"]).reshape(4, 128),
        "ev_w_uq": f(inputs["ev_w_uq"]), "ev_kv_norm_w": f(inputs["ev_kv_norm_w"]).reshape(4, 128),
        "ev_w_ukv": f(inputs["ev_w_ukv"]), "ev_conv_w": f(inputs["ev_conv_w"]).reshape(80, 128),
        "ev_conv_b": f(inputs["ev_conv_b"]).reshape(16, 128), "ev_dt_bias": f(inputs["ev_dt_bias"]).reshape(2, 16),
        "ev_a_log": f(inputs["ev_a_log"]).reshape(2, 16), "ev_d_skip": f(inputs["ev_d_skip"]),
        "ev_ssm_norm_w": f(inputs["ev_ssm_norm_w"]), "ev_w_out": f(inputs["ev_w_out"]),
        "od_w_in": f(inputs["od_w_in"]), "od_conv_w": f(inputs["od_conv_w"]).reshape(80, 128),
        "od_conv_b": f(inputs["od_conv_b"]).reshape(16, 128), "od_i_bias": f(inputs["od_i_bias"]).reshape(2, 16),
        "od_f_bias": f(inputs["od_f_bias"]).reshape(2, 16), "od_head_norm_w": f(inputs["od_head_norm_w"]),
        "od_w_out": f(inputs["od_w_out"]), "final_norm_w": f(inputs["final_norm_w"]).reshape(1, D),
        "cst": cst, "ropeC": C, "ropeS": S,
    }
    x = f(inputs["x"]); c = f(inputs["c"]); ctx = f(inputs["ctx"])
    maps = []
    for b in range(n):
        m = dict(shared)
        m["x"] = x[b]
        m["ctx"] = ctx[b]
        m["c"] = c[b].reshape(8, 128)
        maps.append(m)
    return maps


def kernel(**inputs):
    n = 8
    key = "full"
    if key not in _NC_CACHE:
        mk = MK()
        _NC_CACHE[key] = (mk, mk.P.build())
    mk, nc = _NC_CACHE[key]
    maps = make_in_maps(inputs, n)
    used = set(mk.d.keys())
    maps = [{k: v for k, v in m.items() if k in used} for m in maps]
    res = run_bass_kernel_spmd(nc, maps, core_ids=list(range(n)))
    return np.stack([r["out"] for r in res.results], axis=0).astype(np.float32)
```

```python
import bisect
from contextlib import ExitStack

import numpy as np
import concourse.bass as bass
import concourse.mybir as mybir
from concourse.bass_utils import run_bass_kernel_spmd

F32 = mybir.dt.float32
BF16 = mybir.dt.bfloat16
AF = mybir.ActivationFunctionType
ALU = mybir.AluOpType
AX = mybir.AxisListType

COMPUTE = ("pe", "act", "dve", "pool")
QUEUES = ("pe", "act", "dve", "pool", "sp")


class Buf:
    __slots__ = ("name", "w", "r", "dkey", "dcount", "psum")

    def __init__(self, name):
        self.name = name
        self.psum = False
        self.w = None
        self.r = []
        self.dkey = None
        self.dcount = 0


class T:
    __slots__ = ("ap", "buf")

    def __init__(self, ap, buf):
        self.ap = ap
        self.buf = buf

    def __getitem__(self, key):
        return T(self.ap[key], self.buf)

    def v(self, ap):
        return T(ap, self.buf)

    def sub(self, name, key):
        return T(self.ap[key], Buf(name))

    def bc(self, shape):
        return T(self.ap.to_broadcast(shape), self.buf)

    def re(self, s, **kw):
        return T(self.ap.rearrange(s, **kw), self.buf)


class Op:
    __slots__ = ("fn", "waits", "signal", "dma", "idx")

    def __init__(self, fn, waits, dma=None):
        self.fn = fn
        self.waits = waits
        self.signal = False
        self.dma = dma
        self.idx = 0


class Prog:
    def __init__(self):
        self.nc = bass.Bass("TRN2", target_bir_lowering=False)
        self.es = ExitStack()
        self.ops = {q: [] for q in QUEUES}
        self.count = {q: 0 for q in COMPUTE}
        self.known = {q: {} for q in QUEUES}
        self.snaps = {q: [(0, {})] for q in COMPUTE}
        self.dsnap = {}
        self.ndma = 0
        self.sbuf_used = 0
        self.psum_i = 0
        self.psum = []

    def dram(self, name, shape, dtype, kind):
        return self.nc.dram_tensor(name, list(shape), dtype, kind=kind).ap()

    def tile(self, name, shape, dtype):
        t = self.es.enter_context(self.nc.sbuf_tensor(name, list(shape), dtype))
        per = int(np.prod(shape[1:])) * (4 if dtype == F32 else 2)
        self.sbuf_used += per
        return T(t[:] if not isinstance(t, bass.AP) else t, Buf(name))

    def psum_tile(self, name, shape, dtype):
        t = self.es.enter_context(self.nc.psum_tensor(name, list(shape), dtype))
        b = Buf(name)
        b.psum = True
        return T(t[:] if not isinstance(t, bass.AP) else t, b)

    def _snapshot(self, key, count):
        if isinstance(key, str):
            sn = self.snaps[key]
            i = bisect.bisect_right(sn, count, key=lambda e: e[0]) - 1
            return sn[i][1]
        return self.dsnap.get((key, count), {})

    def _resolve(self, q, reads, writes):
        need = {}

        def add(tok, is_write_dep_on_write=False):
            if tok is None:
                return
            k, c = tok
            if need.get(k, 0) < c:
                need[k] = c

        def addw(w, pe_ok):
            if w is None:
                return
            if isinstance(w, list):
                for x in w:
                    add(x)
            elif not (pe_ok and w[0] == "pe"):
                add(w)

        for b in reads:
            addw(b.w, False)
            if b.psum:
                for tok in b.r:
                    if tok[0] != q:
                        add(tok)
        for b in writes:
            addw(b.w, q == "pe")
            for tok in b.r:
                add(tok)
        known = self.known[q]
        waits = []
        changed = False
        for k, c in need.items():
            if known.get(k, 0) >= c:
                continue
            waits.append((k, c))
        if waits:
            known = dict(known)
            for k, c in waits:
                snap = self._snapshot(k, c)
                for kk, cc in snap.items():
                    if known.get(kk, 0) < cc:
                        known[kk] = cc
                if known.get(k, 0) < c:
                    known[k] = c
            self.known[q] = known
            changed = True
        return waits, changed

    def op(self, q, fn, reads=(), writes=()):
        rb = [t.buf for t in reads if isinstance(t, T)]
        wb = [t.buf for t in writes if isinstance(t, T)]
        waits, changed = self._resolve(q, rb, wb)
        self.count[q] += 1
        idx = self.count[q]
        if changed:
            self.snaps[q].append((idx, self.known[q]))
        o = Op(fn, waits)
        o.idx = idx
        self.ops[q].append(o)
        tok = (q, idx)
        for b in rb:
            b.r.append(tok)
        for b in wb:
            b.w = tok
            b.r = []
        return o

    def dma(self, q, out, in_, owner=None, **kw):
        reads = [in_] if isinstance(in_, T) else []
        writes = [out] if isinstance(out, T) else []
        own = owner if owner is not None else (out if isinstance(out, T) else in_)
        ob = own.buf
        rb = [t.buf for t in reads]
        wb = [t.buf for t in writes]
        waits, changed = self._resolve(q, rb, wb)
        if q in COMPUTE:
            pass
        if ob.dkey is None:
            self.ndma += 1
            ob.dkey = self.ndma
        ob.dcount += 1
        tok = (ob.dkey, ob.dcount)
        self.dsnap[tok] = self.known[q]
        oap = out.ap if isinstance(out, T) else out
        iap = in_.ap if isinstance(in_, T) else in_
        o = Op(lambda e: e.dma_start(out=oap, in_=iap, **kw), waits, dma=ob.dkey)
        self.ops[q].append(o)
        for b in rb:
            b.r.append(tok)
        for b in wb:
            b.w = tok
            b.r = []
        return tok

    def wait_all_dma(self, q, toks):
        o = Op(None, list(toks))
        self.ops[q].append(o)

    def build(self):
        nc = self.nc
        sig = {q: set() for q in COMPUTE}
        for q in QUEUES:
            for o in self.ops[q]:
                for k, c in o.waits:
                    if isinstance(k, str):
                        sig[k].add(c)
        sigmap = {}
        for q in COMPUTE:
            s = sorted(sig[q])
            sigmap[q] = {c: i + 1 for i, c in enumerate(s)}
        for q in COMPUTE:
            for o in self.ops[q]:
                if o.dma is None and o.fn is not None and o.idx in sigmap[q]:
                    o.signal = True
        sems = {q: self.es.enter_context(nc.semaphore("s_" + q)) for q in COMPUTE}
        dsems = {k: self.es.enter_context(nc.semaphore("d%d" % k)) for k in range(1, self.ndma + 1)}
        self.nsig = {q: len(sigmap[q]) for q in COMPUTE}

        def emit(q, eng):
            for o in self.ops[q]:
                for k, c in o.waits:
                    if isinstance(k, str):
                        eng.wait_ge(sems[k], sigmap[k][c])
                    else:
                        eng.wait_ge(dsems[k], 16 * c)
                if o.fn is None:
                    continue
                ins = o.fn(eng)
                if o.dma is not None:
                    ins.then_inc(dsems[o.dma], 16)
                elif o.signal:
                    ins.then_inc(sems[q], 1)

        with nc.Block() as block:
            @block.tensor
            def _(e):
                emit("pe", e)

            @block.scalar
            def _(e):
                emit("act", e)

            @block.vector
            def _(e):
                emit("dve", e)

            @block.gpsimd
            def _(e):
                emit("pool", e)

            @block.sync
            def _(e):
                emit("sp", e)
        self.es.close()
        return nc

    @staticmethod
    def _a(x):
        return x.ap if isinstance(x, T) else x

    def mm(self, out, lhsT, rhs, start=True, stop=True):
        o, l, r = out.ap, lhsT.ap, rhs.ap
        return self.op("pe", lambda e: e.matmul(o, l, r, start=start, stop=stop),
                       reads=(lhsT, rhs), writes=(out,))

    def transpose(self, out, in_, ident):
        o, i, d = out.ap, in_.ap, ident.ap
        return self.op("pe", lambda e: e.transpose(o, i, d), reads=(in_, ident), writes=(out,))

    def act(self, out, in_, func, bias=0.0, scale=1.0, accum=None, q="act"):
        o, i = out.ap, in_.ap
        b, s = self._a(bias), self._a(scale)
        kw = {}
        if accum is not None:
            kw["accum_out"] = accum.ap
        wr = (out,) if accum is None else (out, accum)
        return self.op(q, lambda e: e.activation(o, i, func, bias=b, scale=s, **kw),
                       reads=(in_, bias, scale), writes=wr)

    def tt(self, q, out, in0, in1, op):
        o, a, b = out.ap, in0.ap, in1.ap
        return self.op(q, lambda e: e.tensor_tensor(o, a, b, op), reads=(in0, in1), writes=(out,))

    def ts(self, q, out, in0, s1, s2, op0, op1=None, accum=None):
        o, a = out.ap, in0.ap
        x1, x2 = self._a(s1), self._a(s2)
        kw = {}
        if op1 is not None:
            kw["op1"] = op1
        if accum is not None:
            kw["accum_out"] = accum.ap
        wr = (out,) if accum is None else (out, accum)
        return self.op(q, lambda e: e.tensor_scalar(o, a, x1, x2, op0, **kw),
                       reads=(in0, s1, s2), writes=wr)

    def stt(self, q, out, in0, scalar, in1, op0, op1):
        o, a, b = out.ap, in0.ap, in1.ap
        s = self._a(scalar)
        return self.op(q, lambda e: e.scalar_tensor_tensor(o, a, s, b, op0, op1),
                       reads=(in0, scalar, in1), writes=(out,))

    def copy(self, q, out, in_):
        o, i = out.ap, in_.ap
        if q == "act":
            return self.op(q, lambda e: e.copy(o, i), reads=(in_,), writes=(out,))
        return self.op(q, lambda e: e.tensor_copy(o, i), reads=(in_,), writes=(out,))

    def memset(self, q, out, val):
        o = out.ap
        return self.op(q, lambda e: e.memset(o, val), writes=(out,))

    def reduce(self, q, out, in_, op, axis=AX.X):
        o, i = out.ap, in_.ap
        return self.op(q, lambda e: e.tensor_reduce(o, i, axis, op), reads=(in_,), writes=(out,))

    def recip(self, out, in_):
        o, i = out.ap, in_.ap
        return self.op("dve", lambda e: e.reciprocal(o, i), reads=(in_,), writes=(out,))
import ml_dtypes

D = 1024
NL = 2048
NCX = 256
NT = NL + NCX
EPS = 1e-6
TILES = [(0, 512), (512, 512), (1024, 512), (1536, 512), (2048, 256)]
NCH = 18


class Rot:
    def __init__(self, items):
        self.items = items
        self.i = 0

    def next(self):
        t = self.items[self.i % len(self.items)]
        self.i += 1
        return t


class MK:
    def __init__(self, depth=4, mixers=("even", "odd"), dbg=None, stop=None):
        self.stop = stop
        self.depth = depth
        self.mixers = mixers
        P = self.P = Prog()
        self.dbg = dbg
        self.declare_io()
        self.alloc()
        self.prologue()
        for l in range(depth):
            self.layer(l)
        self.epilogue()

    def declare_io(self):
        P = self.P
        I = "ExternalInput"
        d = {}
        d["x"] = P.dram("x", [NL, D], F32, I)
        d["ctx"] = P.dram("ctx", [NCX, D], F32, I)
        d["c"] = P.dram("c", [8, 128], F32, I)
        d["c_ctx"] = P.dram("c_ctx", [8, 128], F32, I)
        d["ada_w"] = P.dram("ada_w", [4, D, 6 * D], F32, I)
        d["ada_b"] = P.dram("ada_b", [4 * 48, 128], F32, I)
        d["norm_mix_w"] = P.dram("norm_mix_w", [32, 128], F32, I)
        d["norm_mlp_w"] = P.dram("norm_mlp_w", [32, 128], F32, I)
        d["mlp_w1"] = P.dram("mlp_w1", [4, D, 4 * D], F32, I)
        d["mlp_w2"] = P.dram("mlp_w2", [4, 4 * D, D], F32, I)
        d["ev_w_in"] = P.dram("ev_w_in", [2, D, 2096], F32, I)
        d["ev_q_norm_w"] = P.dram("ev_q_norm_w", [4, 128], F32, I)
        d["ev_w_uq"] = P.dram("ev_w_uq", [2, 256, 768], F32, I)
        d["ev_kv_norm_w"] = P.dram("ev_kv_norm_w", [4, 128], F32, I)
        d["ev_w_ukv"] = P.dram("ev_w_ukv", [2, 256, 1024], F32, I)
        d["ev_conv_w"] = P.dram("ev_conv_w", [80, 128], F32, I)
        d["ev_conv_b"] = P.dram("ev_conv_b", [16, 128], F32, I)
        d["ev_dt_bias"] = P.dram("ev_dt_bias", [2, 16], F32, I)
        d["ev_a_log"] = P.dram("ev_a_log", [2, 16], F32, I)
        d["ev_d_skip"] = P.dram("ev_d_skip", [2, 8], F32, I)
        d["ev_ssm_norm_w"] = P.dram("ev_ssm_norm_w", [2, 512], F32, I)
        d["ev_w_out"] = P.dram("ev_w_out", [2, D, D], F32, I)
        d["od_w_in"] = P.dram("od_w_in", [2, D, 3104], F32, I)
        d["od_conv_w"] = P.dram("od_conv_w", [80, 128], F32, I)
        d["od_conv_b"] = P.dram("od_conv_b", [16, 128], F32, I)
        d["od_i_bias"] = P.dram("od_i_bias", [2, 16], F32, I)
        d["od_f_bias"] = P.dram("od_f_bias", [2, 16], F32, I)
        d["od_head_norm_w"] = P.dram("od_head_norm_w", [2, D], F32, I)
        d["od_w_out"] = P.dram("od_w_out", [2, D, D], F32, I)
        d["final_norm_w"] = P.dram("final_norm_w", [1, D], F32, I)
        d["cst"] = P.dram("cst", [128, 5 * 128], F32, I)
        d["ropeC"] = P.dram("ropeC", [32, NT], F32, I)
        d["ropeS"] = P.dram("ropeS", [32, NT], F32, I)
        self.d = d
        self.out = P.dram("out", [NL, D], F32, "ExternalOutput")
        zs = P.nc.dram_tensor("zscr", [NT, 512], BF16, kind="Internal").ap()
        self.zscr = zs
        self.zch = [T(zs[i * 128:(i + 1) * 128, :], Buf("zch%d" % i)) for i in range(NCH)]

    def alloc(self):
        P = self.P
        self.XT = P.tile("XT", [128, 8, NT], F32)
        self.xt = [[self.XT.sub("xt%d_%d" % (c, i), (slice(None), c, slice(t0, t0 + w)))
                    for i, (t0, w) in enumerate(TILES)] for c in range(8)]
        self.PR = P.tile("PR", [128, 49152], BF16)
        self.pr_bufs = []
        self.pr_fence = []
        self.arena = [P.tile("wa%d" % i, [128, 4096], BF16) for i in range(3)]
        self.arot = Rot(self.arena)
        self.SCR = P.tile("SCR", [128, 4096], BF16)
        self.cst = P.tile("cst_sb", [128, 5 * 128], F32)
        self.identf = self.cst[:, 0:128]
        self.tri = [self.cst[:, 128:256], self.cst[:, 256:384]]
        self.neg = [self.cst[:, 384:512], self.cst[:, 512:640]]
        self.ident = P.tile("ident", [128, 128], BF16)
        self.ones_bf = P.tile("ones_bf", [128, 128], BF16)
        self.ones_f = P.tile("ones_f", [128, 128], F32)
        self.epsc = P.tile("epsc", [128, 1], F32)
        self.VEC = P.tile("VEC", [128, 512], F32)
        self.SV = P.tile("SV", [128, 8, 2], BF16)
        self.MOD = [P.tile("MOD%d" % l, [128, 48, 2], F32) for l in range(4)]
        self.AB = [P.tile("AB%d" % l, [128, 2, 8, 2], F32) for l in range(4)]
        self.stat = Rot([P.tile("stat%d" % i, [128, 8], F32) for i in range(4)])
        banks = [P.psum_tile("pb%d" % i, [128, 512], F32) for i in range(8)]
        self.banks = banks
        self.psA = Rot(banks[0:4])
        self.psB = Rot(banks[4:7])
        self.psC = Rot(banks[7:8])
        self.scr_bufs = []
        self.scr_fence = []

    def _carve(self, base, off, shape, dtype, name, bufs, fence):
        n = int(np.prod(shape[1:]))
        nb = n * (4 if dtype == F32 else 2)
        assert off % 4 == 0 and off + nb <= base.ap.shape[1] * 2, (name, off, nb)
        ap = base.ap[0:shape[0], off // 2: (off + nb) // 2]
        if dtype == F32:
            ap = ap.bitcast(F32)
        if len(shape) > 2:
            names = " ".join("a%d" % i for i in range(len(shape) - 1))
            kw = {"a%d" % i: shape[i + 1] for i in range(len(shape) - 2)}
            ap = ap.rearrange("p (%s) -> p %s" % (names, names), **kw)
        b = Buf(name)
        b.r = list(fence)
        bufs.append(b)
        return T(ap, b)

    def pr(self, off_kib, shape, dtype, name):
        return self._carve(self.PR, int(off_kib * 1024), shape, dtype, name, self.pr_bufs, self.pr_fence)

    def pbump(self, shape, dtype, name):
        n = int(np.prod(shape[1:])) * (4 if dtype == F32 else 2)
        off = (self.bump + 31) // 32 * 32
        assert off + n <= self.bump_end, (name, off, n, self.bump_end)
        self.bump = off + n
        return self._carve(self.PR, off, shape, dtype, name, self.pr_bufs, self.pr_fence)

    def scr(self, off_kib, shape, dtype, name):
        return self._carve(self.SCR, int(off_kib * 1024), shape, dtype, name, self.scr_bufs, self.scr_fence)

    def fence(self):
        for bufs, attr in ((self.pr_bufs, "pr_fence"), (self.scr_bufs, "scr_fence")):
            toks = {}
            for b in bufs:
                ws = b.w if isinstance(b.w, list) else ([b.w] if b.w else [])
                for tok in ws + b.r:
                    if toks.get(tok[0], 0) < tok[1]:
                        toks[tok[0]] = tok[1]
            for k, c in getattr(self, attr):
                if toks.get(k, 0) < c:
                    toks[k] = c
            setattr(self, attr, list(toks.items()))
            for b in bufs:
                if len(b.r) > 8:
                    m = {}
                    for k, c in b.r:
                        if m.get(k, 0) < c:
                            m[k] = c
                    b.r = list(m.items())

    def subs(self, t, name):
        out = []
        for c in range(t.ap.shape[1]):
            row = []
            for i, (t0, w) in enumerate(TILES):
                b = Buf("%s%d_%d" % (name, c, i))
                b.r = list(t.buf.r)
                self.pr_bufs.append(b)
                row.append(T(t.ap[:, c, t0:t0 + w], b))
            out.append(row)
        return out

    def norm_scratch(self):
        self.sq = Rot([self.scr(i, [128, 512], BF16, "sq%d" % i) for i in range(2)])
        self.rs = Rot([self.scr(2, [128, 512], F32, "rs0")])
        self.tmpf = Rot([self.scr(4 + 2 * i, [128, 512], F32, "tmpf%d" % i) for i in range(2)])

    def wslot(self):
        return self.arot.next()

    def load_w(self, dram_ap, rows, cols, slot=None):
        P = self.P
        k = rows // 128
        s = slot if slot is not None else self.wslot()
        v = s[:, 0:k * cols].re("p (k n) -> p k n", k=k)
        P.dma("pool", v, dram_ap.rearrange("(k p) n -> p k n", p=128))
        return v

    def prologue(self):
        P = self.P
        d = self.d
        P.dma("sp", self.cst, d["cst"])
        P.copy("dve", self.ident, self.identf)
        P.memset("dve", self.ones_bf, 1.0)
        P.memset("dve", self.ones_f, 1.0)
        P.memset("dve", self.epsc, EPS)
        rows = [("ada_b", 192), ("norm_mix_w", 32), ("norm_mlp_w", 32), ("final_norm_w_rows", 8), ("c", 8),
                ("c_ctx", 8), ("ev_conv_w", 80), ("ev_conv_b", 16), ("od_conv_w", 80), ("od_conv_b", 16),
                ("ev_q_norm_w", 4), ("ev_kv_norm_w", 4)]
        self.voff = {}
        off = 0
        for n, k in rows:
            self.voff[n] = off
            off += k
        assert off <= 512
        stg = [self.pr(16 + 0.5 * i, [128, 128], F32, "vstg%d" % i) for i in range(4)]
        for s in stg:
            P.memset("dve", s, 0.0)
        for n, k in rows:
            src = d["final_norm_w"].rearrange("o (r p) -> (o r) p", p=128) if n == "final_norm_w_rows" else d[n]
            o = self.voff[n]
            done = 0
            while done < k:
                ti, r0 = divmod(o + done, 128)
                m = min(k - done, 128 - r0)
                P.dma("sp", stg[ti][r0:r0 + m, :], src[done:done + m, :])
                done += m
        for i in range(4):
            pt = self.psC.next()
            P.transpose(pt[:, 0:128], stg[i], self.identf)
            P.copy("dve", self.VEC[:, i * 128:(i + 1) * 128], pt[:, 0:128])
        oc, occ = self.voff["c"], self.voff["c_ctx"]
        P.act(self.SV[:, :, 0], self.VEC[:, oc:oc + 8], AF.Silu)
        P.act(self.SV[:, :, 1], self.VEC[:, occ:occ + 8], AF.Silu)
        xs = Rot([self.pr(4 * i, [128, D], F32, "xstg%d" % i) for i in range(4)])
        for ti, (t0, w) in enumerate(TILES):
            subs = []
            for j in range(w // 128):
                s = xs.next()
                tok = t0 + j * 128
                src = d["x"][tok:tok + 128, :] if tok < NL else d["ctx"][tok - NL:tok - NL + 128, :]
                P.dma("sp", s, src)
                subs.append(s)
            for c in range(8):
                pt = self.psA.next()
                for j, s in enumerate(subs):
                    P.transpose(pt[:, j * 128:(j + 1) * 128], s[:, c * 128:(c + 1) * 128], self.identf)
                eng = "act" if c % 2 else "dve"
                P.copy(eng, self.xt[c][ti], pt[:, 0:w])
        for q in range(12):
            self.ada_piece(0, q)
        self.ada_finish(0)

    def ada_load(self, l, q, slot=None):
        return self.load_w(self.d["ada_w"][l][:, q * 512:(q + 1) * 512], 1024, 512, slot=slot)

    def ada_mm(self, l, q, wv):
        P = self.P
        pt = self.psA.next()
        av = pt[:, 0:8].re("p (j v) -> p j v", v=2)
        for i in range(4):
            for k in range(8):
                P.mm(av[:, i, :], wv[:, k, i * 128:(i + 1) * 128], self.SV[:, k, :], start=(k == 0), stop=(k == 7))
        P.copy("act", self.MOD[l][:, q * 4:(q + 1) * 4, :], av)

    def ada_piece(self, l, q):
        self.ada_mm(l, q, self.ada_load(l, q))

    def ada_finish(self, l):
        P = self.P
        ob = self.voff["ada_b"] + l * 48
        P.tt("dve", self.MOD[l], self.MOD[l], self.VEC[:, ob:ob + 48].re("p (j o) -> p j o", o=1).bc([128, 48, 2]), ALU.add)
        for which, (scj, nwn) in enumerate(((1, "norm_mix_w"), (4, "norm_mlp_w"))):
            on = self.voff[nwn] + l * 8
            P.stt("dve", self.AB[l][:, which], self.MOD[l][:, scj * 8:(scj + 1) * 8, :], 1.0,
                  self.VEC[:, on:on + 8].re("p (j o) -> p j o", o=1).bc([128, 8, 2]), ALU.add, ALU.mult)

    class AdaStream:
        def __init__(self, mk, l):
            self.mk, self.l = mk, l
            self.q_loaded, self.q_done, self.wv = 0, 0, None
            self.slot = None

        def step(self, load=True):
            mk, l = self.mk, self.l
            if l is None:
                return
            if self.wv is not None:
                mk.ada_mm(l, self.q_done, self.wv)
                self.q_done += 1
                self.wv = None
            if load and self.q_loaded < 12:
                self.wv = mk.ada_load(l, self.q_loaded, slot=self.slot)
                self.q_loaded += 1

        def finish(self):
            if self.l is None:
                return
            while self.q_done < 12:
                self.step()
            self.mk.ada_finish(self.l)

    def norm_phase(self, l, which, tiles=None):
        P = self.P
        self.norm_scratch()
        self.HT = self.pr(60, [128, 8, NT], BF16, "HT")
        self.ht = self.subs(self.HT, "ht")
        shj = 0 if which == 0 else 3
        for ti, (t0, w) in enumerate(TILES):
            if tiles is not None and ti not in tiles:
                continue
            v = 1 if ti == 4 else 0
            ssp = self.psC.next()
            for c in range(8):
                sq = self.sq.next()
                if c % 4 == 3:
                    P.act(sq[:, :w], self.xt[c][ti], AF.Square)
                else:
                    P.tt("pool", sq[:, :w], self.xt[c][ti], self.xt[c][ti], ALU.mult)
                P.mm(ssp[:, :w], self.ones_bf, sq[:, :w], start=(c == 0), stop=(c == 7))
            rs = self.rs.next()
            P.act(rs[:, :w], ssp[:, :w], AF.Ln, bias=self.epsc, scale=1.0 / D)
            P.act(rs[:, :w], rs[:, :w], AF.Exp, scale=-0.5)
            for c in range(8):
                tmp = self.tmpf.next()
                P.tt("dve", tmp[:, :w], self.xt[c][ti], rs[:, :w], ALU.mult)
                P.act(self.ht[c][ti], tmp[:, :w], AF.Identity,
                      bias=self.MOD[l][:, shj * 8 + c, v:v + 1], scale=self.AB[l][:, which, c, v:v + 1])

    def mlp(self, l):
        P = self.P
        d = self.d
        last = (l == self.depth - 1)
        tiles = [i for i in range(5) if not (last and i == 4)]
        self.fence()
        self.norm_phase(l, 1, tiles)
        self.fence()
        self.relu = Rot([self.scr(i, [128, 512], BF16, "relu%d" % i) for i in range(3)])
        hid = [self.subs(self.pr(18 * b, [128, 4, NT], BF16, "hid%d" % b), "hid%d_" % b) for b in range(2)]
        nxt = None
        for g in range(8):
            w1 = self.load_w(d["mlp_w1"][l][:, g * 512:(g + 1) * 512], 1024, 512)
            w2 = self.load_w(d["mlp_w2"][l][g * 512:(g + 1) * 512, :], 512, 1024)
            hb = hid[g % 2]
            for ti in tiles:
                t0, w = TILES[ti]
                for hc in range(4):
                    pt = self.psA.next()
                    for k in range(8):
                        P.mm(pt[:, :w], w1[:, k, hc * 128:(hc + 1) * 128], self.ht[k][ti],
                             start=(k == 0), stop=(k == 7))
                    r = self.relu.next()
                    P.act(r[:, :w], pt[:, :w], AF.Relu)
                    P.tt("pool", hb[hc][ti], r[:, :w], r[:, :w], ALU.mult)
            if nxt is not None and g < 6:
                self.ada_piece(nxt, 2 * g)
            for ti in tiles:
                t0, w = TILES[ti]
                v = 1 if ti == 4 else 0
                for dc in range(8):
                    po = self.psB.next()
                    for hc in range(4):
                        P.mm(po[:, :w], w2[:, hc, dc * 128:(dc + 1) * 128], hb[hc][ti],
                             start=(hc == 0), stop=(hc == 3))
                    P.stt("dve", self.xt[dc][ti], po[:, :w], self.MOD[l][:, 40 + dc, v:v + 1], self.xt[dc][ti],
                          ALU.mult, ALU.add)
            if nxt is not None and g < 6:
                self.ada_piece(nxt, 2 * g + 1)
        if nxt is not None:
            self.ada_finish(nxt)

    def layer(self, l):
        if getattr(self, "halt", False):
            return
        self.fence()
        self.ada = MK.AdaStream(self, l + 1 if l + 1 < self.depth else None)
        if l % 2 == 0 and "even" in self.mixers:
            self.even_mixer(l)
        if l % 2 == 1 and "odd" in self.mixers:
            self.odd_mixer(l)
        if getattr(self, "halt", False):
            return
        self.ada.finish()
        self.mlp(l)

    def even_mixer(self, l):
        P = self.P
        d = self.d
        e = l // 2
        last = (l == self.depth - 1)
        tiles = [i for i in range(5) if not (last and i == 4)]
        do_ssd = "nossd" not in self.mixers
        do_attn = "noattn" not in self.mixers
        self.norm_phase(l, 0)
        self.fence()
        htall = self.join([t for row in self.ht for t in row], self.HT.ap, "htall")
        Win = d["ev_w_in"][e]
        XSB = self.pr(0, [128, 4, NT], BF16, "XSB")
        BCB = self.pr(18, [128, 4, NT], BF16, "BCB")
        CQT = self.pr(36, [128, 2, NT], BF16, "CQT")
        CKVT = self.pr(45, [128, 2, NT], BF16, "CKVT")
        KRAB = self.pr(54, [128, NT], BF16, "KRAB")
        self.bump, self.bump_end = int(58.5 * 1024), 60 * 1024
        DT = self.pbump([128, 18, 16], F32, "DT")
        A_b = self.pbump([128, 16], F32, "A_b")
        DTB = self.pbump([128, 16], F32, "DTB")
        DSK = self.pbump([128, 8], F32, "DSK")
        P.dma("sp", A_b, d["ev_a_log"][e:e + 1, :].to_broadcast([128, 16]))
        P.dma("sp", DTB, d["ev_dt_bias"][e:e + 1, :].to_broadcast([128, 16]))
        P.dma("sp", DSK, d["ev_d_skip"][e:e + 1, :].to_broadcast([128, 8]))
        P.act(A_b, A_b, AF.Exp)
        P.ts("dve", A_b, A_b, -1.0, None, ALU.mult)
        stg_rot = Rot([self.scr(1.03125 * i, [128, 516], BF16, "stg%d" % i) for i in range(3)])
        dg_rot = Rot([self.scr(3.125, [128, 5, 128], BF16, "dg0")])
        zs_rot = Rot([self.scr(4.5 + i, [128, 512], BF16, "zs%d" % i) for i in range(2)])
        cw = self.voff["ev_conv_w"] + e * 40
        cb = self.voff["ev_conv_b"] + e * 8
        wz = self.load_w(Win[:, 544:1056], 1024, 512)
        slot_dk = self.wslot()
        wdk = slot_dk[:, 0:8 * 112].re("p (k n) -> p k n", k=8)
        P.memset("pool", wdk[:, :, 48:80], 0.0)
        P.dma("pool", wdk[:, :, 0:16], Win[:, 2080:2096].rearrange("(k p) n -> p k n", p=128))
        P.dma("pool", wdk[:, :, 80:112], Win[:, 512:544].rearrange("(k p) n -> p k n", p=128))
        for b_ in range(2):
            for hf in range(2):
                so = 512 + 16 * b_ + 8 * (1 - hf)
                do = 16 + 16 * b_ + 8 * hf
                P.dma("pool", wdk[:, :, do:do + 8], Win[:, so:so + 8].rearrange("(k p) n -> p k n", p=128))
        for i in range(NCH):
            cs = slice(i * 128, (i + 1) * 128)
            pz = self.psA.next()
            for k in range(8):
                P.mm(pz, htall[:, k, cs], wz[:, k, :], start=(k == 0), stop=(k == 7))
            zs = zs_rot.next()
            P.act(zs, pz, AF.Silu)
            P.dma("sp", self.zch[i], zs, owner=zs)
            pd = self.psA.next()
            for k in range(8):
                P.mm(pd[:, 0:16], htall[:, k, cs], wdk[:, k, 0:16], start=(k == 0), stop=(k == 7))
            P.tt("dve", DT[:, i, :], pd[:, 0:16], DTB, ALU.add)
        P.act(DT, DT, AF.Exp)
        P.act(DT, DT, AF.Ln, bias=1.0)
        for ti, (t0, w) in enumerate(TILES):
            ts_ = slice(t0, t0 + w)
            pk = self.psA.next()
            for k in range(8):
                P.mm(pk[0:96, :w], wdk[:, k, 16:112], htall[:, k, ts_], start=(k == 0), stop=(k == 7))
            P.copy("act", KRAB[0:96, ts_], pk[0:96, :w])
        for half in range(2):
            wx = self.load_w(Win[:, 1056 + 512 * half:1056 + 512 * half + 512], 1024, 512)
            for cc in range(4):
                c = half * 4 + cc
                dst = XSB[:, c, :] if c < 4 else BCB[:, c - 4, :]
                self.conv_chunk(wx, cc * 128, cw + c, cb + c, dst, htall, stg_rot, dg_rot)
        self.fence()
        wl = self.load_w(Win[:, 0:512], 1024, 512)
        rawl = [self.scr(2 * i, [128, 512], F32, "rawl%d" % i) for i in range(2)]
        sqs = [self.scr(4 + i, [128, 512], BF16, "sql%d" % i) for i in range(2)]
        rsl = self.scr(6, [128, 512], F32, "rsl")
        for ti, (t0, w) in enumerate(TILES):
            ts_ = slice(t0, t0 + w)
            for lat in range(2):
                dstT = CQT if lat == 0 else CKVT
                nwo = self.voff["ev_q_norm_w" if lat == 0 else "ev_kv_norm_w"] + e * 2
                ssp = self.psC.next()
                for c in range(2):
                    pl = self.psA.next()
                    for k in range(8):
                        P.mm(pl[:, :w], wl[:, k, (lat * 2 + c) * 128:(lat * 2 + c + 1) * 128], htall[:, k, ts_],
                             start=(k == 0), stop=(k == 7))
                    P.copy("act", rawl[c][:, :w], pl[:, :w])
                    P.act(sqs[c][:, :w], rawl[c][:, :w], AF.Square)
                    P.mm(ssp[:, :w], self.ones_bf, sqs[c][:, :w], start=(c == 0), stop=(c == 1))
                P.act(rsl[:, :w], ssp[:, :w], AF.Ln, bias=self.epsc, scale=1.0 / 256)
                P.act(rsl[:, :w], rsl[:, :w], AF.Exp, scale=-0.5)
                for c in range(2):
                    P.stt("dve", dstT[:, c, ts_], rawl[c][:, :w], self.VEC[:, nwo + c:nwo + c + 1], rsl[:, :w],
                          ALU.mult, ALU.mult)
        self.fence()
        if do_ssd:
            self.ssd_scan(l, e, XSB, BCB, DT, A_b, DSK, last)
            wo = self.load_w(d["ev_w_out"][e][512:1024, :], 512, 1024)
            self.out_proj(l, wo, 4, [XSB[:, c, :] for c in range(4)], tiles)
        self.fence()
        if do_attn:
            self.attention(l, e, CQT, CKVT, KRAB, last, tiles)

    def ssd_scan(self, l, e, XSB, BCB, DT, A_b, DSK, last):
        P = self.P
        d = self.d
        self.bump, self.bump_end = 60 * 1024, 96 * 1024
        SNAP = self.pbump([128, 18, 512], BF16, "SSNAP")
        RA = self.pbump([128, 8, 128], F32, "RA")
        WTt = self.pbump([128, 8, 128], BF16, "WTt")
        CBM = [self.pbump([128, 2, 128], BF16, "CBM%d" % i) for i in range(2)]
        XS_TM = self.pbump([128, 8, 64], BF16, "XS_TM")
        BM_TM = self.pbump([128, 2, 128], BF16, "BM_TM")
        XP = [self.pbump([128, 8, 64], BF16, "XP%d" % i) for i in range(2)]
        H = self.pbump([128, 8, 64], F32, "Hst")
        HBF = self.pbump([128, 512], BF16, "HBF")
        SNW = self.pbump([128, 512], F32, "SNW")
        GNb = self.pbump([128, 512], BF16, "GNb")
        TMP = self.scr(0, [128, 512], F32, "ssd_tmp")
        ACCs = [self.scr(2 + 2 * i, [128, 512], F32, "ssd_acc%d" % i) for i in range(2)]
        zin = Rot([self.scr(6 + i, [128, 512], BF16, "zin%d" % i) for i in range(2)])
        junk = GNb[:, 0:256]
        P.dma("sp", SNW, d["ev_ssm_norm_w"][e:e + 1, :].to_broadcast([128, 512]))

        sl = self.wslot()
        def slv(k):
            return sl.v(sl.ap[:, k * 576:(k + 1) * 576].bitcast(F32).rearrange("p (d c h) -> p d c h", d=2, c=18))
        DTA, NACa, EACa, TOEa, CDa, DTOE = slv(0), slv(1), slv(2), slv(3), slv(4), slv(5)
        for dd in range(2):
            P.tt("dve", DTA[:, dd], DT[:, :, dd * 8:dd * 8 + 8],
                 A_b[:, dd * 8:dd * 8 + 8].re("p (o h) -> p o h", o=1).bc([128, 18, 8]), ALU.mult)
        pa = self.psA.next()
        pb = self.psA.next()
        for dd in range(2):
            P.mm(pa[:, dd * 144:(dd + 1) * 144], self.tri[dd], DTA[:, dd].re("p c h -> p (c h)"))
        P.mm(pb[:, 0:288], self.ones_f, DTA.re("p d c h -> p (d c h)"))
        pav = pa[:, 0:288].re("p (d c h) -> p d c h", d=2, c=18)
        pbv = pb[:, 0:288].re("p (d c h) -> p d c h", d=2, c=18)
        P.act(NACa, pav, AF.Copy, scale=-1.0)
        P.act(EACa, pav, AF.Exp)
        P.act(CDa, pbv, AF.Exp)
        P.tt("dve", TOEa, pbv, NACa, ALU.add)
        P.act(TOEa, TOEa, AF.Exp)
        for dd in range(2):
            P.tt("dve", DTOE[:, dd], TOEa[:, dd], DT[:, :, dd * 8:dd * 8 + 8], ALU.mult)

        sl2 = self.wslot()
        RA2 = sl2.v(sl2.ap[:, 0:2048].bitcast(F32).rearrange("p (h t) -> p h t", h=8))
        WTt2 = sl2.v(sl2.ap[:, 2048:3072].rearrange("p (h t) -> p h t", h=8))
        RAs, WTts = [RA, RA2], [WTt, WTt2]

        def prep(dd, i, need_decay):
            cs = slice(i * 128, (i + 1) * 128)
            dt = DT[:, i, dd * 8:dd * 8 + 8]
            dtA, NAC, EAC, TOE, CD = (DTA[:, dd, i, :], NACa[:, dd, i, :], EACa[:, dd, i, :], TOEa[:, dd, i, :], CDa[:, dd, i, :])
            r = dict(dt=dt, dtA=dtA, NAC=NAC, EAC=EAC, TOE=TOE, CD=CD, cs=cs, DTOE=DTOE[:, dd, i, :])
            return r

        def decay_a(dd, pr_):
            RAd = RAs[dd]
            P.tt("pool", RAd, self.tri[dd].re("p (o t) -> p o t", o=1).bc([128, 8, 128]),
                 pr_["dtA"].re("p (h o) -> p h o", o=1).bc([128, 8, 128]), ALU.mult)
            pA = [self.psA.next(), self.psA.next()]
            for hf in range(2):
                P.mm(pA[hf], self.ones_f, RAd[:, 4 * hf:4 * hf + 4, :].re("p h t -> p (h t)"))
            return pA

        def decay_b(dd, pr_, pA):
            RAd = RAs[dd]
            for hf in range(2):
                P.tt("dve", RAd[:, 4 * hf:4 * hf + 4, :], pA[hf].re("p (h t) -> p h t", h=4),
                     pr_["NAC"][:, 4 * hf:4 * hf + 4].re("p (h o) -> p h o", o=1).bc([128, 4, 128]), ALU.add)
            P.act(RAd, RAd, AF.Relu, scale=-1.0)
            P.act(RAd, RAd, AF.Exp, scale=-1.0)

        def transposes(i, need_x=True):
            cs = slice(i * 128, (i + 1) * 128)
            pt = self.psA.next()
            ptb = pt.v(pt.ap.bitcast(BF16))
            for c in range(4):
                P.transpose(ptb[:, c * 128:(c + 1) * 128], XSB[:, c, cs], self.ident)
            P.copy("act", XS_TM.re("p h v -> p (h v)"), ptb[:, 0:512])
            pt2 = self.psA.next()
            ptb2 = pt2.v(pt2.ap.bitcast(BF16))
            for g in range(2):
                P.transpose(ptb2[:, g * 128:(g + 1) * 128], BCB[:, g, cs], self.ident)
            P.copy("act", BM_TM.re("p g n -> p (g n)"), ptb2[:, 0:256])

        def state_update(pr_, xp, XPP):
            P.tt("pool", XPP, XS_TM, pr_["DTOE"].re("p (h o) -> p h o", o=1).bc([128, 8, 64]), ALU.mult)
            ph = self.psB.next()
            for g in range(2):
                P.mm(ph[:, g * 256:(g + 1) * 256], BM_TM[:, g, :], XPP[:, 4 * g:4 * g + 4, :].re("p h v -> p (h v)"))
            P.tt("dve", H, H, pr_["CD"].re("p (h o) -> p h o", o=1).bc([128, 8, 64]), ALU.mult)
            P.tt("dve", H.re("p h v -> p (h v)"), H.re("p h v -> p (h v)"), ph, ALU.add)

        P.memset("dve", H, 0.0)
        orderA = [16, 17] + list(range(16))
        for n_, i in enumerate(orderA):
            P.copy("dve", SNAP[:, i, :], H.re("p h v -> p (h v)"))
            if n_ == len(orderA) - 1:
                break
            pr_ = prep(0, i, False)
            transposes(i)
            state_update(pr_, None, XP[1])
        P.memset("dve", H, 0.0)
        orderB = [17, 16] + list(range(15, -1, -1))

        def head(n_, i, need_out):
            cs = slice(i * 128, (i + 1) * 128)
            ACC = ACCs[n_ % 2]
            transposes(i)
            zt = None
            if need_out:
                zt = zin.next()
                P.dma("sp", zt, self.zch[i], owner=zt)
                P.copy("dve", HBF, H.re("p h v -> p (h v)"))
                pcb = self.psA.next()
                for g in range(2):
                    P.mm(pcb[:, g * 128:(g + 1) * 128], BCB[:, g, cs], BCB[:, 2 + g, cs])
                pcv = pcb[:, 0:256].re("p (g t) -> p g t", g=2)
                for dd in range(2):
                    P.tt("dve", CBM[dd], pcv, self.tri[dd].re("p (o t) -> p o t", o=1).bc([128, 2, 128]), ALU.mult)
                P.tt("pool", ACC.re("p (h v) -> p h v", h=8), XS_TM, DSK.re("p (h o) -> p h o", o=1).bc([128, 8, 64]), ALU.mult)
            prs = [prep(dd, i, need_out) for dd in range(2)]
            for dd in range(2):
                P.tt("pool", XP[dd], XS_TM, prs[dd]["dt"].re("p (h o) -> p h o", o=1).bc([128, 8, 64]), ALU.mult)
            if need_out:
                pAs = [decay_a(dd, prs[dd]) for dd in range(2)]
                for dd in range(2):
                    decay_b(dd, prs[dd], pAs[dd])
                pys = []
                for dd in range(2):
                    for g in range(2):
                        P.tt("dve", WTts[dd][:, 4 * g:4 * g + 4, :], RAs[dd][:, 4 * g:4 * g + 4, :],
                             CBM[dd][:, g:g + 1, :].bc([128, 4, 128]), ALU.mult)
                    py = self.psB.next()
                    for h in range(8):
                        P.mm(py[:, h * 64:(h + 1) * 64], WTts[dd][:, h, :], XP[dd][:, h, :])
                    pys.append(py)
                for dd in range(2):
                    pyi = self.psB.next() if dd == 0 else self.psC.next()
                    hsrc = SNAP[:, i, :] if dd == 0 else HBF
                    for g in range(2):
                        P.mm(pyi[:, g * 256:(g + 1) * 256], BCB[:, 2 + g, cs], hsrc[:, g * 256:(g + 1) * 256])
                    P.tt("dve", TMP.re("p (h v) -> p h v", h=8), pyi.re("p (h v) -> p h v", h=8),
                         prs[dd]["EAC"].re("p (h o) -> p h o", o=1).bc([128, 8, 64]), ALU.mult)
                    P.tt("dve", TMP, TMP, pys[dd], ALU.add)
                    P.tt("pool", ACC, ACC, TMP, ALU.add)
            if n_ < len(orderB) - 1:
                state_update(prs[1], XP[1], XP[0])
            if not need_out:
                return None

            def tail():
                Rr = self.stat.next()
                P.tt("pool", ACC, ACC, zt, ALU.mult)
                for g in range(2):
                    P.act(junk, ACC[:, g * 256:(g + 1) * 256], AF.Square, accum=Rr[:, g:g + 1])
                P.act(Rr[:, 2:4], Rr[:, 0:2], AF.Ln, bias=self.epsc, scale=1.0 / 256)
                P.act(Rr[:, 2:4], Rr[:, 2:4], AF.Exp, scale=-0.5)
                for g in range(2):
                    gs = slice(g * 256, (g + 1) * 256)
                    P.stt("dve", GNb[:, gs], ACC[:, gs], Rr[:, 2 + g:3 + g], SNW[:, gs], ALU.mult, ALU.mult)
                pg = self.psA.next()
                pgb = pg.v(pg.ap.bitcast(BF16))
                for c in range(4):
                    P.transpose(pgb[:, c * 128:(c + 1) * 128], GNb[:, c * 128:(c + 1) * 128], self.ident)
                P.copy("act", XSB[:, :, cs], pgb[:, 0:512].re("p (c t) -> p c t", c=4))
            return tail

        pending = None
        for n_, i in enumerate(orderB):
            need_out = not (last and i >= 16)
            t_ = head(n_, i, need_out)
            if pending is not None:
                pending()
            pending = t_
        if pending is not None:
            pending()

    def attention(self, l, e, CQT, CKVT, KRAB, last, tiles):
        P = self.P
        d = self.d
        SCALE = 96.0 ** -0.5
        CT = self.pr(0, [128, NT], F32, "ropeCT")
        ST = self.pr(9, [128, NT], F32, "ropeST")
        GTA = self.pr(18, [128, 4, NT], BF16, "GTA")
        self.bump, self.bump_end = 60 * 1024, 96 * 1024
        QTs = [self.pbump([128, NT], BF16, "QT%d" % i) for i in range(2)]
        KTs = [self.pbump([128, NT], BF16, "KT%d" % i) for i in range(2)]
        VA = [self.pbump([128, 18, 128], BF16, "VA%d" % i) for i in range(2)]
        pts = Rot([self.pbump([128, 512], BF16, "PT%d" % i) for i in range(5)])
        T1 = self.scr(0, [128, 512], F32, "aT1")
        T2 = self.scr(2, [128, 512], F32, "aT2")
        RD = self.scr(4, [128, 512], F32, "aRD")
        T2s = self.scr(6, [128, 512], F32, "aT2s")
        P.memset("dve", CT[0:64, :], 1.0)
        P.dma("sp", CT[64:96, :], d["ropeC"])
        P.dma("sp", ST[64:96, :], d["ropeS"])
        P.dma("sp", ST[0:32, :], d["ropeS"])
        P.memset("dve", ST[32:64, :], 0.0)
        P.memset("pool", VA[0][:, :, 64:128], 1.0)
        P.memset("pool", VA[1][:, :, 0:64], 1.0)
        s1 = self.wslot()
        WUQ = s1[:, 0:1536].re("p (k n) -> p k n", k=2)
        WUQP = s1[:, 1536:3072].re("p (k n) -> p k n", k=2)
        P.memset("pool", WUQP, 0.0)
        P.dma("pool", WUQ, d["ev_w_uq"][e].rearrange("(k p) n -> p k n", p=128))
        src5 = d["ev_w_uq"][e].rearrange("(k p) (h f) -> p k h f", p=128, f=96)
        dst5 = WUQP.re("p k (h f) -> p k h f", f=96)
        for k in range(2):
            for b_ in range(2):
                for hf in range(2):
                    so = 64 + 16 * b_ + 8 * (1 - hf)
                    do = 64 + 16 * b_ + 8 * hf
                    P.dma("pool", dst5[:, k, :, do:do + 8], src5[:, k, :, so:so + 8])
        WUKV = self.load_w(d["ev_w_ukv"][e], 256, 1024)
        for ti, (t0, w) in enumerate(TILES):
            ts_ = slice(t0, t0 + w)
            P.tt("dve", T2[0:32, :w], KRAB[0:32, ts_], ST[0:32, ts_], ALU.mult)
            P.copy("act", T2s[64:96, :w], T2[0:32, :w])
            P.tt("dve", T1[64:96, :w], KRAB[64:96, ts_], CT[64:96, ts_], ALU.mult)
            P.tt("pool", KTs[0][64:96, ts_], T1[64:96, :w], T2s[64:96, :w], ALU.add)
            P.tt("pool", KTs[1][64:96, ts_], T1[64:96, :w], T2s[64:96, :w], ALU.add)

        def prep(h):
            par = h % 2
            QT, KT = QTs[par], KTs[par]
            for ti, (t0, w) in enumerate(TILES):
                ts_ = slice(t0, t0 + w)
                pa_ = self.psA.next()
                pb_ = self.psA.next()
                for k in range(2):
                    P.mm(pa_[0:96, :w], WUQ[:, k, h * 96:(h + 1) * 96], CQT[:, k, ts_], start=(k == 0), stop=(k == 1))
                for k in range(2):
                    P.mm(pb_[0:96, :w], WUQP[:, k, h * 96:(h + 1) * 96], CQT[:, k, ts_], start=(k == 0), stop=(k == 1))
                P.tt("dve", T1[0:96, :w], pa_[0:96, :w], CT[0:96, ts_], ALU.mult)
                P.tt("dve", T2[0:96, :w], pb_[0:96, :w], ST[0:96, ts_], ALU.mult)
                P.tt("pool", QT[0:96, ts_], T1[0:96, :w], T2[0:96, :w], ALU.add)
                pk = self.psA.next()
                for k in range(2):
                    P.mm(pk[0:64, :w], WUKV[:, k, h * 128:h * 128 + 64], CKVT[:, k, ts_], start=(k == 0), stop=(k == 1))
                P.copy("dve", KT[0:64, ts_], pk[0:64, :w])
            va = VA[par]
            vo = 64 * par
            for i0 in range(0, NCH, 8):
                n = min(8, NCH - i0)
                pv = self.psA.next()
                for jj in range(n):
                    i = i0 + jj
                    for k in range(2):
                        P.mm(pv[:, jj * 64:(jj + 1) * 64], CKVT[:, k, i * 128:(i + 1) * 128],
                             WUKV[:, k, h * 128 + 64:h * 128 + 128], start=(k == 0), stop=(k == 1))
                P.copy("dve", va[:, i0:i0 + n, vo:vo + 64], pv[:, 0:n * 64].re("p (c v) -> p c v", c=n))

        def attend(h):
            par = h % 2
            QT, KT, va = QTs[par], KTs[par], VA[par]
            orow = slice(64 * par, 64 * par + 64)
            drow = slice(64 * (1 - par), 64 * (1 - par) + 64)
            for ti in tiles:
                t0, w = TILES[ti]
                ts_ = slice(t0, t0 + w)
                chunks = list(range(NCH)) if ti < 4 else [16, 17]
                po = self.psB.next()
                pend = []

                def do_pv(idx, i, ptile):
                    P.mm(po[:, :w], va[:, i, :], ptile[:, :w], start=(idx == 0), stop=(idx == len(chunks) - 1))

                for idx, i in enumerate(chunks):
                    ps_ = self.psA.next()
                    P.mm(ps_[:, :w], KT[0:96, i * 128:(i + 1) * 128], QT[0:96, ts_])
                    ptile = pts.next()
                    P.act(ptile[:, :w], ps_[:, :w], AF.Exp, scale=SCALE)
                    pend.append((idx, i, ptile))
                    if len(pend) > 2:
                        do_pv(*pend.pop(0))
                while pend:
                    do_pv(*pend.pop(0))
                P.recip(RD[orow, :w], po[drow, :w])
                P.tt("dve", GTA[orow, h // 2, ts_], po[orow, :w], RD[orow, :w], ALU.mult)

        self.ada.slot = self.wslot()
        prep(0)
        for h in range(8):
            if h + 1 < 8:
                prep(h + 1)
            self.ada.step()
            attend(h)
            if h < 4:
                self.ada.step()
        if self.ada.wv is not None:
            self.ada.step(load=False)
        self.ada.slot = None
        if self.stop == "attn_end":
            self.halt = True
            return
        wo = self.load_w(d["ev_w_out"][e][0:512, :], 512, 1024)
        self.out_proj(l, wo, 4, [GTA[:, c, :] for c in range(4)], tiles)

    def join(self, tlist, ap, name):
        b = Buf(name)
        ws = []
        for t in tlist:
            w = t.buf.w
            if w is None:
                continue
            ws.extend(w if isinstance(w, list) else [w])
        b.w = ws
        self.pr_bufs.append(b)
        return T(ap, b)

    def bcast_load(self, dst, dram_row_ap, n):
        self.P.dma("sp", dst, dram_row_ap.to_broadcast([128, n]))

    CONV_TILES = [(0, 508), (508, 1016), (1016, 1524), (1524, 2032), (2032, 2048), (2048, 2304)]

    def conv_chunk(self, wv, col0, vec_w_off, vec_b_off, dst, htall, stg_rot, dg_rot):
        P = self.P
        dg = dg_rot.next()
        for j in range(5):
            P.ts("pool", dg[:, j, :], self.identf, self.VEC[:, vec_w_off + 8 * j:vec_w_off + 8 * j + 1], None, ALU.mult)
        for (a, b) in self.CONV_TILES:
            s0, s1 = (0, NL) if a < NL else (NL, NT)
            ia, ib = max(a - 2, s0), min(b + 2, s1)
            w = b - a
            win = ib - ia
            j0 = ia - (a - 2)
            pt = self.psA.next()
            for k in range(8):
                P.mm(pt[:, 0:win], wv[:, k, col0:col0 + 128], htall[:, k, ia:ib], start=(k == 0), stop=(k == 7))
            stg = stg_rot.next()
            if j0 > 0:
                P.memset("pool", stg[:, 0:j0], 0.0)
            if j0 + win < w + 4:
                P.memset("pool", stg[:, j0 + win:w + 4], 0.0)
            P.copy("act", stg[:, j0:j0 + win], pt[:, 0:win])
            pc = self.psB.next()
            for j in range(5):
                P.mm(pc[:, 0:w], dg[:, j, :], stg[:, j:j + w], start=(j == 0), stop=(j == 4))
            P.act(dst[:, a:b], pc[:, 0:w], AF.Silu, bias=self.VEC[:, vec_b_off:vec_b_off + 1])

    def out_proj(self, l, wv, nk, gts, tiles):
        P = self.P
        for ti in tiles:
            t0, w = TILES[ti]
            v = 1 if ti == 4 else 0
            for dc in range(8):
                po = self.psB.next()
                for k in range(nk):
                    P.mm(po[:, :w], wv[:, k, dc * 128:(dc + 1) * 128], gts[k][:, t0:t0 + w],
                         start=(k == 0), stop=(k == nk - 1))
                P.stt("dve", self.xt[dc][ti], po[:, :w], self.MOD[l][:, 16 + dc, v:v + 1], self.xt[dc][ti],
                      ALU.mult, ALU.add)

    def odd_mixer(self, l):
        P = self.P
        d = self.d
        o = l // 2
        last = (l == self.depth - 1)
        tiles = [i for i in range(5) if not (last and i == 4)]
        self.norm_phase(l, 0)
        self.fence()
        htall = self.join([t for row in self.ht for t in row], self.HT.ap, "htall")
        Win = d["od_w_in"][o]
        self.bump, self.bump_end = 41 * 1024, 60 * 1024
        IGF = self.pbump([128, 18, 32], F32, "IGF")
        LFn = self.pbump([128, 2, 18, 8], F32, "LFn")
        BALL = self.pbump([128, 2, 18, 8], F32, "BALL")
        EC = self.pbump([128, 2, 18, 8], F32, "EC")
        THR = self.pbump([128, 2, 18, 8], F32, "THR")
        EBL = self.pbump([128, 2, 18, 8], F32, "EBL")
        WT = self.pbump([128, 2, 18, 8], F32, "WT")
        HNW = self.pbump([128, D], F32, "HNW")
        BIAS = self.pbump([128, 32], F32, "BIAS")
        MSK2 = self.pbump([128, 2, 128], F32, "MSK2")
        MSK = [MSK2[:, dd, :] for dd in range(2)]

        tsm = Rot([self.pbump([128, 2, 128], BF16, "sm%d" % i) for i in range(2)])
        tv2 = Rot([self.pbump([128, 2, 2, 129], BF16, "v2_%d" % i) for i in range(1)])
        tv3 = Rot([self.pbump([128, 2, 129], BF16, "v3_%d" % i) for i in range(1)])
        KTM = self.scr(7.5, [128, 128], BF16, "KTM")
        CBF = self.pbump([128, 129], BF16, "CBF")
        GN = self.pbump([128, 2, 128], BF16, "GN")
        CST = [self.pbump([128, 129], F32, "C_dir0")]
        T0 = self.scr(4.5, [128, 2, 128], F32, "T0")
        HS = self.scr(5.5, [128, 2, 128], F32, "HS")
        HOs = [self.scr(6.5 + 0.5 * i, [128, 256], BF16, "HO%d" % i) for i in range(2)]
        sg_rot = Rot([T0.re("p h v -> p (h v)"), HS.re("p h v -> p (h v)")])
        self.bcast_load(HNW, d["od_head_norm_w"][o:o + 1, :], D)
        P.dma("sp", BIAS[:, 0:16], d["od_i_bias"][o:o + 1, :].to_broadcast([128, 16]))
        P.dma("sp", BIAS[:, 16:32], d["od_f_bias"][o:o + 1, :].to_broadcast([128, 16]))
        for dd in range(2):
            P.ts("dve", MSK2[:, dd, :], self.tri[dd], 0.125, None, ALU.mult)
        stg_rot = Rot([self.scr(1.03125 * i, [128, 516], BF16, "stg%d" % i) for i in range(3)])
        dg_rot = Rot([self.scr(3.125, [128, 5, 128], BF16, "dg0")])
        accb = self.scr(0, [128, 512], F32, "accb")
        for j in range(4):
            QKT = self.pr(0, [128, 2, NT], BF16, "QKT")
            VA = self.pr(9, [128, 18, 2, 129], BF16, "VA")
            OG = self.pr(18.25, [128, 18, 256], BF16, "OG")
            SNAP = self.pr(27.25, [128, 18, 129], BF16, "SNAP")
            GTp = self.pr(32, [128, 2, NT], BF16, "GTp")
            slotA = self.wslot()
            wA = slotA[:, 0:4096].re("p (k n) -> p k n", k=8)
            P.dma("pool", wA[:, :, 0:128], Win[:, 128 * j:128 * j + 128].rearrange("(k p) n -> p k n", p=128))
            P.dma("pool", wA[:, :, 128:256], Win[:, 512 + 128 * j:512 + 128 * j + 128].rearrange("(k p) n -> p k n", p=128))
            P.dma("pool", wA[:, :, 256:512], Win[:, 1024 + 256 * j:1024 + 256 * j + 256].rearrange("(k p) n -> p k n", p=128))
            slotB = self.wslot()
            wB = slotB[:, 0:8 * 288].re("p (k n) -> p k n", k=8)
            P.dma("pool", wB[:, :, 0:256], Win[:, 2048 + 256 * j:2048 + 256 * j + 256].rearrange("(k p) n -> p k n", p=128))
            if j == 0:
                P.dma("pool", wB[:, :, 256:288], Win[:, 3072:3104].rearrange("(k p) n -> p k n", p=128))
            P.memset("pool", VA[:, :, :, 128:129], 1.0)
            if self.stop == "wload":
                continue
            cw = self.voff["od_conv_w"] + o * 40
            cb = self.voff["od_conv_b"] + o * 8
            self.conv_chunk(wA, 0, cw + j, cb + j, QKT[:, 0, :], htall, stg_rot, dg_rot)
            self.conv_chunk(wA, 128, cw + 4 + j, cb + 4 + j, QKT[:, 1, :], htall, stg_rot, dg_rot)
            if self.stop == "conv":
                continue
            for i in range(NCH):
                pt = self.psA.next()
                for k in range(8):
                    P.mm(pt[:, 0:256], htall[:, k, i * 128:(i + 1) * 128], wA[:, k, 256:512], start=(k == 0), stop=(k == 7))
                for k in range(8):
                    P.mm(pt[:, 256:512], htall[:, k, i * 128:(i + 1) * 128], wB[:, k, 0:256], start=(k == 0), stop=(k == 7))
                P.copy("act", VA[:, i, :, 0:128], pt[:, 0:256].re("p (h v) -> p h v", h=2))
                sgt = sg_rot.next()
                P.act(sgt, pt[:, 256:512], AF.Exp, scale=-1.0)
                P.act(sgt, sgt, AF.Ln, bias=1.0)
                P.act(sgt, sgt, AF.Exp, scale=-1.0)
                P.tt("pool", OG[:, i, :], sgt, HNW[:, 2 * j * 128:(2 * j + 2) * 128], ALU.mult)
                if j == 0 and self.stop != "vo_nogate":
                    pg = self.psA.next()
                    for k in range(8):
                        P.mm(pg[:, 0:32], htall[:, k, i * 128:(i + 1) * 128], wB[:, k, 256:288], start=(k == 0), stop=(k == 7))
                    P.tt("dve", IGF[:, i, :], pg[:, 0:32], BIAS, ALU.add)
            if self.stop in ("vo", "vo_nogate"):
                continue
            if j == 0:
                FGv = IGF[:, :, 16:32].re("p c (d h) -> p d c h", d=2)
                IGv = IGF[:, :, 0:16].re("p c (d h) -> p d c h", d=2)
                P.act(LFn, FGv, AF.Exp, scale=-1.0)
                P.act(LFn, LFn, AF.Ln, bias=1.0)
                P.ts("dve", LFn, LFn, -1.0, None, ALU.mult)
                pb_ = self.psA.next()
                pbv = pb_[:, 0:288].re("p (d c h) -> p d c h", c=18, d=2)
                for dd in range(2):
                    P.mm(pb_[:, dd * 144:(dd + 1) * 144], self.tri[dd], LFn[:, dd].re("p c h -> p (c h)"))
                P.copy("dve", BALL, pbv)
                pe_ = self.psA.next()
                P.mm(pe_[:, 0:288], self.ones_f, LFn.re("p d c h -> p (d c h)"))
                P.act(EBL, pe_[:, 0:288].re("p (d c h) -> p d c h", c=18, d=2), AF.Exp)
                P.tt("dve", EC, IGv, BALL, ALU.subtract)
                P.act(EC, EC, AF.Exp)
                P.act(THR, BALL, AF.Exp, scale=-1.0)
                P.tt("dve", WT, EC, EBL, ALU.mult)
            h0 = 2 * j
            if self.stop == "inproj":
                continue

            def state_update(Cst, i, dd):
                ptk = self.psA.next()
                ptkb = ptk.v(ptk.ap.bitcast(BF16))
                P.transpose(ptkb[:, 0:128], QKT[:, 1, i * 128:(i + 1) * 128], self.ident)
                P.copy("act", KTM, ptkb[:, 0:128])
                v3 = tv3.next()
                P.tt("pool", v3, VA[:, i], WT[:, dd, i, h0:h0 + 2].re("p (h o) -> p h o", o=1).bc([128, 2, 129]), ALU.mult)
                pd = self.psA.next()
                pdv = pd[:, 0:258].re("p (h n) -> p h n", h=2)
                for h in range(2):
                    P.mm(pdv[:, h, :], KTM, v3[:, h, :])
                for h in range(2):
                    r = slice(64 * h, 64 * h + 64)
                    P.stt("dve", Cst[r, :], Cst[r, :], EBL[r, dd, i, h0 + h:h0 + h + 1], pdv[r, h, :], ALU.mult, ALU.add)

            Cst = CST[0]
            P.memset("dve", Cst, 0.0)
            orderA = [16, 17] + list(range(16))
            for n_, i in enumerate(orderA):
                P.ts("dve", SNAP[:, i, :], Cst, 0.125, None, ALU.mult)
                if n_ % 4 == 2:
                    self.ada.step(load=(n_ < 14))
                if n_ < len(orderA) - 1:
                    state_update(Cst, i, 0)
            if self.stop == "passA":
                continue
            P.memset("dve", Cst, 0.0)
            orderB = [17, 16] + list(range(15, -1, -1))
            HSs = [HS, accb[:, 0:256].re("p (h v) -> p h v", h=2)]
            junk2 = accb[:, 256:384]
            TP = accb.v(accb.ap[:, 384:512].bitcast(BF16))

            def lockstep(*gens):
                gens = [g for g in gens if g is not None]
                while gens:
                    for g in list(gens):
                        try:
                            next(g)
                        except StopIteration:
                            gens.remove(g)

            def head(n_, i, out):
                cs = slice(i * 128, (i + 1) * 128)
                HSc = HSs[n_ % 2]
                HO = HOs[n_ % 2]
                pst = []
                for h in range(2):
                    r = slice(64 * h, 64 * h + 64)
                    pb1 = self.psA.next()
                    P.mm(pb1[:, 0:128], QKT[r, 1, cs], QKT[r, 0, cs])
                    pst.append(pb1)
                P.ts("dve", CBF, Cst, 0.125, None, ALU.mult)
                yield
                Rr = self.stat.next()
                smh = [tsm.next(), tsm.next()]
                for h in range(2):
                    P.tt("dve", smh[h], pst[h][:, 0:128].re("p (o t) -> p o t", o=1).bc([128, 2, 128]), MSK2, ALU.mult)
                    yield
                v2b = tv2.next()
                P.tt("pool", v2b, VA[:, i].re("p (o h) n -> p o h n", o=1).bc([128, 2, 2, 129]),
                     EC[:, :, i, h0:h0 + 2].re("p d (h o) -> p d h o", o=1).bc([128, 2, 2, 129]), ALU.mult)
                yield
                poss = [[None, None], [None, None]]
                for h in range(2):
                    r = slice(64 * h, 64 * h + 64)
                    for dd in range(2):
                        po = self.psB2[h].next()
                        cb_ = SNAP[:, i, :] if dd == 0 else CBF
                        P.mm(po[:, 0:129], smh[h][:, dd, :], v2b[:, dd, h, :], start=True, stop=False)
                        P.mm(po[:, 0:129], QKT[r, 0, cs], cb_[r, :], start=False, stop=True)
                        yield
                        c = 2 * dd + h
                        P.act(Rr[:, c:c + 1], po[:, 128:129], AF.Abs)
                        yield
                        poss[h][dd] = po
                P.tt("dve", Rr[:, 0:4].re("p (d h) -> p d h", d=2), Rr[:, 0:4].re("p (d h) -> p d h", d=2),
                     THR[:, :, i, h0:h0 + 2], ALU.max)
                yield
                P.recip(Rr[:, 0:4], Rr[:, 0:4])
                yield
                for h in range(2):
                    P.act(T0[:, h, :], poss[h][0][:, 0:128], AF.Identity, scale=Rr[:, h:h + 1])
                    yield
                for h in range(2):
                    P.stt("dve", HSc[:, h, :], poss[h][1][:, 0:128], Rr[:, 2 + h:3 + h], T0[:, h, :], ALU.mult, ALU.add)
                    yield

                def tail():
                    for h in range(2):
                        P.act(junk2, HSc[:, h, :], AF.Square, accum=Rr[:, 4 + h:5 + h])
                        yield
                    P.act(Rr[:, 6:8], Rr[:, 4:6], AF.Ln, bias=self.epsc, scale=1.0 / 128)
                    yield
                    P.act(Rr[:, 6:8], Rr[:, 6:8], AF.Exp, scale=-0.5)
                    yield
                    P.tt("pool", TP.re("p (h v) -> p h v", h=2), HSc, Rr[:, 6:8].re("p (h o) -> p h o", o=1).bc([128, 2, 128]), ALU.mult)
                    yield
                    P.tt("pool", GN.re("p h v -> p (h v)"), TP, OG[:, i, :], ALU.mult)
                    yield
                    ptg = self.psA.next()
                    ptgb = ptg.v(ptg.ap.bitcast(BF16))
                    for h in range(2):
                        P.transpose(ptgb[:, h * 128:(h + 1) * 128], GN[:, h, :], self.ident)
                    yield
                    P.copy("act", GTp[:, :, cs], ptgb[:, 0:256].re("p (h t) -> p h t", h=2))
                    yield
                out.append(tail)

            self.psB2 = [Rot(self.banks[4:6]), Rot(self.banks[6:8])]
            pending = None
            for n_, i in enumerate(orderB):
                need_out = not (last and i >= 16)
                out = []
                hg = head(n_, i, out) if need_out else None
                lockstep(hg, pending() if pending is not None else None)
                pending = out[0] if out else None
                if n_ < len(orderB) - 1:
                    state_update(Cst, i, 1)
            if pending is not None:
                lockstep(pending())
            if self.stop in ("passB", "B1", "B2", "B3"):
                continue
            wo = self.load_w(d["od_w_out"][o][256 * j:256 * j + 256, :], 256, 1024)
            self.out_proj(l, wo, 2, [GTp[:, 0, :], GTp[:, 1, :]], tiles)

    def epilogue(self):
        P = self.P
        self.fence()
        ostg = Rot([self.pr(4 * i, [128, D], F32, "ostg%d" % i) for i in range(2)])
        junk = self.pr(8, [128, 512], BF16, "junk")
        self.FNW = self.pr(12, [128, D], F32, "FNW")
        P.dma("sp", self.FNW, self.d["final_norm_w"].to_broadcast([128, D]))
        toks = []
        for t in range(16):
            ti, j = divmod(t, 4)
            pa = self.psA.next()
            pb = self.psA.next()
            for c in range(8):
                dst = pa if c < 4 else pb
                cc = c % 4
                P.transpose(dst[:, cc * 128:(cc + 1) * 128], self.xt[c][ti][:, j * 128:(j + 1) * 128], self.identf)
            st = self.stat.next()
            P.act(junk, pa, AF.Square, accum=st[:, 0:1])
            P.act(junk, pb, AF.Square, accum=st[:, 1:2])
            P.tt("dve", st[:, 2:3], st[:, 0:1], st[:, 1:2], ALU.add)
            P.act(st[:, 3:4], st[:, 2:3], AF.Ln, bias=self.epsc, scale=1.0 / D)
            P.act(st[:, 4:5], st[:, 3:4], AF.Exp, scale=-0.5)
            o = ostg.next()
            P.stt("dve", o[:, 0:512], pa, st[:, 4:5], self.FNW[:, 0:512], ALU.mult, ALU.mult)
            P.stt("dve", o[:, 512:1024], pb, st[:, 4:5], self.FNW[:, 512:1024], ALU.mult, ALU.mult)
            toks.append(P.dma("sp", self.out[t * 128:(t + 1) * 128, :], o))
        P.wait_all_dma("sp", toks)


def host_consts():
    p = np.arange(128)
    ident = np.eye(128, dtype=np.float32)
    tri0 = (p[:, None] <= p[None, :]).astype(np.float32)
    tri1 = (p[:, None] >= p[None, :]).astype(np.float32)
    neg0 = np.where(p[:, None] <= p[None, :], 0.0, -30000.0).astype(np.float32)
    neg1 = np.where(p[:, None] >= p[None, :], 0.0, -30000.0).astype(np.float32)
    cst = np.concatenate([ident, tri0, tri1, neg0, neg1], axis=1)
    pos = np.arange(NL)
    row = (pos // 64).astype(np.float32)
    col = (pos % 64).astype(np.float32)
    half = 16
    inv = (1.0 / (10000.0 ** (np.arange(0, half, 2, dtype=np.float32) / half))).astype(np.float32)
    ar = row[None, :] * inv[:, None]
    ac = col[None, :] * inv[:, None]
    C = np.ones((32, NT), np.float32)
    S = np.zeros((32, NT), np.float32)
    C[0:8, :NL] = np.cos(ar); C[8:16, :NL] = np.cos(ar); C[16:24, :NL] = np.cos(ac); C[24:32, :NL] = np.cos(ac)
    S[0:8, :NL] = -np.sin(ar); S[8:16, :NL] = np.sin(ar); S[16:24, :NL] = -np.sin(ac); S[24:32, :NL] = np.sin(ac)
    import os
    if os.environ.get("NOROPE"):
        C[:] = 1.0
        S[:] = 0.0
    return cst, C, S


_NC_CACHE = {}


def make_in_maps(inputs, n):
    cst, C, S = host_consts()
    f = lambda a: np.ascontiguousarray(np.asarray(a, dtype=np.float32))
    shared = {
        "c_ctx": f(inputs["c_ctx"]).reshape(8, 128),
        "ada_w": f(inputs["ada_w"]), "ada_b": f(inputs["ada_b"]).reshape(192, 128),
        "norm_mix_w": f(inputs["norm_mix_w"]).reshape(32, 128), "norm_mlp_w": f(inputs["norm_mlp_w"]).reshape(32, 128),
        "mlp_w1": f(inputs["mlp_w1"]), "mlp_w2": f(inputs["mlp_w2"]),
        "ev_w_in": f(inputs["ev_w_in"]), "ev_q_norm_w": f(inputs["ev_q_norm_w"]).reshape(4, 128),
        "ev_w_uq": f(inputs["ev_w_uq"]), "ev_kv_norm_w": f(inputs["ev_kv_norm_w"]).reshape(4, 128),
        "ev_w_ukv": f(inputs["ev_w_ukv"]), "ev_conv_w": f(inputs["ev_conv_w"]).reshape(80, 128),
        "ev_conv_b": f(inputs["ev_conv_b"]).reshape(16, 128), "ev_dt_bias": f(inputs["ev_dt_bias"]).reshape(2, 16),
        "ev_a_log": f(inputs["ev_a_log"]).reshape(2, 16), "ev_d_skip": f(inputs["ev_d_skip"]),
        "ev_ssm_norm_w": f(inputs["ev_ssm_norm_w"]), "ev_w_out": f(inputs["ev_w_out"]),
        "od_w_in": f(inputs["od_w_in"]), "od_conv_w": f(inputs["od_conv_w"]).reshape(80, 128),
        "od_conv_b": f(inputs["od_conv_b"]).reshape(16, 128), "od_i_bias": f(inputs["od_i_bias"]).reshape(2, 16),
        "od_f_bias": f(inputs["od_f_bias"]).reshape(2, 16), "od_head_norm_w": f(inputs["od_head_norm_w"]),
        "od_w_out": f(inputs["od_w_out"]), "final_norm_w": f(inputs["final_norm_w"]).reshape(1, D),
        "cst": cst, "ropeC": C, "ropeS": S,
    }
    x = f(inputs["x"]); c = f(inputs["c"]); ctx = f(inputs["ctx"])
    maps = []
    for b in range(n):
        m = dict(shared)
        m["x"] = x[b]
        m["ctx"] = ctx[b]
        m["c"] = c[b].reshape(8, 128)
        maps.append(m)
    return maps


def kernel(**inputs):
    n = 8
    key = "full"
    if key not in _NC_CACHE:
        mk = MK()
        _NC_CACHE[key] = (mk, mk.P.build())
    mk, nc = _NC_CACHE[key]
    maps = make_in_maps(inputs, n)
    used = set(mk.d.keys())
    maps = [{k: v for k, v in m.items() if k in used} for m in maps]
    res = run_bass_kernel_spmd(nc, maps, core_ids=list(range(n)))
    return np.stack([r["out"] for r in res.results], axis=0).astype(np.float32)
```
